# Optimizing a Trainium2 kernel written in Bass

```python
import jax, jax.numpy as jnp
from jax import lax
import numpy as np

D_MODEL = 2048
BATCH = 4
SEQ = 2048
DEPTH = 4
DEC_BATCH = 32
DEC_SEQ = 8
PAST_LEN = 16384
PAGE_SIZE = 128

N_MIXERS = 3
ALPHA = (2 * DEPTH) ** 0.25
BETA = (8 * DEPTH) ** -0.25
LN_EPS = 1e-5

GLA_HEADS = 4
GLA_DK = D_MODEL // 2 // GLA_HEADS
GLA_DV = D_MODEL // GLA_HEADS
GLA_RANK = 16
GLA_TAU = 16.0
GLA_CHUNK = 64

SWA_HEAD_DIM = 64
SWA_Q_HEADS = D_MODEL // SWA_HEAD_DIM
SWA_KV_HEADS = 8
SWA_GROUP = SWA_Q_HEADS // SWA_KV_HEADS
WINDOW = 128
ROT_DIM = SWA_HEAD_DIM // 4
ROPE_THETA = 500000.0

SG_WIDTH = D_MODEL
SG_GROUPS = 4
SG_CHUNK = 128

D_FF = 5632
CONV_W = 3

N_GLA_LAYERS = len(range(0, DEPTH, N_MIXERS))
N_SWA_LAYERS = len(range(1, DEPTH, N_MIXERS))
N_SG_LAYERS = len(range(2, DEPTH, N_MIXERS))

kernel_name = 'hybrid_gla_swa_gmlp_convffn_step'


def _layer_norm(x, g, b):
    xf = x.astype(jnp.float32)
    mu = jnp.mean(xf, -1, keepdims=True)
    var = jnp.mean(jnp.square(xf - mu), -1, keepdims=True)
    return ((xf - mu) * lax.rsqrt(var + LN_EPS) * g + b).astype(x.dtype)


def _gla_recurrence(q, k, v, log_a, s0):
    b, L = q.shape[:2]
    c = min(GLA_CHUNK, L)
    n = -(-L // c)
    pad = n * c - L

    def prep(t):
        t = jnp.pad(t.astype(jnp.float32), ((0, 0), (0, pad), (0, 0), (0, 0)))
        return t.reshape(b, n, c, *t.shape[2:]).swapaxes(0, 1)

    causal = jnp.tril(jnp.ones((c, c), dtype=bool))

    def step(s, xs):
        qc, kc, vc, gc = xs
        cum = jnp.cumsum(gc, axis=1)
        q_dec = qc * jnp.exp(cum)
        k_inv = kc * jnp.exp(-cum)
        attn = jnp.where(causal, jnp.einsum('bthd,bshd->bhts', q_dec, k_inv), 0.0)
        o = jnp.einsum('bhts,bshe->bthe', attn, vc) + jnp.einsum('bthd,bhde->bthe', q_dec, s)
        last = cum[:, -1]
        k_last = kc * jnp.exp(last[:, None] - cum)
        s = s * jnp.exp(last)[..., None] + jnp.einsum('bshd,bshe->bhde', k_last, vc)
        return s, o

    s, o = lax.scan(step, s0.astype(jnp.float32), (prep(q), prep(k), prep(v), prep(log_a)))
    o = o.swapaxes(0, 1).reshape(b, n * c, *o.shape[3:])[:, :L]
    return o, s


def _gla_mixer(x, s0, w_in, w_g2, b_g, norm_w, w_out):
    b, L, _ = x.shape
    qk = GLA_HEADS * GLA_DK
    vd = GLA_HEADS * GLA_DV
    q, k, v, r, g_low = jnp.split(x @ w_in, [qk, 2 * qk, 2 * qk + vd, 2 * qk + 2 * vd], axis=-1)
    q = q.reshape(b, L, GLA_HEADS, GLA_DK) * GLA_DK ** -0.5
    k = k.reshape(b, L, GLA_HEADS, GLA_DK)
    v = v.reshape(b, L, GLA_HEADS, GLA_DV)
    log_a = jax.nn.log_sigmoid((g_low @ w_g2 + b_g).astype(jnp.float32)) / GLA_TAU
    log_a = log_a.reshape(b, L, GLA_HEADS, GLA_DK)
    o, s = _gla_recurrence(q, k, v, log_a, s0)
    o = o * lax.rsqrt(jnp.mean(o * o, -1, keepdims=True) + LN_EPS) * norm_w
    o = o.reshape(b, L, vd) * jax.nn.silu(r.astype(jnp.float32))
    return o.astype(x.dtype) @ w_out, s.astype(s0.dtype)


def _rope(x, pos):
    half = ROT_DIM // 2
    inv = ROPE_THETA ** (-jnp.arange(half, dtype=jnp.float32) / half)
    ang = pos.astype(jnp.float32)[:, None] * inv
    bshape = (ang.shape[0],) + (1,) * (x.ndim - 3) + (half,)
    cos = jnp.cos(ang).reshape(bshape)
    sin = jnp.sin(ang).reshape(bshape)
    xf = x.astype(jnp.float32)
    x1 = xf[..., :half]
    x2 = xf[..., half:ROT_DIM]
    out = jnp.concatenate([x1 * cos - x2 * sin, x2 * cos + x1 * sin, xf[..., ROT_DIM:]], -1)
    return out.astype(x.dtype)


def _swa_qkv(x, pos, w_qkv, b_qkv):
    b, L, _ = x.shape
    qd = SWA_Q_HEADS * SWA_HEAD_DIM
    kd = SWA_KV_HEADS * SWA_HEAD_DIM
    q, k, v = jnp.split(x @ w_qkv + b_qkv, [qd, qd + kd], axis=-1)
    q = _rope(q.reshape(b, L, SWA_KV_HEADS, SWA_GROUP, SWA_HEAD_DIM), pos)
    k = _rope(k.reshape(b, L, SWA_KV_HEADS, SWA_HEAD_DIM), pos)
    v = v.reshape(b, L, SWA_KV_HEADS, SWA_HEAD_DIM)
    return q, k, v


def _sink_attention(q, k, v, mask, sinks):
    s = jnp.einsum('...qhgd,...khd->...hgqk', q, k).astype(jnp.float32) * SWA_HEAD_DIM ** -0.5
    s = jnp.where(mask, s, -jnp.inf)
    sink = jnp.broadcast_to(sinks.astype(jnp.float32)[..., None, None], s.shape[:-1] + (1,))
    p = jax.nn.softmax(jnp.concatenate([s, sink], axis=-1), axis=-1)[..., :-1]
    return jnp.einsum('...hgqk,...khd->...qhgd', p.astype(v.dtype), v)


def _swa_prompt(x, w_qkv, b_qkv, sinks, w_out, b_out):
    b, L, _ = x.shape
    pos = jnp.arange(L)
    q, k, v = _swa_qkv(x, pos, w_qkv, b_qkv)
    nb = L // WINDOW
    qb = q.reshape(b, nb, WINDOW, SWA_KV_HEADS, SWA_GROUP, SWA_HEAD_DIM)

    def band(t):
        tb = t.reshape(b, nb, WINDOW, SWA_KV_HEADS, SWA_HEAD_DIM)
        prev = jnp.concatenate([jnp.zeros_like(tb[:, :1]), tb[:, :-1]], axis=1)
        return jnp.concatenate([prev, tb], axis=2)

    qpos = pos.reshape(nb, WINDOW)
    kpos = jnp.concatenate([qpos - WINDOW, qpos], axis=-1)
    diff = qpos[:, :, None] - kpos[:, None, :]
    mask = (diff >= 0) & (diff < WINDOW) & (kpos[:, None, :] >= 0)
    o = _sink_attention(qb, band(k), band(v), mask[None, :, None, None], sinks)
    y = o.reshape(b, L, SWA_Q_HEADS * SWA_HEAD_DIM) @ w_out + b_out
    nbuf = min(WINDOW, L)
    return y, k[:, -nbuf:], v[:, -nbuf:]


def _swa_sample(x, k_buf, v_buf, w_qkv, b_qkv, sinks, w_out, b_out):
    b, L, _ = x.shape
    nbuf = k_buf.shape[1]
    pos = PAST_LEN + jnp.arange(L)
    q, k, v = _swa_qkv(x, pos, w_qkv, b_qkv)
    k_all = jnp.concatenate([k_buf.astype(k.dtype), k], axis=1)
    v_all = jnp.concatenate([v_buf.astype(v.dtype), v], axis=1)
    kpos = jnp.concatenate([PAST_LEN - nbuf + jnp.arange(nbuf), pos])
    diff = pos[:, None] - kpos[None, :]
    mask = (diff >= 0) & (diff < WINDOW)
    o = _sink_attention(q, k_all, v_all, mask, sinks)
    y = o.reshape(b, L, SWA_Q_HEADS * SWA_HEAD_DIM) @ w_out + b_out
    return y, k_all[:, -nbuf:], v_all[:, -nbuf:]


def _sg_mixer(x, w_in, b_in, ln_g, ln_b, w_s, b_s, w_out, b_out):
    b, L, _ = x.shape
    z = jax.nn.gelu(x @ w_in + b_in, approximate=False)
    u, v = jnp.split(z, 2, axis=-1)
    v = _layer_norm(v, ln_g, ln_b)
    c = min(SG_CHUNK, L)
    n = -(-L // c)
    pad = n * c - L
    vc = jnp.pad(v, ((0, 0), (0, pad), (0, 0))).reshape(b, n, c, SG_GROUPS, SG_WIDTH // SG_GROUPS)
    w = jnp.tril(w_s[:, :c, :c])
    mixed = jnp.einsum('gts,bnsgc->bntgc', w, vc) + b_s[:, :c].T[:, :, None]
    mixed = mixed.reshape(b, n * c, SG_WIDTH)[:, :L]
    return (u * mixed) @ w_out + b_out, v


def _conv_ffn(x, prev, w_in, conv_w, conv_b, w_out):
    L = x.shape[1]
    gate, val = jnp.split(x @ w_in, 2, axis=-1)
    ext = jnp.concatenate([prev.astype(gate.dtype), gate], axis=1)
    conv = conv_b
    for i in range(CONV_W):
        conv = conv + conv_w[i] * ext[:, i:i + L]
    y = (jax.nn.gelu(conv, approximate=False) * val) @ w_out
    return y, ext[:, -(CONV_W - 1):]


def setup_inputs(seed: int = 0) -> dict:
    key = jax.random.key(seed)
    ks = iter(jax.random.split(key, 40))

    def nrm(shape, scale=1.0):
        return jax.random.normal(next(ks), shape, jnp.float32) * scale

    qk = GLA_HEADS * GLA_DK
    vd = GLA_HEADS * GLA_DV
    swa_buf = min(WINDOW, PAST_LEN)
    return {
        'x_prompt': nrm((BATCH, SEQ, D_MODEL)),
        'x_sample': nrm((DEC_BATCH, DEC_SEQ, D_MODEL)),
        'state_gla': nrm((N_GLA_LAYERS, DEC_BATCH, GLA_HEADS, GLA_DK, GLA_DV)),
        'cache_swa_k': nrm((N_SWA_LAYERS, DEC_BATCH, swa_buf, SWA_KV_HEADS, SWA_HEAD_DIM)),
        'cache_swa_v': nrm((N_SWA_LAYERS, DEC_BATCH, swa_buf, SWA_KV_HEADS, SWA_HEAD_DIM)),
        'state_ffn_conv': nrm((DEPTH, DEC_BATCH, CONV_W - 1, D_FF)),
        'ln_mix_g': 1.0 + nrm((DEPTH, D_MODEL), 0.02),
        'ln_mix_b': nrm((DEPTH, D_MODEL), 0.02),
        'ln_ffn_g': 1.0 + nrm((DEPTH, D_MODEL), 0.02),
        'ln_ffn_b': nrm((DEPTH, D_MODEL), 0.02),
        'gla_w_in': nrm((N_GLA_LAYERS, D_MODEL, 2 * qk + 2 * vd + GLA_RANK), D_MODEL ** -0.5),
        'gla_w_g2': nrm((N_GLA_LAYERS, GLA_RANK, qk), GLA_RANK ** -0.5),
        'gla_b_g': nrm((N_GLA_LAYERS, qk), 0.1),
        'gla_norm_w': 1.0 + nrm((N_GLA_LAYERS, GLA_DV), 0.02),
        'gla_w_out': nrm((N_GLA_LAYERS, vd, D_MODEL), vd ** -0.5 * BETA),
        'swa_w_qkv': nrm((N_SWA_LAYERS, D_MODEL, (SWA_Q_HEADS + 2 * SWA_KV_HEADS) * SWA_HEAD_DIM), D_MODEL ** -0.5),
        'swa_b_qkv': nrm((N_SWA_LAYERS, (SWA_Q_HEADS + 2 * SWA_KV_HEADS) * SWA_HEAD_DIM), 0.02),
        'swa_sinks': nrm((N_SWA_LAYERS, SWA_KV_HEADS, SWA_GROUP), 1.0),
        'swa_w_out': nrm((N_SWA_LAYERS, SWA_Q_HEADS * SWA_HEAD_DIM, D_MODEL), (SWA_Q_HEADS * SWA_HEAD_DIM) ** -0.5 * BETA),
        'swa_b_out': nrm((N_SWA_LAYERS, D_MODEL), 0.02),
        'sg_w_in': nrm((N_SG_LAYERS, D_MODEL, 2 * SG_WIDTH), D_MODEL ** -0.5),
        'sg_b_in': nrm((N_SG_LAYERS, 2 * SG_WIDTH), 0.02),
        'sg_ln_g': 1.0 + nrm((N_SG_LAYERS, SG_WIDTH), 0.02),
        'sg_ln_b': nrm((N_SG_LAYERS, SG_WIDTH), 0.02),
        'sg_w_s': nrm((N_SG_LAYERS, SG_GROUPS, SG_CHUNK, SG_CHUNK), SG_CHUNK ** -0.5),
        'sg_b_s': 1.0 + nrm((N_SG_LAYERS, SG_GROUPS, SG_CHUNK), 0.02),
        'sg_w_out': nrm((N_SG_LAYERS, SG_WIDTH, D_MODEL), SG_WIDTH ** -0.5 * BETA),
        'sg_b_out': nrm((N_SG_LAYERS, D_MODEL), 0.02),
        'ffn_w_in': nrm((DEPTH, D_MODEL, 2 * D_FF), D_MODEL ** -0.5),
        'ffn_conv_w': nrm((DEPTH, CONV_W, D_FF), CONV_W ** -0.5),
        'ffn_conv_b': nrm((DEPTH, D_FF), 0.02),
        'ffn_w_out': nrm((DEPTH, D_FF, D_MODEL), D_FF ** -0.5 * BETA),
    }


def reference(x_prompt, x_sample, state_gla, cache_swa_k, cache_swa_v, state_ffn_conv,
              ln_mix_g, ln_mix_b, ln_ffn_g, ln_ffn_b,
              gla_w_in, gla_w_g2, gla_b_g, gla_norm_w, gla_w_out,
              swa_w_qkv, swa_b_qkv, swa_sinks, swa_w_out, swa_b_out,
              sg_w_in, sg_b_in, sg_ln_g, sg_ln_b, sg_w_s, sg_b_s, sg_w_out, sg_b_out,
              ffn_w_in, ffn_conv_w, ffn_conv_b, ffn_w_out):
    xp, xs = x_prompt, x_sample
    gla_p, gla_s = [], []
    swk_p, swv_p, swk_s, swv_s = [], [], [], []
    sgv_s = []
    conv_p, conv_s = [], []
    for i in range(DEPTH):
        j = i // N_MIXERS
        kind = i % N_MIXERS
        if kind == 0:
            w = (gla_w_in[j], gla_w_g2[j], gla_b_g[j], gla_norm_w[j], gla_w_out[j])
            s0 = jnp.zeros((xp.shape[0], GLA_HEADS, GLA_DK, GLA_DV), state_gla.dtype)
            mp, st_p = _gla_mixer(xp, s0, *w)
            ms, st_s = _gla_mixer(xs, state_gla[j], *w)
            gla_p.append(st_p)
            gla_s.append(st_s)
        elif kind == 1:
            w = (swa_w_qkv[j], swa_b_qkv[j], swa_sinks[j], swa_w_out[j], swa_b_out[j])
            mp, kp, vp = _swa_prompt(xp, *w)
            ms, ks_, vs_ = _swa_sample(xs, cache_swa_k[j], cache_swa_v[j], *w)
            swk_p.append(kp)
            swv_p.append(vp)
            swk_s.append(ks_)
            swv_s.append(vs_)
        else:
            w = (sg_w_in[j], sg_b_in[j], sg_ln_g[j], sg_ln_b[j], sg_w_s[j], sg_b_s[j], sg_w_out[j], sg_b_out[j])
            mp, _ = _sg_mixer(xp, *w)
            ms, v_rows = _sg_mixer(xs, *w)
            sgv_s.append(v_rows)
        xp = _layer_norm(ALPHA * xp + mp, ln_mix_g[i], ln_mix_b[i])
        xs = _layer_norm(ALPHA * xs + ms, ln_mix_g[i], ln_mix_b[i])
        wf = (ffn_w_in[i], ffn_conv_w[i], ffn_conv_b[i], ffn_w_out[i])
        fp, cp = _conv_ffn(xp, jnp.zeros((xp.shape[0], CONV_W - 1, D_FF), xp.dtype), *wf)
        fs, cs = _conv_ffn(xs, state_ffn_conv[i], *wf)
        conv_p.append(cp)
        conv_s.append(cs)
        xp = _layer_norm(ALPHA * xp + fp, ln_ffn_g[i], ln_ffn_b[i])
        xs = _layer_norm(ALPHA * xs + fs, ln_ffn_g[i], ln_ffn_b[i])
    return (xp, xs, jnp.stack(gla_p), jnp.stack(gla_s), jnp.stack(swk_p), jnp.stack(swv_p),
            jnp.stack(swk_s), jnp.stack(swv_s), jnp.stack(sgv_s), jnp.stack(conv_p), jnp.stack(conv_s))
```

```python
import os
from contextlib import ExitStack
import numpy as np
import concourse.bass as bass
import concourse.mybir as mybir
from concourse.bass_utils import run_bass_kernel_spmd

F32 = mybir.dt.float32
BF16 = mybir.dt.bfloat16
AF = mybir.ActivationFunctionType
ALU = mybir.AluOpType
AX = mybir.AxisListType

D = 2048
NCH = 16
T = 1024
NS = 32
NT = T + NS
NX = NT + 2
DFF = 5632
FCH = 44
DEPTH = 4
ALPHA = (2 * DEPTH) ** 0.25
EPS = 1e-5
TGS = [(0, 512), (512, 1024), (1024, NT)]
WSLOT = 4096
SWC_N = 2 * NT + 128 + 256 + 640 + 512
NWS = 4
PAIRS = [[2 * i, 2 * i + 1] for i in range(4)]

ENGS = ("pe", "act", "dve", "pool", "sp")
NDMA = 8


class Buf:
    __slots__ = ("name", "w", "r")

    def __init__(self, name):
        self.name = name
        self.w = None
        self.r = []


class Sched:
    def __init__(self, nc):
        self.nc = nc
        self.ops = {e: [] for e in ENGS}
        self.sems = {}
        self.cnt = {}
        self.seen = {e: {} for e in ENGS}
        self.dma_i = {e: 0 for e in ENGS}
        self.n_inst = {e: 0 for e in ENGS}

    def setup(self, stack):
        for e in ("pe", "act", "dve", "pool"):
            self.sems[e] = stack.enter_context(self.nc.semaphore("s_" + e))
            self.cnt[e] = 0
        for q in ("sp", "pool"):
            for i in range(NDMA):
                k = "d_%s%d" % (q, i)
                self.sems[k] = stack.enter_context(self.nc.semaphore(k))
                self.cnt[k] = 0
        self.sems["cc"] = stack.enter_context(self.nc.semaphore("s_cc"))
        self.cnt["cc"] = 0

    def _deps(self, reads, writes):
        deps = {}
        for b in reads:
            t = b.w
            if t is not None and deps.get(t[0], 0) < t[1]:
                deps[t[0]] = t[1]
        for b in writes:
            t = b.w
            if t is not None and deps.get(t[0], 0) < t[1]:
                deps[t[0]] = t[1]
            for t in b.r:
                if deps.get(t[0], 0) < t[1]:
                    deps[t[0]] = t[1]
        return deps

    def _waits(self, eng, deps):
        out = []
        seen = self.seen[eng]
        for k, v in deps.items():
            if seen.get(k, 0) < v:
                seen[k] = v
                out.append((k, v))
        return out

    def _mark(self, tok, reads, writes):
        for b in reads:
            b.r.append(tok)
            if len(b.r) > 64:
                m = {}
                for k, v in b.r:
                    if m.get(k, 0) < v:
                        m[k] = v
                b.r = list(m.items())
        for b in writes:
            b.w = tok
            b.r = []

    def group(self, eng, fns, reads=(), writes=()):
        if getattr(self, "_cap", None) is not None:
            self._cap.append((eng, fns, tuple(reads), tuple(writes)))
            return None
        deps = self._deps(reads, writes)
        waits = self._waits(eng, deps)
        self.cnt[eng] += 1
        tok = (eng, self.cnt[eng])
        self._mark(tok, reads, writes)
        sems = self.sems
        semh = sems[eng]
        n = len(fns)

        def run(h):
            for k, v in waits:
                h.wait_ge(sems[k], v)
            for i, f in enumerate(fns):
                ins = f(h)
                if i == n - 1:
                    ins.then_inc(semh, 1)
        self.ops[eng].append(run)
        self.n_inst[eng] += n
        return tok

    def op(self, eng, fn, reads=(), writes=()):
        return self.group(eng, [fn], reads, writes)

    def capture(self, fn):
        self._cap = []
        fn()
        out, self._cap = self._cap, None
        return out

    def replay_interleaved(self, chains):
        n = max(len(c) for c in chains)
        for k in range(n):
            for c in chains:
                if k < len(c):
                    self.group(*c[k])

    def dma(self, q, out_ap, in_ap, reads=(), writes=(), **kw):
        i = self.dma_i[q]
        self.dma_i[q] += 1
        k = "d_%s%d" % (q, i % NDMA)
        deps = self._deps(reads, writes)
        prev = self.cnt[k]
        if prev and deps.get(k, 0) < prev:
            deps[k] = prev
        waits = self._waits(q, deps)
        self.cnt[k] += 16
        tok = (k, self.cnt[k])
        self._mark(tok, reads, writes)
        sems = self.sems
        semh = sems[k]

        def run(h):
            for kk, v in waits:
                h.wait_ge(sems[kk], v)
            h.dma_start(out=out_ap, in_=in_ap, **kw).then_inc(semh, 16)
        self.ops[q].append(run)
        return tok

    def allgather(self, cin, cout, reads=(), writes=()):
        deps = self._deps(reads, writes)
        prev = self.cnt["cc"]
        if prev and deps.get("cc", 0) < prev:
            deps["cc"] = prev
        waits = self._waits("pool", deps)
        self.cnt["cc"] += 1
        tok = ("cc", self.cnt["cc"])
        self._mark(tok, reads, writes)
        sems = self.sems

        def run(h):
            for kk, v in waits:
                h.wait_ge(sems[kk], v)
            h.collective_compute("AllGather", ALU.bypass, replica_groups=PAIRS,
                                 ins=[cin], outs=[cout]).then_inc(sems["cc"], 1)
        self.ops["pool"].append(run)
        return tok

    def barrier(self, engs=ENGS):
        for e in engs:
            waits = [(k, v) for k, v in self.cnt.items() if v > 0 and self.seen[e].get(k, 0) < v]
            for k, v in waits:
                self.seen[e][k] = v
            sems = self.sems

            def run(h, waits=waits):
                for kk, v in waits:
                    h.wait_ge(sems[kk], v)
            self.ops[e].append(run)

    def emit(self, block):
        ops = self.ops

        @block.tensor
        def _(h):
            for f in ops["pe"]:
                f(h)

        @block.scalar
        def _(h):
            for f in ops["act"]:
                f(h)

        @block.vector
        def _(h):
            for f in ops["dve"]:
                f(h)

        @block.gpsimd
        def _(h):
            for f in ops["pool"]:
                f(h)

        @block.sync
        def _(h):
            for f in ops["sp"]:
                f(h)


class Cols:
    def __init__(self):
        self.off = {}
        self.n = 0

    def add(self, name, n):
        self.off[name] = (self.n, n)
        self.n += n


def const_layout():
    C = Cols()
    C.add("ident", 128)
    C.add("onesd", 128)
    C.add("flag", 1)
    C.add("alpha", 1)
    C.add("eps", 1)
    C.add("m16", 1)
    C.add("qs", 1)
    C.add("i512", 1)
    C.add("tmask", 128)
    C.add("smask", 32)
    C.add("rowm", 4)
    C.add("reset", NT)
    C.add("ones", 128)
    C.add("tri", 128)
    C.add("sgbi", 32)
    C.add("sglg", 16)
    C.add("sglb", 16)
    C.add("sgbo", 16)
    C.add("swbq", 16)
    C.add("swbk", 8)
    C.add("sink", 32)
    C.add("sinkp", 32)
    C.add("swbo", 16)
    for j in range(2):
        C.add("bg%d" % j, 8)
        C.add("gnw%d" % j, 4)
    for i in range(DEPTH):
        for nm in ("lmg", "lmb", "lfg", "lfb"):
            C.add("%s%d" % (nm, i), NCH)
        for nm in ("cw0_", "cw1_", "cw2_", "cb_"):
            C.add("%s%d" % (nm, i), FCH)
    return C


def fm(vec):
    v = np.asarray(vec, np.float32)
    return np.ascontiguousarray(v.reshape(-1, 128).T)


class K:
    pass


def build_nc(cfg):
    nc = bass.Bass("TRN2", target_bir_lowering=False)
    C = const_layout()
    k = K()
    k.nc, k.C = nc, C

    def din(name, shape):
        return nc.dram_tensor(name, list(shape), F32, kind="ExternalInput").ap()

    def dout(name, shape):
        return nc.dram_tensor(name, list(shape), F32, kind="ExternalOutput").ap()

    def dint(name, shape):
        return nc.dram_tensor(name, list(shape), F32, kind="Internal").ap()

    xin = din("xin", [128, NCH, NT])
    cst = din("cst", [128, C.n])
    fconv = din("fconv", [128, DEPTH * FCH * 8])
    nl_ = cfg.get("layers", DEPTH)
    kinds_ = cfg.get("kinds", (0, 1, 2, 0))[:nl_] if cfg.get("mix", True) else ()
    need = {"ffn": not cfg.get("noffn", False), "gla": 0 in kinds_, "swa": 1 in kinds_, "sg": 2 in kinds_}
    k.need = need

    def dinw(name, shape, fam):
        return din(name, shape if need[fam] else [1, 128, 128])
    ffn_w_in = dinw("ffn_w_in", [DEPTH, D, 2 * DFF], "ffn")
    ffn_w_out = dinw("ffn_w_out", [DEPTH, DFF, D], "ffn")
    swa_w_qkv = dinw("swa_w_qkv", [1, D, 3072], "swa")
    swa_w_out = dinw("swa_w_out", [1, D, D], "swa")
    swa_wk = din("swa_wk", [8, 128, NCH * 128])
    swa_wv = din("swa_wv", [8, 128, NCH * 64])
    gla_wg = din("gla_wg", [2, 128, NCH * 16])
    swc = din("swc", [128, SWC_N])
    kcache = din("kcache", [8, 128, 512])
    vcache = din("vcache", [8, 128, 768])
    ck = din("ck", [4, 128, 512])
    cv = din("cv", [4, 128, 512])
    swk = dout("swk", [8, 128, 160])
    swv = dout("swv", [8, 128, 128])
    ks_cache = dout("ks_cache", [4, 120, 512])
    vs_cache = dout("vs_cache", [4, 120, 512])
    sx_in = [dint("sx_in%d" % i, [128, 192]) for i in range(8)]
    sx_out = [dint("sx_out%d" % i, [256, 192]) for i in range(8)]
    sg_w_in = dinw("sg_w_in", [1, D, 2 * D], "sg")
    sg_w_out = dinw("sg_w_out", [1, D, D], "sg")
    sgc = din("sgc", [128, 1280])
    sgv = dout("sgv", [128, NCH * 32])
    gla_w_in = dinw("gla_w_in", [2, D, 6160], "gla")
    gla_w_g2 = din("gla_w_g2", [2, 16, 1024])
    gla_w_out = dinw("gla_w_out", [2, D, D], "gla")
    gla_s0 = din("gla_s0", [2, 4, 4, 256, 512] if need["gla"] else [1, 128, 128])
    gla_sp = dout("gla_sp", [2, 4, 256, 512])
    gla_ss = dout("gla_ss", [2, 4, 4, 256, 512])
    gs_in = [dint("gs_in%d" % i, [128, 1024]) for i in range(8)]
    gs_out = [dint("gs_out%d" % i, [256, 1024]) for i in range(8)]
    yT = dout("yT", [128, NCH, NT])
    convT = dout("convT", [128, DEPTH * FCH * 10])
    hx_in = [dint("hx_in%d" % i, [128, 32]) for i in range(DEPTH)]
    hx_out = [dint("hx_out%d" % i, [256, 32]) for i in range(DEPTH)]

    S = Sched(nc)
    k.S = S
    with ExitStack() as st:
        S.setup(st)

        def sb(name, shape, dt):
            return st.enter_context(nc.sbuf_tensor(name, list(shape), dt))

        xT = sb("xT", [128, NCH, NT], F32)
        xb = sb("xb", [128, NCH, NX], BF16)
        cs = sb("cs", [128, C.n], F32)
        fcs = sb("fcs", [128, DEPTH * FCH * 8], F32)
        wsl = [sb("wsl%d" % i, [128, WSLOT], BF16) for i in range(NWS)]
        SCRB = cfg.get("scr_bytes", 59 * 1024)
        scr = sb("scr", [128, SCRB // 4], F32)
        pbank = [st.enter_context(nc.psum_tensor("pb%d" % i, [128, 512], F32)) for i in range(8)]
        block = st.enter_context(nc.Block())

        b_x = [Buf("x%d" % c) for c in range(NCH)]
        b_xb = [Buf("xb%d" % c) for c in range(NCH)]
        b_xh = Buf("xhalo")
        b_cs = Buf("cs")
        b_fcs = Buf("fcs")
        b_ws = [Buf("ws%d" % i) for i in range(NWS)]
        b_pb = [Buf("pb%d" % i) for i in range(8)]
        b_out = Buf("out")

        def col(name, j=0, n=1):
            o, _ = C.off[name]
            return cs[:, o + j:o + j + n]

        class Scr:
            def __init__(self):
                self.p = 0

            def reset(self):
                S.barrier()
                self.p = 0

            def f32(self, n):
                a = scr[:, self.p:self.p + n]
                self.p += n
                assert self.p * 4 <= SCRB, ("scratch overflow", self.p * 4)
                return a

            def bf16(self, n):
                w = (n + 1) // 2
                a = scr[:, self.p:self.p + w].bitcast(BF16)
                self.p += w
                assert self.p * 4 <= SCRB, ("scratch overflow", self.p * 4)
                return a

        scrm = Scr()

        wstate = {"i": 0}

        def wload(src_ap, kc, ncols):
            i = wstate["i"] % NWS
            wstate["i"] += 1
            assert kc * ncols <= WSLOT
            view = wsl[i][:, 0:kc * ncols].rearrange("p (k m) -> p k m", k=kc)
            S.dma("pool", view, src_ap.rearrange("(k p) m -> p k m", p=128), writes=[b_ws[i]])
            return view, b_ws[i]

        def wload_flat(src_ap, kc, ncols):
            i = wstate["i"] % NWS
            wstate["i"] += 1
            flat = wsl[i][:, 0:kc * ncols]
            S.dma("pool", flat, src_ap, writes=[b_ws[i]])
            return flat.rearrange("p (k m) -> p k m", k=kc), b_ws[i]

        class WStream:
            def __init__(self, items, depth=NWS - 1):
                self.items = items
                self.loaded = []
                self.depth = depth

            def get(self, i):
                while len(self.loaded) < min(len(self.items), i + self.depth):
                    self.loaded.append(wload(*self.items[len(self.loaded)]))
                return self.loaded[i]

        S.dma("sp", cs[:], cst, writes=[b_cs])
        S.dma("sp", fcs[:], fconv, writes=[b_fcs])
        for c in range(NCH):
            S.dma("sp", xT[:, c, :], xin[:, c, :], writes=[b_x[c]])
        for c in range(NCH):
            eng = "act" if c % 2 else "dve"
            if eng == "act":
                S.op("act", lambda h, c=c: h.copy(xb[:, c, 0:NT], xT[:, c, :]), reads=[b_x[c]], writes=[b_xb[c]])
            else:
                S.op("dve", lambda h, c=c: h.tensor_copy(xb[:, c, 0:NT], xT[:, c, :]), reads=[b_x[c]], writes=[b_xb[c]])

        ident = col("ident", 0, 128)
        identb = sb("identb", [128, 128], BF16)
        S.op("dve", lambda h: h.tensor_copy(identb[:], ident), reads=[b_cs], writes=[Buf("identb")])
        onesd = col("onesd", 0, 128)
        flag = col("flag")

        def scale_residual():
            for c in range(NCH):
                eng = ("dve", "pool")[c % 2]
                if eng == "dve":
                    S.op("dve", lambda h, c=c: h.tensor_scalar(xT[:, c, :], xT[:, c, :], col("alpha"), None, ALU.mult),
                         reads=[], writes=[b_x[c]])
                else:
                    S.op("pool", lambda h, c=c: h.tensor_scalar(xT[:, c, :], xT[:, c, :], col("alpha"), None, ALU.mult),
                         reads=[], writes=[b_x[c]])

        def layer_norm(gname, bname):
            scrm.reset()
            s1 = scrm.f32(NT)
            s2 = scrm.f32(NT)
            sq = [scrm.f32(NT), scrm.f32(NT)]
            mean = scrm.f32(NT)
            rstd = scrm.f32(NT)
            tmp = [scrm.f32(NT), scrm.f32(NT)]
            b_s1, b_s2, b_mean, b_rstd = Buf("s1"), Buf("s2"), Buf("mean"), Buf("rstd")
            b_sq = [Buf("sq0"), Buf("sq1")]
            b_tmp = [Buf("t0"), Buf("t1")]
            for c in range(NCH):
                if c == 0:
                    S.op("dve", lambda h: h.tensor_copy(s1, xT[:, 0, :]), reads=[b_x[0]], writes=[b_s1])
                else:
                    S.op("dve", lambda h, c=c: h.tensor_tensor(s1, s1, xT[:, c, :], ALU.add), reads=[b_x[c]], writes=[b_s1])
            for c in range(NCH):
                j = c % 2
                S.op("act", lambda h, c=c, j=j: h.activation(sq[j], xT[:, c, :], AF.Square), reads=[b_x[c]], writes=[b_sq[j]])
                if c == 0:
                    S.op("pool", lambda h, j=j: h.tensor_copy(s2, sq[j]), reads=[b_sq[j]], writes=[b_s2])
                else:
                    S.op("pool", lambda h, j=j: h.tensor_tensor(s2, s2, sq[j], ALU.add), reads=[b_sq[j]], writes=[b_s2])
            fns = []
            for ti, (a, b) in enumerate(TGS):
                fns.append(lambda h, ti=ti, a=a, b=b: h.matmul(pbank[ti][:, 0:b - a], onesd, s1[:, a:b], start=True, stop=True))
            S.group("pe", fns, reads=[b_cs, b_s1], writes=b_pb[0:3])
            fns = []
            for ti, (a, b) in enumerate(TGS):
                fns.append(lambda h, ti=ti, a=a, b=b: h.matmul(pbank[3 + ti][:, 0:b - a], onesd, s2[:, a:b], start=True, stop=True))
            S.group("pe", fns, reads=[b_cs, b_s2], writes=b_pb[3:6])
            for ti, (a, b) in enumerate(TGS):
                S.op("act", lambda h, ti=ti, a=a, b=b: h.copy(mean[:, a:b], pbank[ti][:, 0:b - a]), reads=[b_pb[ti]], writes=[b_mean])
            S.op("dve", lambda h: h.tensor_tensor(rstd, mean, mean, ALU.mult), reads=[b_mean], writes=[b_rstd])
            for ti, (a, b) in enumerate(TGS):
                S.op("dve", lambda h, ti=ti, a=a, b=b: h.tensor_tensor(rstd[:, a:b], pbank[3 + ti][:, 0:b - a], rstd[:, a:b], ALU.subtract),
                     reads=[b_pb[3 + ti]], writes=[b_rstd])
            S.op("dve", lambda h: h.tensor_scalar(rstd, rstd, col("eps"), None, ALU.add), reads=[], writes=[b_rstd])
            S.op("act", lambda h: h.activation(rstd, rstd, AF.Sqrt), reads=[], writes=[b_rstd])
            S.op("dve", lambda h: h.reciprocal(rstd, rstd), reads=[], writes=[b_rstd])
            for c in range(NCH):
                j = c % 2
                S.op("dve", lambda h, c=c, j=j: h.tensor_tensor(tmp[j], xT[:, c, :], mean, ALU.subtract),
                     reads=[b_x[c], b_mean], writes=[b_tmp[j]])
                S.op("pool", lambda h, j=j: h.tensor_tensor(tmp[j], tmp[j], rstd, ALU.mult), reads=[b_rstd], writes=[b_tmp[j]])
                S.op("act", lambda h, c=c, j=j: h.activation(xT[:, c, :], tmp[j], AF.Identity, bias=col(bname, c), scale=col(gname, c)),
                     reads=[b_tmp[j], b_cs], writes=[b_x[c]])
                S.op("act", lambda h, c=c, j=j: h.activation(xb[:, c, 0:NT], tmp[j], AF.Identity, bias=col(bname, c), scale=col(gname, c)),
                     reads=[b_tmp[j], b_cs], writes=[b_xb[c]])

        def halo_exchange(i):
            scrm.reset()
            hs = scrm.f32(32)
            hr = scrm.f32(32)
            b_hs, b_hr, b_ci, b_co = Buf("hs"), Buf("hr"), Buf("ci"), Buf("co")
            S.op("dve", lambda h: h.tensor_copy(hs.rearrange("p (c t) -> p c t", t=2), xT[:, :, T - 2:T]), reads=b_x, writes=[b_hs])
            S.dma("sp", hx_in[i], hs, reads=[b_hs], writes=[b_ci])
            S.allgather(hx_in[i], hx_out[i], reads=[b_ci], writes=[b_co])
            S.dma("sp", hr, hx_out[i][0:128, :], reads=[b_co], writes=[b_hr])
            S.op("dve", lambda h: h.tensor_scalar(xb[:, :, NT:NX], hr.rearrange("p (c t) -> p c t", t=2), flag, None, ALU.mult),
                 reads=[b_hr, b_cs], writes=[b_xh])

        def ffn(i):
            scrm.reset()
            GW = 11
            hbuf = scrm.bf16(GW * NT).rearrange("p (m t) -> p m t", m=GW)
            gext = [scrm.f32(NT + 10), scrm.f32(NT + 10)]
            acc = [scrm.f32(NT), scrm.f32(NT)]
            convo = scrm.f32(FCH * 10)
            b_h = [Buf("h%d" % m) for m in range(GW)]
            b_ge = [Buf("ge0"), Buf("ge1")]
            b_ac = [Buf("ac0"), Buf("ac1")]
            b_co = Buf("convo")
            win = ffn_w_in[i]
            wout = ffn_w_out[i]
            xbufs = b_xb + [b_xh]

            def rhs(kk, ti):
                a, b = TGS[ti]
                if ti == 2:
                    return xb[:, kk, a:NX]
                return xb[:, kk, a:b]

            mglob = 0
            for g in range(FCH // GW):
                items = []
                blocks = []
                c0 = g * GW * 128
                off = 0
                while off < GW * 128:
                    n = min(256, GW * 128 - off)
                    items.append((win[:, c0 + off:c0 + off + n], NCH, n))
                    items.append((win[:, DFF + c0 + off:DFF + c0 + off + n], NCH, n))
                    blocks.append((off, n))
                    off += n
                ws = WStream(items)
                for bi, (off, n) in enumerate(blocks):
                    gv, gb = ws.get(2 * bi)
                    vv, vb = ws.get(2 * bi + 1)
                    for mo in range(0, n, 128):
                        m = (off + mo) // 128
                        mg = g * GW + m
                        j = mglob % 2
                        mglob += 1
                        fns = []
                        for kk in range(NCH):
                            for ti in range(3):
                                w = (TGS[ti][1] - TGS[ti][0]) + (2 if ti == 2 else 0)
                                fns.append(lambda h, kk=kk, ti=ti, w=w, mo=mo, gv=gv: h.matmul(
                                    pbank[ti][:, 0:w], gv[:, kk, mo:mo + 128], rhs(kk, ti), start=(kk == 0), stop=(kk == NCH - 1)))
                        S.group("pe", fns, reads=[gb] + xbufs, writes=b_pb[0:3])
                        fns = []
                        for kk in range(NCH):
                            for ti in range(3):
                                w = (TGS[ti][1] - TGS[ti][0])
                                fns.append(lambda h, kk=kk, ti=ti, w=w, mo=mo, vv=vv: h.matmul(
                                    pbank[3 + ti][:, 0:w], vv[:, kk, mo:mo + 128], xb[:, kk, TGS[ti][0]:TGS[ti][1]],
                                    start=(kk == 0), stop=(kk == NCH - 1)))
                        S.group("pe", fns, reads=[vb] + b_xb, writes=b_pb[3:6])
                        ge, ac = gext[j], acc[j]
                        ges = ge[:, T + 2:T + 2 + 40].rearrange("p (b t) -> p b t", t=10)
                        acs = ac[:, T:NT].rearrange("p (b t) -> p b t", t=8)
                        S.op("act", lambda h, ge=ge: h.copy(ge[:, 2:514], pbank[0][:, 0:512]), reads=[b_pb[0]], writes=[b_ge[j]])
                        S.op("act", lambda h, ge=ge: h.copy(ge[:, 514:1026], pbank[1][:, 0:512]), reads=[b_pb[1]], writes=[b_ge[j]])
                        S.op("act", lambda h, ges=ges: h.copy(ges[:, :, 2:10], pbank[2][:, 0:32].rearrange("p (b t) -> p b t", t=8)),
                             reads=[b_pb[2]], writes=[b_ge[j]])
                        S.op("act", lambda h, ge=ge: h.copy(ge[:, 0:2], pbank[2][:, 32:34]), reads=[b_pb[2]], writes=[b_ge[j]])
                        fo = (i * FCH + mg) * 8
                        S.op("pool", lambda h, ges=ges, fo=fo: h.tensor_copy(ges[:, :, 0:2], fcs[:, fo:fo + 8].rearrange("p (b t) -> p b t", t=2)),
                             reads=[b_fcs], writes=[b_ge[j]])
                        w0, w1, w2, cb = (col("cw0_%d" % i, mg), col("cw1_%d" % i, mg), col("cw2_%d" % i, mg), col("cb_%d" % i, mg))
                        S.op("act", lambda h, ge=ge, ac=ac, w2=w2, cb=cb: h.activation(ac[:, 0:T], ge[:, 2:T + 2], AF.Identity, bias=cb, scale=w2),
                             reads=[b_ge[j], b_cs], writes=[b_ac[j]])
                        S.op("act", lambda h, ges=ges, acs=acs, w2=w2, cb=cb: h.activation(acs, ges[:, :, 2:10], AF.Identity, bias=cb, scale=w2),
                             reads=[b_ge[j], b_cs], writes=[b_ac[j]])
                        S.op("dve", lambda h, ge=ge, ac=ac, w1=w1: h.scalar_tensor_tensor(ac[:, 0:T], ge[:, 1:T + 1], w1, ac[:, 0:T], ALU.mult, ALU.add),
                             reads=[b_ge[j]], writes=[b_ac[j]])
                        S.op("dve", lambda h, ges=ges, acs=acs, w1=w1: h.scalar_tensor_tensor(acs, ges[:, :, 1:9], w1, acs, ALU.mult, ALU.add),
                             reads=[b_ge[j]], writes=[b_ac[j]])
                        S.op("dve", lambda h, ge=ge, ac=ac, w0=w0: h.scalar_tensor_tensor(ac[:, 0:T], ge[:, 0:T], w0, ac[:, 0:T], ALU.mult, ALU.add),
                             reads=[b_ge[j]], writes=[b_ac[j]])
                        S.op("dve", lambda h, ges=ges, acs=acs, w0=w0: h.scalar_tensor_tensor(acs, ges[:, :, 0:8], w0, acs, ALU.mult, ALU.add),
                             reads=[b_ge[j]], writes=[b_ac[j]])
                        S.op("pool", lambda h, ge=ge, mg=mg: h.tensor_copy(convo[:, mg * 10:mg * 10 + 2], ge[:, T:T + 2]),
                             reads=[b_ge[j]], writes=[b_co])
                        S.op("pool", lambda h, ges=ges, mg=mg: h.tensor_copy(convo[:, mg * 10 + 2:mg * 10 + 10].rearrange("p (b t) -> p b t", t=2), ges[:, :, 8:10]),
                             reads=[b_ge[j]], writes=[b_co])
                        S.op("act", lambda h, ac=ac: h.activation(ac, ac, AF.Gelu), reads=[], writes=[b_ac[j]])
                        for ti, (a, b) in enumerate(TGS):
                            S.op("dve", lambda h, ac=ac, m=m, ti=ti, a=a, b=b: h.tensor_tensor(hbuf[:, m, a:b], ac[:, a:b], pbank[3 + ti][:, 0:b - a], ALU.mult),
                                 reads=[b_ac[j], b_pb[3 + ti]], writes=[b_h[m]])
                r0 = g * GW * 128
                items = [(wout[r0:r0 + GW * 128, n0:n0 + 256], GW, 256) for n0 in range(0, D, 256)]
                ws = WStream(items)
                for li in range(len(items)):
                    wv, wb = ws.get(li)
                    for mo in (0, 128):
                        n = (li * 256 + mo) // 128
                        pset = (n % 2) * 3
                        fns = []
                        for kk in range(GW):
                            for ti, (a, b) in enumerate(TGS):
                                fns.append(lambda h, kk=kk, ti=ti, a=a, b=b, mo=mo, wv=wv, pset=pset: h.matmul(
                                    pbank[pset + ti][:, 0:b - a], wv[:, kk, mo:mo + 128], hbuf[:, kk, a:b], start=(kk == 0), stop=(kk == GW - 1)))
                        S.group("pe", fns, reads=[wb] + b_h, writes=b_pb[pset:pset + 3])
                        for ti, (a, b) in enumerate(TGS):
                            S.op("dve", lambda h, n=n, ti=ti, a=a, b=b, pset=pset: h.tensor_tensor(xT[:, n, a:b], xT[:, n, a:b], pbank[pset + ti][:, 0:b - a], ALU.add),
                                 reads=[b_pb[pset + ti]], writes=[b_x[n]])
            S.dma("sp", convT[:, i * FCH * 10:(i + 1) * FCH * 10], convo, reads=[b_co], writes=[b_out])


        def gla(j):
            scrm.reset()
            win, wg2d, wout = gla_w_in[j], gla_w_g2[j], gla_w_out[j]
            glow = scrm.bf16(NT)
            wg2 = scrm.bf16(1024)
            b_glow, b_wg2 = Buf("glow"), Buf("wg2")
            S.dma("pool", wg2[0:16, :], wg2d, writes=[b_wg2])
            gv, gb = wload_flat(gla_wg[j], NCH, 16)
            fns = []
            for kk in range(NCH):
                for ti, (a, b) in enumerate(TGS):
                    fns.append(lambda h, kk=kk, ti=ti, a=a, b=b: h.matmul(pbank[ti][0:16, 0:b - a], gv[:, kk, 0:16], xb[:, kk, a:b],
                                                                         start=(kk == 0), stop=(kk == NCH - 1)))
            S.group("pe", fns, reads=[gb] + b_xb, writes=b_pb[0:3])
            for ti, (a, b) in enumerate(TGS):
                S.op("act", lambda h, ti=ti, a=a, b=b: h.copy(glow[0:16, a:b], pbank[ti][0:16, 0:b - a]), reads=[b_pb[ti]], writes=[b_glow])
            mark0 = scrm.p
            NTILE = 9
            for hd in range(4):
                S.barrier()
                scrm.p = mark0
                qd = scrm.bf16(2 * NT).rearrange("p (c t) -> p c t", c=2)
                ki = scrm.bf16(2 * NT).rearrange("p (c t) -> p c t", c=2)
                kltm = scrm.bf16(NTILE * 256).rearrange("p (n d) -> p n d", n=NTILE)
                vtm = scrm.bf16(NTILE * 512).rearrange("p (n d) -> p n d", n=NTILE)
                gr = scrm.bf16(4 * NT).rearrange("p (c t) -> p c t", c=4)
                Sf = scrm.f32(1024).rearrange("p (c d) -> p c d", c=2)
                Sb = scrm.bf16(1024).rearrange("p (c d) -> p c d", c=2)
                elast = scrm.f32(2 * 20).rearrange("p (c n) -> p c n", c=2)
                b_qd, b_ki, b_kl, b_v, b_gr, b_S, b_Sb, b_el = (Buf("qd"), Buf("ki"), Buf("kl"), Buf("v"), Buf("gr"), Buf("S"), Buf("Sb"), Buf("el"))
                mark1 = scrm.p
                cum = scrm.f32(NT)
                ep = scrm.f32(NT)
                en = scrm.f32(NT)
                klf = scrm.bf16(NT)
                b_cum, b_ep, b_en, b_klf = Buf("cum"), Buf("ep"), Buf("en"), Buf("klf")
                qv, qb = wload(win[:, hd * 256:(hd + 1) * 256], NCH, 256)
                kv, kb = wload(win[:, 1024 + hd * 256:1024 + (hd + 1) * 256], NCH, 256)
                for dc in range(2):
                    cidx = hd * 2 + dc
                    fns = []
                    for ti, (a, b) in enumerate(TGS):
                        fns.append(lambda h, ti=ti, a=a, b=b, cidx=cidx: h.matmul(pbank[ti][:, 0:b - a], wg2[0:16, cidx * 128:(cidx + 1) * 128],
                                                                                 glow[0:16, a:b], start=True, stop=True))
                    S.group("pe", fns, reads=[b_wg2, b_glow], writes=b_pb[0:3])
                    for ti, (a, b) in enumerate(TGS):
                        S.op("act", lambda h, ti=ti, a=a, b=b, cidx=cidx: h.activation(cum[:, a:b], pbank[ti][:, 0:b - a], AF.Identity,
                                                                                       bias=col("bg%d" % j, cidx), scale=1.0),
                             reads=[b_pb[ti], b_cs], writes=[b_cum])
                    S.op("act", lambda h: h.activation(ep, cum, AF.Exp, scale=-1.0), reads=[b_cum], writes=[b_ep])
                    S.op("act", lambda h: h.activation(ep, ep, AF.Ln, bias=1.0, scale=1.0), reads=[], writes=[b_ep])
                    S.op("dve", lambda h: h.tensor_scalar(en, ep, col("m16"), None, ALU.mult), reads=[b_ep], writes=[b_en])
                    S.op("dve", lambda h: h.tensor_tensor_scan(cum, col("reset", 0, NT), en, 0.0, ALU.mult, ALU.add),
                         reads=[b_en, b_cs], writes=[b_cum])
                    S.op("act", lambda h: h.activation(ep, cum, AF.Exp), reads=[b_cum], writes=[b_ep])
                    S.op("act", lambda h: h.activation(en, cum, AF.Exp, scale=-1.0), reads=[b_cum], writes=[b_en])
                    S.op("dve", lambda h, dc=dc: h.tensor_copy(elast[:, dc, 0:16], ep[:, 63:T:64]), reads=[b_ep], writes=[b_el])
                    S.op("dve", lambda h, dc=dc: h.tensor_copy(elast[:, dc, 16:20], ep[:, T + 7:NT:8]), reads=[b_ep], writes=[b_el])
                    fns = []
                    for kk in range(NCH):
                        for ti, (a, b) in enumerate(TGS):
                            fns.append(lambda h, kk=kk, ti=ti, a=a, b=b, dc=dc: h.matmul(pbank[3 + ti][:, 0:b - a], kv[:, kk, dc * 128:(dc + 1) * 128],
                                                                                         xb[:, kk, a:b], start=(kk == 0), stop=(kk == NCH - 1)))
                    S.group("pe", fns, reads=[kb] + b_xb, writes=b_pb[3:6])
                    for ti, (a, b) in enumerate(TGS):
                        S.op("dve", lambda h, ti=ti, a=a, b=b, dc=dc: h.tensor_tensor(ki[:, dc, a:b], pbank[3 + ti][:, 0:b - a], en[:, a:b], ALU.mult),
                             reads=[b_pb[3 + ti], b_en], writes=[b_ki])
                    S.op("dve", lambda h, dc=dc: h.tensor_tensor(klf[:, 0:T].rearrange("p (c t) -> p c t", t=64), ki[:, dc, 0:T].rearrange("p (c t) -> p c t", t=64),
                                                                 elast[:, dc, 0:16].unsqueeze(2).to_broadcast([128, 16, 64]), ALU.mult),
                         reads=[b_ki, b_el], writes=[b_klf])
                    S.op("dve", lambda h, dc=dc: h.tensor_tensor(klf[:, T:NT].rearrange("p (c t) -> p c t", t=8), ki[:, dc, T:NT].rearrange("p (c t) -> p c t", t=8),
                                                                 elast[:, dc, 16:20].unsqueeze(2).to_broadcast([128, 4, 8]), ALU.mult),
                         reads=[b_ki, b_el], writes=[b_klf])
                    ptr = pbank[6][:].bitcast(BF16)
                    for half in range(3):
                        tiles = list(range(half * 4, min(NTILE, half * 4 + 4)))
                        fns = []
                        for n in tiles:
                            w = 128 if n < 8 else 32
                            fns.append(lambda h, n=n, w=w, half=half: h.transpose(ptr[0:w, (n - half * 4) * 128:(n - half * 4) * 128 + 128],
                                                                                  klf[:, n * 128:n * 128 + w], identb[:]))
                        S.group("pe", fns, reads=[b_klf], writes=[b_pb[6]])
                        for n in tiles:
                            w = 128 if n < 8 else 32
                            S.op("act", lambda h, n=n, w=w, half=half, dc=dc: h.copy(kltm[0:w, n, dc * 128:(dc + 1) * 128],
                                                                                     ptr[0:w, (n - half * 4) * 128:(n - half * 4) * 128 + 128]),
                                 reads=[b_pb[6]], writes=[b_kl])
                    fns = []
                    for kk in range(NCH):
                        for ti, (a, b) in enumerate(TGS):
                            fns.append(lambda h, kk=kk, ti=ti, a=a, b=b, dc=dc: h.matmul(pbank[ti][:, 0:b - a], qv[:, kk, dc * 128:(dc + 1) * 128],
                                                                                         xb[:, kk, a:b], start=(kk == 0), stop=(kk == NCH - 1)))
                    S.group("pe", fns, reads=[qb] + b_xb, writes=b_pb[0:3])
                    for ti, (a, b) in enumerate(TGS):
                        S.op("dve", lambda h, ti=ti, a=a, b=b, dc=dc: h.scalar_tensor_tensor(qd[:, dc, a:b], pbank[ti][:, 0:b - a], col("qs"), ep[:, a:b],
                                                                                             ALU.mult, ALU.mult),
                             reads=[b_pb[ti], b_ep], writes=[b_qd])
                for hf in range(2):
                    vv, vb = wload(win[:, 2048 + hd * 512 + hf * 256:2048 + hd * 512 + (hf + 1) * 256], NCH, 256)
                    for n in range(NTILE):
                        w = 128 if n < 8 else 32
                        pbk = 6 + (n % 2)
                        fns = []
                        for kk in range(NCH):
                            fns.append(lambda h, kk=kk, n=n, w=w, pbk=pbk, vv=vv: h.matmul(pbank[pbk][0:w, 0:256], xb[:, kk, n * 128:n * 128 + w], vv[:, kk, :],
                                                                                           start=(kk == 0), stop=(kk == NCH - 1)))
                        S.group("pe", fns, reads=[vb] + b_xb, writes=[b_pb[pbk]])
                        S.op("act", lambda h, n=n, w=w, pbk=pbk, hf=hf: h.copy(vtm[0:w, n, hf * 256:(hf + 1) * 256], pbank[pbk][0:w, 0:256]),
                             reads=[b_pb[pbk]], writes=[b_v])
                S.barrier()
                scrm.p = mark1
                at = [scrm.bf16(128), scrm.bf16(128)]
                sq = scrm.f32(512)
                rst = scrm.f32(128)
                t1 = scrm.f32(512)
                s0f = [scrm.f32(1024).rearrange("p (c d) -> p c d", c=2)] * 2
                s0b = [scrm.bf16(1024).rearrange("p (c d) -> p c d", c=2)] * 2
                klm = scrm.bf16(4 * 256).rearrange("p (b d) -> p b d", b=4)
                srecv = scrm.f32(1024).rearrange("p (c d) -> p c d", c=2)
                b_at = [Buf("at0"), Buf("at1")]
                b_sq, b_rst, b_t1, b_klm, b_srecv = Buf("sq"), Buf("rst"), Buf("t1"), Buf("klm"), Buf("srecv")
                b_s0f = [Buf("s0f0")] * 2
                b_s0b = [Buf("s0b0")] * 2
                b_gin, b_gout = Buf("gin"), Buf("gout")

                sp_state = {"banks": [3, 4, 6, 7], "i": 0}

                def state_update(n, r0, r1, ci, Sdst, b_Sdst, Ssrc, b_Ssrc, lhs_fn):
                    for dc in range(2):
                        bk = sp_state["banks"][sp_state["i"] % len(sp_state["banks"])]
                        sp_state["i"] += 1
                        S.group("pe", [lambda h, dc=dc, bk=bk: h.matmul(pbank[bk][:, 0:512], lhs_fn(dc), vtm[r0:r1, n, :], start=True, stop=True)],
                                reads=[b_kl, b_v, b_klm], writes=[b_pb[bk]])
                        S.op("dve", lambda h, dc=dc, bk=bk: h.scalar_tensor_tensor(Sdst[:, dc, :], Ssrc[:, dc, :], elast[:, dc, ci:ci + 1], pbank[bk][:, 0:512],
                                                                                   ALU.mult, ALU.add),
                             reads=[b_pb[bk], b_el, b_Ssrc], writes=[b_Sdst])

                def prompt_scan(want_o):
                    for n in range(8):
                        if want_o:
                            pa = pbank[0][:, (n % 2) * 128:(n % 2) * 128 + 128]
                            fns = [lambda h, dc=dc, n=n, pa=pa: h.matmul(pa, ki[:, dc, n * 128:(n + 1) * 128], qd[:, dc, n * 128:(n + 1) * 128],
                                                                        start=(dc == 0), stop=(dc == 1)) for dc in range(2)]
                            S.group("pe", fns, reads=[b_ki, b_qd], writes=[b_pb[0]])
                            a_ = at[n % 2]
                            S.op("dve", lambda h, pa=pa, a_=a_: h.tensor_tensor(a_, pa, col("tmask", 0, 128), ALU.mult),
                                 reads=[b_pb[0], b_cs], writes=[b_at[n % 2]])
                            ob = 1 + (n % 2)
                            po = pbank[ob][:, 0:512].rearrange("p (c t) -> p c t", c=4)
                        for cc in range(2):
                            ci = 2 * n + cc
                            r0 = cc * 64
                            if want_o:
                                fns = []
                                for dvc in range(4):
                                    fns.append(lambda h, dvc=dvc, cc=cc, n=n, a_=a_, po=po: h.matmul(po[:, dvc, cc * 64:cc * 64 + 64], vtm[:, n, dvc * 128:(dvc + 1) * 128],
                                                                                                   a_[:, cc * 64:cc * 64 + 64], start=True, stop=False))
                                    for dc in range(2):
                                        fns.append(lambda h, dvc=dvc, dc=dc, cc=cc, n=n, po=po: h.matmul(po[:, dvc, cc * 64:cc * 64 + 64], Sb[:, dc, dvc * 128:(dvc + 1) * 128],
                                                                                                       qd[:, dc, n * 128 + cc * 64:n * 128 + cc * 64 + 64], start=False, stop=(dc == 1)))
                                S.group("pe", fns, reads=[b_v, b_at[n % 2], b_Sb, b_qd], writes=[b_pb[ob]])
                            state_update(n, r0, r0 + 64, ci, Sf, b_S, Sf, b_S, lambda dc, n=n, r0=r0: kltm[r0:r0 + 64, n, dc * 128:(dc + 1) * 128])
                            if want_o:
                                S.op("act", lambda h: h.copy(Sb[:], Sf[:]), reads=[b_S], writes=[b_Sb])
                        if want_o:
                            finish_tile(n, 128, ob, po)

                def finish_tile(n, w, ob, po):
                    c0 = n * 128
                    sqv = sq.rearrange("p (c t) -> p c t", c=4)
                    S.op("act", lambda h: h.activation(sqv[:, :, 0:w], po[:, :, 0:w], AF.Square), reads=[b_pb[ob]], writes=[b_sq])
                    fns = [lambda h, dvc=dvc: h.matmul(pbank[5][:, 0:w], col("ones", 0, 128), sqv[:, dvc, 0:w], start=(dvc == 0), stop=(dvc == 3)) for dvc in range(4)]
                    S.group("pe", fns, reads=[b_sq, b_cs], writes=[b_pb[5]])
                    S.op("dve", lambda h: h.tensor_scalar(rst[:, 0:w], pbank[5][:, 0:w], col("i512"), col("eps"), ALU.mult, ALU.add), reads=[b_pb[5]], writes=[b_rst])
                    S.op("act", lambda h: h.activation(rst[:, 0:w], rst[:, 0:w], AF.Sqrt), reads=[], writes=[b_rst])
                    S.op("dve", lambda h: h.reciprocal(rst[:, 0:w], rst[:, 0:w]), reads=[], writes=[b_rst])
                    t1v = t1.rearrange("p (c t) -> p c t", c=4)
                    for dvc in range(4):
                        S.op("dve", lambda h, dvc=dvc: h.scalar_tensor_tensor(t1v[:, dvc, 0:w], po[:, dvc, 0:w], col("gnw%d" % j, dvc), rst[:, 0:w], ALU.mult, ALU.mult),
                             reads=[b_pb[ob], b_rst, b_cs], writes=[b_t1])
                    S.op("pool", lambda h: h.tensor_tensor(gr[:, :, c0:c0 + w], t1v[:, :, 0:w], gr[:, :, c0:c0 + w], ALU.mult), reads=[b_t1], writes=[b_gr])

                S.op("pool", lambda h: h.memset(Sf[:], 0.0), writes=[b_S])
                sp_state["banks"] = [0, 1, 2, 3, 4, 5, 6, 7]
                prompt_scan(False)
                sp_state["banks"] = [3, 4, 6, 7]
                xi = j * 4 + hd
                S.dma("sp", gs_in[xi], Sf[:].rearrange("p c d -> p (c d)"), reads=[b_S], writes=[b_gin])
                S.allgather(gs_in[xi], gs_out[xi], reads=[b_gin], writes=[b_gout])
                for hf in range(2):
                    rv, rb = wload(win[:, 4096 + hd * 512 + hf * 256:4096 + hd * 512 + (hf + 1) * 256], NCH, 256)
                    for mo in (0, 128):
                        dvc = hf * 2 + mo // 128
                        pset = (dvc % 2) * 3
                        fns = []
                        for kk in range(NCH):
                            for ti, (a, b) in enumerate(TGS):
                                fns.append(lambda h, kk=kk, ti=ti, a=a, b=b, mo=mo, rv=rv, pset=pset: h.matmul(pbank[pset + ti][:, 0:b - a], rv[:, kk, mo:mo + 128],
                                                                                                               xb[:, kk, a:b], start=(kk == 0), stop=(kk == NCH - 1)))
                        S.group("pe", fns, reads=[rb] + b_xb, writes=b_pb[pset:pset + 3])
                        for ti, (a, b) in enumerate(TGS):
                            S.op("act", lambda h, ti=ti, a=a, b=b, dvc=dvc, pset=pset: h.activation(gr[:, dvc, a:b], pbank[pset + ti][:, 0:b - a], AF.Silu),
                                 reads=[b_pb[pset + ti]], writes=[b_gr])
                S.dma("sp", srecv[:].rearrange("p c d -> p (c d)"), gs_out[xi][0:128, :], reads=[b_gout], writes=[b_srecv])
                S.op("dve", lambda h: h.tensor_scalar(Sf[:], srecv[:], flag, None, ALU.mult), reads=[b_srecv, b_cs], writes=[b_S])
                S.op("act", lambda h: h.copy(Sb[:], Sf[:]), reads=[b_S], writes=[b_Sb])
                prompt_scan(True)
                for dc in range(2):
                    S.dma("sp", gla_sp[j, hd, dc * 128:(dc + 1) * 128, :], Sf[:, dc, :], reads=[b_S], writes=[b_out])
                n = 8
                pa = pbank[0][0:32, 0:32]
                fns = [lambda h, dc=dc: h.matmul(pa, ki[:, dc, T:NT], qd[:, dc, T:NT], start=(dc == 0), stop=(dc == 1)) for dc in range(2)]
                S.group("pe", fns, reads=[b_ki, b_qd], writes=[b_pb[0]])
                a_ = at[0]
                S.op("dve", lambda h: h.tensor_tensor(a_[0:32, 0:32], pa, col("smask", 0, 32)[0:32, :], ALU.mult), reads=[b_pb[0], b_cs], writes=[b_at[0]])
                for bb in range(4):
                    S.op("dve", lambda h, bb=bb: h.tensor_scalar(klm[0:32, bb, :], kltm[0:32, 8, :], col("rowm", bb)[0:32, :], None, ALU.mult),
                         reads=[b_kl, b_cs], writes=[b_klm])
                ob = 1
                po = pbank[ob][:, 0:512].rearrange("p (c t) -> p c t", c=4)
                first = [True] * 4
                for bb in range(4):
                    sj = bb % 2
                    for dc in range(2):
                        S.dma("sp", s0f[sj][:, dc, :], gla_s0[j, bb, hd, dc * 128:(dc + 1) * 128, :], writes=[b_s0f[sj]])
                    S.op("act", lambda h, sj=sj: h.copy(s0b[sj][:], s0f[sj][:]), reads=[b_s0f[sj]], writes=[b_s0b[sj]])
                    fns = []
                    for dvc in range(4):
                        fns.append(lambda h, dvc=dvc, bb=bb: h.matmul(po[:, dvc, bb * 8:bb * 8 + 8], vtm[0:32, 8, dvc * 128:(dvc + 1) * 128],
                                                                      a_[0:32, bb * 8:bb * 8 + 8], start=True, stop=False))
                        for dc in range(2):
                            fns.append(lambda h, dvc=dvc, dc=dc, bb=bb, sj=sj: h.matmul(po[:, dvc, bb * 8:bb * 8 + 8], s0b[sj][:, dc, dvc * 128:(dvc + 1) * 128],
                                                                                       qd[:, dc, T + bb * 8:T + bb * 8 + 8], start=False, stop=(dc == 1)))
                    S.group("pe", fns, reads=[b_v, b_at[0], b_s0b[sj], b_qd], writes=[b_pb[ob]])
                    state_update(8, 0, 32, 16 + bb, s0f[sj], b_s0f[sj], s0f[sj], b_s0f[sj], lambda dc, bb=bb: klm[0:32, bb, dc * 128:(dc + 1) * 128])
                    for dc in range(2):
                        S.dma("sp", gla_ss[j, bb, hd, dc * 128:(dc + 1) * 128, :], s0f[sj][:, dc, :], reads=[b_s0f[sj]], writes=[b_out])
                finish_tile(8, 32, ob, po)
                for hf in range(2):
                    wv, wb = wload(wout[hd * 512:(hd + 1) * 512, hf * 1024:(hf + 1) * 1024], 4, 1024)
                    for mo in range(0, 1024, 128):
                        nn = (hf * 1024 + mo) // 128
                        pset = (nn % 2) * 3
                        fns = []
                        for kk in range(4):
                            for ti, (a, b) in enumerate(TGS):
                                fns.append(lambda h, kk=kk, ti=ti, a=a, b=b, mo=mo, wv=wv, pset=pset: h.matmul(pbank[pset + ti][:, 0:b - a], wv[:, kk, mo:mo + 128],
                                                                                                               gr[:, kk, a:b], start=(kk == 0), stop=(kk == 3)))
                        S.group("pe", fns, reads=[wb, b_gr], writes=b_pb[pset:pset + 3])
                        for ti, (a, b) in enumerate(TGS):
                            S.op("dve", lambda h, nn=nn, ti=ti, a=a, b=b, pset=pset: h.tensor_tensor(xT[:, nn, a:b], xT[:, nn, a:b], pbank[pset + ti][:, 0:b - a], ALU.add),
                                 reads=[b_pb[pset + ti]], writes=[b_x[nn]])


        def sgmix():
            scrm.reset()
            win, wout = sg_w_in[0], sg_w_out[0]
            vb = scrm.bf16(NCH * NT).rearrange("p (c t) -> p c t", c=NCH)
            vs32 = scrm.f32(NCH * 32).rearrange("p (c t) -> p c t", c=NCH)
            b_vb = [Buf("vb%d" % c) for c in range(NCH)]
            b_vs = Buf("vs32")
            markA = scrm.p
            tmp = [scrm.f32(NT), scrm.f32(NT)]
            s1 = scrm.f32(NT)
            s2 = scrm.f32(NT)
            ts = scrm.f32(32)
            b_tmp = [Buf("sgt0"), Buf("sgt1")]
            b_s1, b_s2, b_ts = Buf("sgs1"), Buf("sgs2"), Buf("sgts")
            ws = WStream([(win[:, D + n0:D + n0 + 256], NCH, 256) for n0 in range(0, D, 256)])
            for li in range(8):
                wv, wb = ws.get(li)
                for mo in (0, 128):
                    c = (li * 256 + mo) // 128
                    pset = (c % 2) * 3
                    jt = c % 2
                    fns = []
                    for kk in range(NCH):
                        for ti, (a, b) in enumerate(TGS):
                            fns.append(lambda h, kk=kk, ti=ti, a=a, b=b, mo=mo, wv=wv, pset=pset: h.matmul(pbank[pset + ti][:, 0:b - a], wv[:, kk, mo:mo + 128],
                                                                                                           xb[:, kk, a:b], start=(kk == 0), stop=(kk == NCH - 1)))
                    S.group("pe", fns, reads=[wb] + b_xb, writes=b_pb[pset:pset + 3])
                    for ti, (a, b) in enumerate(TGS):
                        S.op("act", lambda h, ti=ti, a=a, b=b, pset=pset, jt=jt, c=c: h.activation(tmp[jt][:, a:b], pbank[pset + ti][:, 0:b - a], AF.Gelu,
                                                                                                   bias=col("sgbi", NCH + c), scale=1.0),
                             reads=[b_pb[pset + ti], b_cs], writes=[b_tmp[jt]])
                    S.op("dve", lambda h, c=c, jt=jt: h.tensor_copy(vb[:, c, :], tmp[jt]), reads=[b_tmp[jt]], writes=[b_vb[c]])
                    S.op("dve", lambda h, c=c, jt=jt: h.tensor_copy(vs32[:, c, :], tmp[jt][:, T:NT]), reads=[b_tmp[jt]], writes=[b_vs])
                    if c == 0:
                        S.op("pool", lambda h, jt=jt: h.tensor_copy(s1, tmp[jt]), reads=[b_tmp[jt]], writes=[b_s1])
                    else:
                        S.op("pool", lambda h, jt=jt: h.tensor_tensor(s1, s1, tmp[jt], ALU.add), reads=[b_tmp[jt]], writes=[b_s1])
                    S.op("act", lambda h, jt=jt: h.activation(tmp[jt], tmp[jt], AF.Square), reads=[], writes=[b_tmp[jt]])
                    if c == 0:
                        S.op("pool", lambda h, jt=jt: h.tensor_copy(s2, tmp[jt]), reads=[b_tmp[jt]], writes=[b_s2])
                    else:
                        S.op("pool", lambda h, jt=jt: h.tensor_tensor(s2, s2, tmp[jt], ALU.add), reads=[b_tmp[jt]], writes=[b_s2])
            fns = [lambda h, ti=ti, a=a, b=b: h.matmul(pbank[ti][:, 0:b - a], onesd, s1[:, a:b], start=True, stop=True) for ti, (a, b) in enumerate(TGS)]
            S.group("pe", fns, reads=[b_cs, b_s1], writes=b_pb[0:3])
            fns = [lambda h, ti=ti, a=a, b=b: h.matmul(pbank[3 + ti][:, 0:b - a], onesd, s2[:, a:b], start=True, stop=True) for ti, (a, b) in enumerate(TGS)]
            S.group("pe", fns, reads=[b_cs, b_s2], writes=b_pb[3:6])
            for ti, (a, b) in enumerate(TGS):
                S.op("act", lambda h, ti=ti, a=a, b=b: h.copy(s1[:, a:b], pbank[ti][:, 0:b - a]), reads=[b_pb[ti]], writes=[b_s1])
            S.op("dve", lambda h: h.tensor_tensor(s2, s1, s1, ALU.mult), reads=[b_s1], writes=[b_s2])
            for ti, (a, b) in enumerate(TGS):
                S.op("dve", lambda h, ti=ti, a=a, b=b: h.tensor_tensor(s2[:, a:b], pbank[3 + ti][:, 0:b - a], s2[:, a:b], ALU.subtract),
                     reads=[b_pb[3 + ti]], writes=[b_s2])
            S.op("dve", lambda h: h.tensor_scalar(s2, s2, col("eps"), None, ALU.add), reads=[], writes=[b_s2])
            S.op("act", lambda h: h.activation(s2, s2, AF.Sqrt), reads=[], writes=[b_s2])
            S.op("dve", lambda h: h.reciprocal(s2, s2), reads=[], writes=[b_s2])
            for c in range(NCH):
                jt = c % 2
                S.op("dve", lambda h, c=c, jt=jt: h.tensor_tensor(tmp[jt], vb[:, c, :], s1, ALU.subtract), reads=[b_vb[c], b_s1], writes=[b_tmp[jt]])
                S.op("pool", lambda h, jt=jt: h.tensor_tensor(tmp[jt], tmp[jt], s2, ALU.mult), reads=[b_s2], writes=[b_tmp[jt]])
                S.op("act", lambda h, c=c, jt=jt: h.activation(vb[:, c, :], tmp[jt], AF.Identity, bias=col("sglb", c), scale=col("sglg", c)),
                     reads=[b_tmp[jt], b_cs], writes=[b_vb[c]])
                S.op("dve", lambda h, c=c: h.tensor_tensor(ts, vs32[:, c, :], s1[:, T:NT], ALU.subtract), reads=[b_vs, b_s1], writes=[b_ts])
                S.op("dve", lambda h: h.tensor_tensor(ts, ts, s2[:, T:NT], ALU.mult), reads=[b_s2], writes=[b_ts])
                S.op("act", lambda h, c=c: h.activation(vs32[:, c, :], ts, AF.Identity, bias=col("sglb", c), scale=col("sglg", c)),
                     reads=[b_ts, b_cs], writes=[b_vs])
            S.dma("sp", sgv, vs32[:].rearrange("p c t -> p (c t)"), reads=[b_vs], writes=[b_out])
            S.barrier()
            scrm.p = markA
            vtm = [scrm.bf16(D), scrm.bf16(D)]
            utmp = [scrm.f32(NT), scrm.f32(NT)]
            sgcs = scrm.f32(1280)
            wsm = scrm.bf16(512).rearrange("p (g t) -> p g t", g=4)
            wssm = scrm.bf16(128).rearrange("p (g t) -> p g t", g=4)
            b_vtm = [Buf("vtm0"), Buf("vtm1")]
            b_ut = [Buf("ut0"), Buf("ut1")]
            b_sgcs, b_wsm = Buf("sgcs"), Buf("wsm")
            S.dma("sp", sgcs, sgc, writes=[b_sgcs])
            wsT = sgcs[:, 0:512].rearrange("p (g t) -> p g t", g=4)
            wsTs = sgcs[:, 512:640].rearrange("p (g t) -> p g t", g=4)
            bsb = sgcs[:, 640:1152].rearrange("p (g t) -> p g t", g=4)
            bsbs = sgcs[:, 1152:1280].rearrange("p (g t) -> p g t", g=4)
            for g in range(4):
                S.op("dve", lambda h, g=g: h.tensor_tensor(wsm[:, g, :], wsT[:, g, :], col("tri", 0, 128), ALU.mult), reads=[b_sgcs, b_cs], writes=[b_wsm])
                S.op("dve", lambda h, g=g: h.tensor_tensor(wssm[0:32, g, :], wsTs[0:32, g, :], col("smask", 0, 32)[0:32, :], ALU.mult),
                     reads=[b_sgcs, b_cs], writes=[b_wsm])
            ptr = [pbank[6][:].bitcast(BF16), pbank[7][:].bitcast(BF16)]
            for n in range(9):
                w = 128 if n < 8 else 32
                c0 = n * 128
                jt = n % 2
                for hb in range(2):
                    fns = [lambda h, c=c, hb=hb, w=w, c0=c0: h.transpose(ptr[hb][0:w, (c % 8) * 128:(c % 8) * 128 + 128], vb[:, c, c0:c0 + w], identb[:])
                           for c in range(hb * 8, hb * 8 + 8)]
                    S.group("pe", fns, reads=b_vb[hb * 8:hb * 8 + 8], writes=[b_pb[6 + hb]])
                S.op("act", lambda h, jt=jt, w=w: h.copy(vtm[jt][0:w, 0:1024], ptr[0][0:w, :]), reads=[b_pb[6]], writes=[b_vtm[jt]])
                S.op("dve", lambda h, jt=jt, w=w: h.tensor_copy(vtm[jt][0:w, 1024:2048], ptr[1][0:w, :]), reads=[b_pb[7]], writes=[b_vtm[jt]])
                for g in range(4):
                    rhsw = wsm[0:w, g, 0:w] if n < 8 else wssm[0:32, g, 0:32]
                    fns = [lambda h, c=c, g=g, jt=jt, rhsw=rhsw, w=w: h.matmul(pbank[g][:, (c % 4) * 128:(c % 4) * 128 + w], vtm[jt][0:w, c * 128:(c + 1) * 128], rhsw,
                                                                          start=True, stop=True) for c in range(4 * g, 4 * g + 4)]
                    S.group("pe", fns, reads=[b_vtm[jt], b_wsm], writes=[b_pb[g]])
                    bias = (bsb[:, g, 0:w] if n < 8 else bsbs[:, g, 0:32]).unsqueeze(1).to_broadcast([128, 4, w])
                    S.op("dve", lambda h, g=g, bias=bias, w=w, c0=c0: h.tensor_tensor(vb[:, 4 * g:4 * g + 4, c0:c0 + w],
                                                                          pbank[g][:, 0:512].rearrange("p (c t) -> p c t", c=4)[:, :, 0:w], bias, ALU.add),
                         reads=[b_pb[g], b_sgcs], writes=b_vb[4 * g:4 * g + 4])
            ws = WStream([(win[:, n0:n0 + 256], NCH, 256) for n0 in range(0, D, 256)])
            for li in range(8):
                wv, wb = ws.get(li)
                for mo in (0, 128):
                    c = (li * 256 + mo) // 128
                    pset = (c % 2) * 3
                    jt = c % 2
                    fns = []
                    for kk in range(NCH):
                        for ti, (a, b) in enumerate(TGS):
                            fns.append(lambda h, kk=kk, ti=ti, a=a, b=b, mo=mo, wv=wv, pset=pset: h.matmul(pbank[pset + ti][:, 0:b - a], wv[:, kk, mo:mo + 128],
                                                                                                           xb[:, kk, a:b], start=(kk == 0), stop=(kk == NCH - 1)))
                    S.group("pe", fns, reads=[wb] + b_xb, writes=b_pb[pset:pset + 3])
                    for ti, (a, b) in enumerate(TGS):
                        S.op("act", lambda h, ti=ti, a=a, b=b, pset=pset, jt=jt, c=c: h.activation(utmp[jt][:, a:b], pbank[pset + ti][:, 0:b - a], AF.Gelu,
                                                                                                   bias=col("sgbi", c), scale=1.0),
                             reads=[b_pb[pset + ti], b_cs], writes=[b_ut[jt]])
                    S.op("dve", lambda h, c=c, jt=jt: h.tensor_tensor(vb[:, c, :], utmp[jt], vb[:, c, :], ALU.mult), reads=[b_ut[jt]], writes=[b_vb[c]])
            ws = WStream([(wout[:, n0:n0 + 256], NCH, 256) for n0 in range(0, D, 256)])
            for li in range(8):
                wv, wb = ws.get(li)
                for mo in (0, 128):
                    nn = (li * 256 + mo) // 128
                    pset = (nn % 2) * 3
                    fns = []
                    for kk in range(NCH):
                        for ti, (a, b) in enumerate(TGS):
                            fns.append(lambda h, kk=kk, ti=ti, a=a, b=b, mo=mo, wv=wv, pset=pset: h.matmul(pbank[pset + ti][:, 0:b - a], wv[:, kk, mo:mo + 128],
                                                                                                           vb[:, kk, a:b], start=(kk == 0), stop=(kk == NCH - 1)))
                    S.group("pe", fns, reads=[wb] + b_vb, writes=b_pb[pset:pset + 3])
                    for ti, (a, b) in enumerate(TGS):
                        S.op("dve", lambda h, nn=nn, ti=ti, a=a, b=b, pset=pset: h.scalar_tensor_tensor(xT[:, nn, a:b], pbank[pset + ti][:, 0:b - a], col("sgbo", nn),
                                                                                                        xT[:, nn, a:b], ALU.add, ALU.add),
                             reads=[b_pb[pset + ti], b_cs], writes=[b_x[nn]])


        def swa():
            scrm.reset()
            wqkv, wout = swa_w_qkv[0], swa_w_out[0]
            swcs = scrm.f32(SWC_N)
            b_swcs = Buf("swcs")
            S.dma("sp", swcs, swc, writes=[b_swcs])
            cosT, sinT = swcs[:, 0:NT], swcs[:, NT:2 * NT]
            o_ = 2 * NT
            PTf = swcs[:, o_:o_ + 128]
            mask01 = swcs[:, o_ + 128:o_ + 384]
            masks = swcs[:, o_ + 384:o_ + 1024].rearrange("p (b k) -> p b k", b=4)
            bvrow = swcs[:, o_ + 1024:o_ + 1536]
            PTb = scrm.bf16(128)
            mask0 = scrm.f32(256)
            b_PTb, b_mask0 = Buf("PTb"), Buf("mask0")
            S.op("dve", lambda h: h.tensor_copy(PTb, PTf), reads=[b_swcs], writes=[b_PTb])
            S.op("dve", lambda h: h.tensor_scalar(mask0[:, 0:128], mask01[:, 0:128], flag, None, ALU.mult), reads=[b_swcs, b_cs], writes=[b_mask0])
            S.op("dve", lambda h: h.tensor_copy(mask0[:, 128:256], mask01[:, 128:256]), reads=[b_swcs], writes=[b_mask0])
            vz = scrm.bf16(10 * 192).rearrange("p (n d) -> p n d", n=10)
            vcz = scrm.bf16(4 * 192).rearrange("p (n d) -> p n d", n=4)
            b_vz, b_vcz = Buf("vz"), Buf("vcz")
            S.op("pool", lambda h: h.memset(vz[:], 0.0), writes=[b_vz])
            S.op("pool", lambda h: h.memset(vcz[:], 0.0), writes=[b_vcz])
            kT = scrm.bf16(128 + NT)
            qT = scrm.bf16(2 * NT).rearrange("p (c t) -> p c t", c=2)
            kcT = scrm.bf16(512).rearrange("p (b k) -> p b k", b=4)
            raw = [scrm.f32(NT), scrm.f32(NT)]
            rawb = scrm.bf16(NT)
            tcos = scrm.f32(NT)
            kout = scrm.f32(160)
            vout = scrm.f32(128).rearrange("p (n d) -> p n d", n=2)
            xs_ = scrm.f32(192)
            xr = scrm.f32(192)
            b_kT, b_kh, b_kcT, b_rawb, b_tcos, b_kout, b_vout, b_xs, b_xr = (Buf("kT"), Buf("kh"), Buf("kcT"), Buf("rawb"), Buf("tcos"), Buf("kout"),
                                                                              Buf("vout"), Buf("xs"), Buf("xr"))
            b_raw = [Buf("raw0"), Buf("raw1")]
            b_q = [Buf("q%d" % n) for n in range(9)]
            b_si, b_so = Buf("sxi"), Buf("sxo")
            NSET = 2
            e_ = [scrm.f32(256) for _ in range(NSET)]
            pm = [scrm.bf16(256) for _ in range(NSET)]
            pT = [scrm.bf16(256) for _ in range(NSET)]
            dg = [scrm.bf16(128) for _ in range(NSET)]
            stt = [scrm.f32(8) for _ in range(NSET)]
            b_e = [Buf("e%d" % i) for i in range(NSET)]
            b_pm = [Buf("pm%d" % i) for i in range(NSET)]
            b_pT = [Buf("pT%d" % i) for i in range(NSET)]
            b_dg = [Buf("dg%d" % i) for i in range(NSET)]
            b_st = [Buf("st%d" % i) for i in range(NSET)]
            cnt = {"r": 0, "h": 0}

            def rope(pset, bias_col, dst, dst_bufs, scale, side=None):
                j = cnt["r"] % 2
                cnt["r"] += 1
                p2 = 3 - pset
                r_ = raw[j]
                for ti, (a, b) in enumerate(TGS):
                    S.op("act", lambda h, ti=ti, a=a, b=b, r_=r_: h.activation(r_[:, a:b], pbank[pset + ti][:, 0:b - a], AF.Identity, bias=bias_col, scale=1.0),
                         reads=[b_pb[pset + ti], b_cs], writes=[b_raw[j]])
                S.op("dve", lambda h, r_=r_: h.tensor_copy(rawb, r_), reads=[b_raw[j]], writes=[b_rawb])
                fns = [lambda h, ti=ti, a=a, b=b: h.matmul(pbank[p2 + ti][:, 0:b - a], PTb, rawb[:, a:b], start=True, stop=True) for ti, (a, b) in enumerate(TGS)]
                S.group("pe", fns, reads=[b_PTb, b_rawb], writes=b_pb[p2:p2 + 3])
                S.op("dve", lambda h, r_=r_: h.tensor_tensor(tcos, r_, cosT, ALU.mult), reads=[b_raw[j], b_swcs], writes=[b_tcos])
                for ti, (a, b) in enumerate(TGS):
                    S.op("dve", lambda h, ti=ti, a=a, b=b, r_=r_: h.tensor_tensor(r_[:, a:b], pbank[p2 + ti][:, 0:b - a], sinT[:, a:b], ALU.mult),
                         reads=[b_pb[p2 + ti], b_swcs], writes=[b_raw[j]])
                S.op("pool", lambda h, r_=r_: h.tensor_tensor(r_, r_, tcos, ALU.add), reads=[b_tcos], writes=[b_raw[j]])
                S.op("act", lambda h, r_=r_: h.activation(dst, r_, AF.Identity, bias=0.0, scale=scale), reads=[b_raw[j]], writes=dst_bufs)
                if side is not None:
                    side(r_, b_raw[j])

            def softmax_rows(nr, ncols, ps, b_ps, maskap, mask_bufs, sink_col):
                si = cnt["h"] % NSET
                cnt["h"] += 1
                st = stt[si]
                S.op("dve", lambda h: h.tensor_reduce(st[0:nr, 0:1], ps[0:nr, 0:ncols], AX.X, ALU.max), reads=[b_ps], writes=[b_st[si]])
                S.op("dve", lambda h: h.tensor_scalar(st[0:nr, 1:2], st[0:nr, 0:1], -1.0, None, ALU.mult), reads=[], writes=[b_st[si]])
                S.op("act", lambda h: h.activation(e_[si][0:nr, 0:ncols], ps[0:nr, 0:ncols], AF.Exp, bias=st[0:nr, 1:2], scale=1.0),
                     reads=[b_ps, b_st[si]], writes=[b_e[si]])
                S.op("dve", lambda h: h.tensor_tensor(e_[si][0:nr, 0:ncols], e_[si][0:nr, 0:ncols], maskap, ALU.mult), reads=mask_bufs, writes=[b_e[si]])
                S.op("dve", lambda h: h.tensor_reduce(st[0:nr, 2:3], e_[si][0:nr, 0:ncols], AX.X, ALU.add), reads=[b_e[si]], writes=[b_st[si]])
                S.op("act", lambda h: h.copy(pm[si][0:nr, 0:ncols], e_[si][0:nr, 0:ncols]), reads=[b_e[si]], writes=[b_pm[si]])
                S.op("act", lambda h: h.activation(st[0:nr, 3:4], sink_col[0:nr, :], AF.Exp, bias=st[0:nr, 1:2], scale=1.0), reads=[b_cs], writes=[b_st[si]])
                S.op("dve", lambda h: h.tensor_tensor(st[0:nr, 2:3], st[0:nr, 2:3], st[0:nr, 3:4], ALU.add), reads=[], writes=[b_st[si]])
                S.op("dve", lambda h: h.reciprocal(st[0:nr, 2:3], st[0:nr, 2:3]), reads=[], writes=[b_st[si]])
                S.op("pool", lambda h: h.tensor_scalar(dg[si][0:nr, 0:nr], identb[0:nr, 0:nr], st[0:nr, 2:3], None, ALU.mult), reads=[b_st[si]], writes=[b_dg[si]])
                return si

            S.dma("sp", ks_cache, ck[:, 8:128, :], writes=[b_out])
            S.dma("sp", vs_cache, cv[:, 8:128, :], writes=[b_out])
            stage = cfg.get("swa_stage", 9)
            for kv in range(cfg.get("swa_nkv", 8)):
                kvw, b_kvw = wload_flat(swa_wk[kv], NCH, 128)
                fns = []
                for kk in range(NCH):
                    for ti, (a, b) in enumerate(TGS):
                        fns.append(lambda h, kk=kk, ti=ti, a=a, b=b, kvw=kvw: h.matmul(pbank[ti][:, 0:b - a], kvw[:, kk, :], xb[:, kk, a:b],
                                                                                     start=(kk == 0), stop=(kk == NCH - 1)))
                S.group("pe", fns, reads=[b_kvw] + b_xb, writes=b_pb[0:3])

                def kside(r_, b_r):
                    S.op("dve", lambda h: h.tensor_copy(kout[:, 0:128], r_[:, T - 128:T]), reads=[b_r], writes=[b_kout])
                    S.op("dve", lambda h: h.tensor_copy(kout[:, 128:160], r_[:, T:NT]), reads=[b_r], writes=[b_kout])
                rope(0, col("swbk", kv), kT[:, 128:128 + NT], [b_kT], 1.0, side=kside)
                S.dma("sp", swk[kv], kout, reads=[b_kout], writes=[b_out])
                if stage < 2:
                    continue
                vv, vb_ = wload_flat(swa_wv[kv], NCH, 64)
                for n in range(9):
                    w = 128 if n < 8 else 32
                    pbk = 6 + (n % 2)
                    fns = [lambda h, kk=kk, n=n, w=w, pbk=pbk, vv=vv: h.matmul(pbank[pbk][0:w, 0:64], xb[:, kk, n * 128:n * 128 + w], vv[:, kk, :],
                                                                             start=(kk == 0), stop=(kk == NCH - 1)) for kk in range(NCH)]
                    S.group("pe", fns, reads=[vb_] + b_xb, writes=[b_pb[pbk]])
                    S.op("dve", lambda h, n=n, w=w, pbk=pbk, kv=kv: h.tensor_tensor(vz[0:w, n + 1, 64:128], pbank[pbk][0:w, 0:64], bvrow[0:w, kv * 64:(kv + 1) * 64], ALU.add),
                         reads=[b_pb[pbk], b_swcs], writes=[b_vz])
                    if n >= 7:
                        S.op("dve", lambda h, n=n, w=w, pbk=pbk, kv=kv: h.tensor_tensor(vout[0:w, n - 7, :], pbank[pbk][0:w, 0:64], bvrow[0:w, kv * 64:(kv + 1) * 64], ALU.add),
                             reads=[b_pb[pbk], b_swcs], writes=[b_vout])
                S.dma("sp", swv[kv], vout[:].rearrange("p n d -> p (n d)"), reads=[b_vout], writes=[b_out])
                S.op("dve", lambda h: h.tensor_copy(xs_[:, 0:128], kT[:, T:T + 128]), reads=[b_kT], writes=[b_xs])
                S.op("dve", lambda h: h.tensor_copy(xs_[:, 128:192], vout[:, 0, :]), reads=[b_vout], writes=[b_xs])
                S.dma("sp", sx_in[kv], xs_, reads=[b_xs], writes=[b_si])
                S.allgather(sx_in[kv], sx_out[kv], reads=[b_si], writes=[b_so])
                S.dma("sp", xr, sx_out[kv][0:128, :], reads=[b_so], writes=[b_xr])
                S.op("dve", lambda h: h.tensor_copy(kT[:, 0:128], xr[:, 0:128]), reads=[b_xr], writes=[b_kh])
                S.op("dve", lambda h: h.tensor_copy(vz[:, 0, 64:128], xr[:, 128:192]), reads=[b_xr], writes=[b_vz])
                if stage < 3:
                    continue
                S.dma("pool", kcT[:].rearrange("p b k -> p (b k)"), kcache[kv], writes=[b_kcT])
                S.dma("pool", vcz[:].rearrange("p b d -> p (b d)"), vcache[kv], writes=[b_vcz])
                qv, qb_ = wload(wqkv[:, kv * 256:(kv + 1) * 256], NCH, 256)
                for qc in range(2):
                    pset = qc * 3
                    fns = []
                    for kk in range(NCH):
                        for ti, (a, b) in enumerate(TGS):
                            fns.append(lambda h, kk=kk, ti=ti, a=a, b=b, qc=qc, qv=qv, pset=pset: h.matmul(pbank[pset + ti][:, 0:b - a], qv[:, kk, qc * 128:(qc + 1) * 128],
                                                                                                           xb[:, kk, a:b], start=(kk == 0), stop=(kk == NCH - 1)))
                    S.group("pe", fns, reads=[qb_] + b_xb, writes=b_pb[pset:pset + 3])
                    rope(pset, col("swbq", kv * 2 + qc), qT[:, qc, :], b_q, 0.125)
                if stage < 4:
                    continue
                for i in range(8):
                    po = pbank[4]

                    def head_body(g, i=i, po=po):
                        hp, qc = g % 2, g // 2
                        rows = slice(hp * 64, hp * 64 + 64)
                        sbk = g % 2
                        ps = pbank[sbk]
                        S.group("pe", [lambda h, rows=rows, qc=qc, i=i, ps=ps: h.matmul(ps[:, 0:256], qT[rows, qc, i * 128:(i + 1) * 128], kT[rows, i * 128:i * 128 + 256],
                                                                                         start=True, stop=True)],
                                reads=[b_q[i], b_kT, b_kh], writes=[b_pb[sbk]])
                        mk = mask0 if i == 0 else mask01
                        si = softmax_rows(128, 256, ps, b_pb[sbk], mk, [b_mask0, b_swcs], col("sink", kv * 4 + g))
                        tb = 2 + (g % 2)
                        fns = [lambda h, j=j, si=si, tb=tb: h.matmul(pbank[tb][:, j * 128:(j + 1) * 128], pm[si][:, j * 128:(j + 1) * 128], dg[si][:, :], start=True, stop=True)
                               for j in range(2)]
                        S.group("pe", fns, reads=[b_pm[si], b_dg[si]], writes=[b_pb[tb]])
                        S.op("act", lambda h, si=si, tb=tb: h.copy(pT[si][:, :], pbank[tb][:, 0:256]), reads=[b_pb[tb]], writes=[b_pT[si]])
                        vs0, vs1 = (64, 192) if hp == 0 else (0, 128)
                        fns = [lambda h, j=j, si=si, qc=qc, i=i, hp=hp, vs0=vs0, vs1=vs1: h.matmul(po[:, qc * 128:(qc + 1) * 128], vz[:, i + j, vs0:vs1], pT[si][:, j * 128:(j + 1) * 128],
                                                                                                     start=(hp == 0 and j == 0), stop=(hp == 1 and j == 1)) for j in range(2)]
                        S.group("pe", fns, reads=[b_vz, b_pT[si]], writes=[b_pb[4]])
                    for g0 in (0, 2):
                        S.replay_interleaved([S.capture(lambda: head_body(g0)), S.capture(lambda: head_body(g0 + 1))])
                    S.op("act", lambda h, i=i: h.copy(qT[:, :, i * 128:(i + 1) * 128], pbank[4][:, 0:256].rearrange("p (c t) -> p c t", c=2)),
                         reads=[b_pb[4]], writes=[b_q[i]])
                if stage < 5:
                    continue
                pos_ = pbank[5]
                def sample_body(qc, bb, hp):
                    g = qc * 2 + hp
                    rows = slice(hp * 64, hp * 64 + 64)
                    vs0, vs1 = (64, 192) if hp == 0 else (0, 128)
                    if True:
                        sbk = hp
                        ps = pbank[sbk]
                        fns = [lambda h, rows=rows, qc=qc, bb=bb, ps=ps: h.matmul(ps[0:8, 0:128], qT[rows, qc, T + bb * 8:T + bb * 8 + 8], kcT[rows, bb, :], start=True, stop=True),
                               lambda h, rows=rows, qc=qc, bb=bb, ps=ps: h.matmul(ps[0:8, 128:160], qT[rows, qc, T + bb * 8:T + bb * 8 + 8], kT[rows, 128 + T:128 + NT], start=True, stop=True)]
                        S.group("pe", fns, reads=[b_q[8], b_kcT, b_kT], writes=[b_pb[sbk]])
                        si = softmax_rows(8, 160, ps, b_pb[sbk], masks[0:8, bb, :], [b_swcs], col("sink", kv * 4 + g))
                        tb = 2 + hp
                        fns = [lambda h, si=si, tb=tb: h.matmul(pbank[tb][:, 0:8], pm[si][0:8, 0:128], dg[si][0:8, 0:8], start=True, stop=True),
                               lambda h, si=si, tb=tb: h.matmul(pbank[tb][0:32, 8:16], pm[si][0:8, 128:160], dg[si][0:8, 0:8], start=True, stop=True)]
                        S.group("pe", fns, reads=[b_pm[si], b_dg[si]], writes=[b_pb[tb]])
                        S.op("act", lambda h, si=si, tb=tb: h.copy(pT[si][:, 0:8], pbank[tb][:, 0:8]), reads=[b_pb[tb]], writes=[b_pT[si]])
                        S.op("act", lambda h, si=si, tb=tb: h.copy(pT[si][0:32, 8:16], pbank[tb][0:32, 8:16]), reads=[b_pb[tb]], writes=[b_pT[si]])
                        oc = qc * 32 + bb * 8
                        fns = [lambda h, si=si, bb=bb, oc=oc, hp=hp, vs0=vs0, vs1=vs1: h.matmul(pos_[:, oc:oc + 8], vcz[:, bb, vs0:vs1], pT[si][:, 0:8], start=(hp == 0), stop=False),
                               lambda h, si=si, oc=oc, hp=hp, vs0=vs0, vs1=vs1: h.matmul(pos_[:, oc:oc + 8], vz[0:32, 9, vs0:vs1], pT[si][0:32, 8:16], start=False, stop=(hp == 1))]
                        S.group("pe", fns, reads=[b_vcz, b_vz, b_pT[si]], writes=[b_pb[5]])
                for qc in range(2):
                    for bb in range(4):
                        S.replay_interleaved([S.capture(lambda: sample_body(qc, bb, 0)), S.capture(lambda: sample_body(qc, bb, 1))])
                S.op("act", lambda h: h.copy(qT[:, :, T:NT], pbank[5][:, 0:64].rearrange("p (c t) -> p c t", c=2)), reads=[b_pb[5]], writes=[b_q[8]])
                if stage < 6:
                    continue
                for hf in range(2):
                    wv, wb = wload(wout[kv * 256:(kv + 1) * 256, hf * 1024:(hf + 1) * 1024], 2, 1024)
                    for mo in range(0, 1024, 128):
                        nn = (hf * 1024 + mo) // 128
                        pset = (nn % 2) * 3
                        fns = []
                        for kk in range(2):
                            for ti, (a, b) in enumerate(TGS):
                                fns.append(lambda h, kk=kk, ti=ti, a=a, b=b, mo=mo, wv=wv, pset=pset: h.matmul(pbank[pset + ti][:, 0:b - a], wv[:, kk, mo:mo + 128],
                                                                                                               qT[:, kk, a:b], start=(kk == 0), stop=(kk == 1)))
                        S.group("pe", fns, reads=[wb] + b_q, writes=b_pb[pset:pset + 3])
                        for ti, (a, b) in enumerate(TGS):
                            if kv == 0:
                                S.op("dve", lambda h, nn=nn, ti=ti, a=a, b=b, pset=pset: h.scalar_tensor_tensor(xT[:, nn, a:b], pbank[pset + ti][:, 0:b - a], col("swbo", nn),
                                                                                                                xT[:, nn, a:b], ALU.add, ALU.add),
                                     reads=[b_pb[pset + ti], b_cs], writes=[b_x[nn]])
                            else:
                                S.op("dve", lambda h, nn=nn, ti=ti, a=a, b=b, pset=pset: h.tensor_tensor(xT[:, nn, a:b], xT[:, nn, a:b], pbank[pset + ti][:, 0:b - a], ALU.add),
                                     reads=[b_pb[pset + ti]], writes=[b_x[nn]])

        nlayers = cfg.get("layers", DEPTH)
        for i in range(nlayers):
            scale_residual()
            kind = cfg.get("kinds", (0, 1, 2, 0))[i]
            if cfg.get("mix", True):
                if kind == 0:
                    gla(i // 3)
                elif kind == 2:
                    sgmix()
                else:
                    swa()
            layer_norm("lmg%d" % i, "lmb%d" % i)
            if need["ffn"]:
                halo_exchange(i)
                scale_residual()
                ffn(i)
                layer_norm("lfg%d" % i, "lfb%d" % i)
        scrm.reset()
        for c in range(NCH):
            S.dma("sp", yT[:, c, :], xT[:, c, :], reads=[b_x[c]], writes=[b_out])
        S.barrier()
        S.emit(block)
    k.ninst = dict(S.n_inst)
    return nc, k


def make_consts(core, inp):
    C = const_layout()
    a = np.zeros((128, C.n), np.float32)

    def put(name, arr):
        o, n = C.off[name]
        a[:, o:o + n] = arr
    put("ident", np.eye(128, dtype=np.float32))
    put("onesd", np.full((128, 128), 1.0 / D, np.float32))
    put("flag", np.full((128, 1), float(core % 2), np.float32))
    put("alpha", np.full((128, 1), ALPHA, np.float32))
    put("eps", np.full((128, 1), EPS, np.float32))
    put("m16", np.full((128, 1), -1.0 / 16.0, np.float32))
    put("qs", np.full((128, 1), 256.0 ** -0.5, np.float32))
    put("i512", np.full((128, 1), 1.0 / 512.0, np.float32))
    ii = np.arange(128)
    put("tmask", ((ii[:, None] // 64 == ii[None, :] // 64) & (ii[:, None] <= ii[None, :])).astype(np.float32))
    sm = np.zeros((128, 32), np.float32)
    jj = np.arange(32)
    sm[:32] = ((jj[:, None] // 8 == jj[None, :] // 8) & (jj[:, None] <= jj[None, :])).astype(np.float32)
    put("smask", sm)
    rm = np.zeros((128, 4), np.float32)
    rm[:32] = (jj[:, None] // 8 == np.arange(4)[None, :]).astype(np.float32)
    put("rowm", rm)
    rs = np.ones((128, NT), np.float32)
    rs[:, 0:T:64] = 0.0
    rs[:, T:NT:8] = 0.0
    put("reset", rs)
    put("ones", np.ones((128, 128), np.float32))
    put("tri", (ii[:, None] <= ii[None, :]).astype(np.float32))
    put("sgbi", fm(inp["sg_b_in"][0]))
    put("sglg", fm(inp["sg_ln_g"][0]))
    put("sglb", fm(inp["sg_ln_b"][0]))
    put("sgbo", fm(inp["sg_b_out"][0]))
    bq = inp["swa_b_qkv"][0]
    put("swbq", fm(bq[0:2048]))
    put("swbk", np.concatenate([bq[2048:2560].reshape(8, 64).T] * 2, 0))
    put("sink", np.broadcast_to(inp["swa_sinks"][0].reshape(1, 32), (128, 32)))
    put("sinkp", np.broadcast_to(inp["swa_sinks"][0].reshape(8, 2, 2).transpose(0, 2, 1).reshape(1, 32), (128, 32)))
    put("swbo", fm(inp["swa_b_out"][0]))
    for j in range(2):
        put("bg%d" % j, fm(inp["gla_b_g"][j]))
        put("gnw%d" % j, fm(inp["gla_norm_w"][j]))
    for i in range(DEPTH):
        put("lmg%d" % i, fm(inp["ln_mix_g"][i]))
        put("lmb%d" % i, fm(inp["ln_mix_b"][i]))
        put("lfg%d" % i, fm(inp["ln_ffn_g"][i]))
        put("lfb%d" % i, fm(inp["ln_ffn_b"][i]))
        for t in range(3):
            put("cw%d_%d" % (t, i), fm(inp["ffn_conv_w"][i, t]))
        put("cb_%d" % i, fm(inp["ffn_conv_b"][i]))
    return a


def make_swc(core, inp):
    a = np.zeros((128, SWC_N), np.float32)
    half = core % 2
    pos = np.concatenate([half * T + np.arange(T), 16384 + (np.arange(NS) % 8)]).astype(np.float32)
    inv = (np.float32(500000.0) ** (-np.arange(8, dtype=np.float32) / np.float32(8))).astype(np.float32)
    ang = (pos[:, None] * inv[None, :]).astype(np.float32)
    cos = np.ones((128, NT), np.float32)
    sin = np.zeros((128, NT), np.float32)
    for p in range(128):
        d = p % 64
        if d < 16:
            cos[p] = np.cos(ang[:, d % 8])
            sin[p] = np.sin(ang[:, d % 8])
    a[:, 0:NT] = cos
    a[:, NT:2 * NT] = sin
    o = 2 * NT
    PT = np.zeros((128, 128), np.float32)
    for m in range(128):
        d = m % 64
        if d < 8:
            PT[m + 8, m] = -1.0
        elif d < 16:
            PT[m - 8, m] = 1.0
    a[:, o:o + 128] = PT
    o += 128
    qa = np.arange(128)[:, None]
    kj = np.arange(256)[None, :]
    m01 = ((kj >= qa + 1) & (kj <= qa + 128)).astype(np.float32)
    a[:, o:o + 256] = m01
    o += 256
    ms = np.zeros((128, 4, 160), np.float32)
    key = np.arange(32)[None, :]
    qq = np.arange(8)[:, None]
    for bb in range(4):
        ms[0:8, bb, 0:128] = m01[0:8, 0:128]
        ms[0:8, bb, 128:160] = ((key // 8 == bb) & (key % 8 <= qq)).astype(np.float32)
    a[:, o:o + 640] = ms.reshape(128, 640)
    o += 640
    a[:, o:o + 512] = np.broadcast_to(inp["swa_b_qkv"][0][2560:3072].reshape(1, 512), (128, 512))
    return a


def make_sgc(inp):
    ws, bs = inp["sg_w_s"][0], inp["sg_b_s"][0]
    a = np.zeros((128, 1280), np.float32)
    a[:, 0:512] = ws.transpose(2, 0, 1).reshape(128, 512)
    i32 = np.arange(32)
    corner = ws[:, :8, :8]
    tiled = corner[:, i32[None, :] % 8, i32[:, None] % 8]
    a[0:32, 512:640] = tiled.transpose(1, 0, 2).reshape(32, 128)
    a[:, 640:1152] = np.broadcast_to(bs.reshape(1, 512), (128, 512))
    a[:, 1152:1280] = np.broadcast_to(bs[:, i32 % 8].reshape(1, 128), (128, 128))
    return a


_CACHE = {}


def kernel(**inp):
    cfg = {"layers": int(os.environ.get("K_LAYERS", DEPTH)), "mix": os.environ.get("K_MIX", "1") == "1",
           "kinds": tuple(int(ch) for ch in os.environ.get("K_KINDS", "0120")),
           "noffn": os.environ.get("K_NOFFN", "0") == "1", "swa_stage": int(os.environ.get("K_SWA_STAGE", "9")),
           "swa_nkv": int(os.environ.get("K_SWA_NKV", "8")),
           "ch": int(os.environ.get("K_CH", "99")), "chs": int(os.environ.get("K_CHS", "99"))}
    inp = {k_: np.asarray(v) for k_, v in inp.items()}
    key = tuple(sorted(cfg.items()))
    if key not in _CACHE:
        _CACHE[key] = build_nc(cfg)
    nc, kk = _CACHE[key]
    xp, xs = inp["x_prompt"], inp["x_sample"]
    wq = inp["swa_w_qkv"][0]
    wk_ = wq[:, 2048:2560].reshape(NCH, 128, 8, 64).transpose(2, 1, 0, 3)
    SWK = np.ascontiguousarray(np.concatenate([wk_, wk_], 3)).reshape(8, 128, NCH * 128)
    SWV = np.ascontiguousarray(wq[:, 2560:3072].reshape(NCH, 128, 8, 64).transpose(2, 1, 0, 3)).reshape(8, 128, NCH * 64)
    GWG = np.ascontiguousarray(inp["gla_w_in"][:, :, 6144:6160].reshape(2, NCH, 128, 16).transpose(0, 2, 1, 3)).reshape(2, 128, NCH * 16)
    in_maps = []
    for c in range(8):
        b, half = c // 2, c % 2
        xtok = np.concatenate([xp[b, half * T:(half + 1) * T], xs[4 * c:4 * c + 4].reshape(NS, D)], 0)
        xin = np.ascontiguousarray(xtok.T.reshape(NCH, 128, NT).transpose(1, 0, 2))
        fc = inp["state_ffn_conv"][:, 4 * c:4 * c + 4]
        fc = fc.reshape(DEPTH, 8, FCH, 128).transpose(3, 0, 2, 1)
        in_maps.append({
            "xin": xin,
            "cst": make_consts(c, inp),
            "fconv": np.ascontiguousarray(fc).reshape(128, -1),
            "ffn_w_in": inp["ffn_w_in"],
            "ffn_w_out": inp["ffn_w_out"],
            "swa_w_qkv": inp["swa_w_qkv"], "swa_w_out": inp["swa_w_out"], "swc": make_swc(c, inp),
            "swa_wk": SWK, "swa_wv": SWV, "gla_wg": GWG,
            "kcache": np.ascontiguousarray(np.concatenate([inp["cache_swa_k"][0, 4 * c:4 * c + 4].transpose(2, 3, 0, 1)] * 2, 1)).reshape(8, 128, 512),
            "vcache": np.ascontiguousarray(np.pad(inp["cache_swa_v"][0, 4 * c:4 * c + 4].transpose(2, 1, 0, 3), ((0, 0), (0, 0), (0, 0), (64, 64)))).reshape(8, 128, 768),
            "ck": np.ascontiguousarray(inp["cache_swa_k"][0, 4 * c:4 * c + 4]).reshape(4, 128, 512),
            "cv": np.ascontiguousarray(inp["cache_swa_v"][0, 4 * c:4 * c + 4]).reshape(4, 128, 512),
            "sg_w_in": inp["sg_w_in"], "sg_w_out": inp["sg_w_out"], "sgc": make_sgc(inp),
            "gla_w_in": inp["gla_w_in"], "gla_w_g2": inp["gla_w_g2"], "gla_w_out": inp["gla_w_out"],
            "gla_s0": np.ascontiguousarray(inp["state_gla"][:, 4 * c:4 * c + 4]),
        })
    dummy = np.zeros((1, 128, 128), np.float32)
    fam = {"ffn_w_in": "ffn", "ffn_w_out": "ffn", "swa_w_qkv": "swa", "swa_w_out": "swa", "sg_w_in": "sg", "sg_w_out": "sg",
           "gla_w_in": "gla", "gla_w_out": "gla", "gla_s0": "gla"}
    for m_ in in_maps:
        for nm, f_ in fam.items():
            if not kk.need[f_]:
                m_[nm] = dummy
    res = run_bass_kernel_spmd(nc, in_maps, core_ids=list(range(8)))
    R = res.results
    y_p = np.zeros((4, 2048, D), np.float32)
    y_s = np.zeros((32, 8, D), np.float32)
    conv_p = np.zeros((DEPTH, 4, 2, DFF), np.float32)
    conv_s = np.zeros((DEPTH, 32, 2, DFF), np.float32)
    for c in range(8):
        b, half = c // 2, c % 2
        y = R[c]["yT"].transpose(1, 0, 2).reshape(D, NT).T
        y_p[b, half * T:(half + 1) * T] = y[:T]
        y_s[4 * c:4 * c + 4] = y[T:].reshape(4, 8, D)
        cv = R[c]["convT"].reshape(128, DEPTH, FCH, 10).transpose(1, 3, 2, 0).reshape(DEPTH, 10, DFF)
        if half == 1:
            conv_p[:, b] = cv[:, 0:2]
        conv_s[:, 4 * c:4 * c + 4] = cv[:, 2:10].reshape(DEPTH, 4, 2, DFF)
    z = np.zeros
    gsp = np.zeros((2, 4, 4, 256, 512), np.float32)
    gss = np.zeros((2, 32, 4, 256, 512), np.float32)
    for c in range(8):
        if c % 2 == 1:
            gsp[:, c // 2] = R[c]["gla_sp"]
        gss[:, 4 * c:4 * c + 4] = R[c]["gla_ss"]
    sgv_s = np.zeros((1, 32, 8, D), np.float32)
    for c in range(8):
        sv = R[c]["sgv"].reshape(128, NCH, 32).transpose(2, 1, 0).reshape(32, D)
        sgv_s[0, 4 * c:4 * c + 4] = sv.reshape(4, 8, D)
    swkp = np.zeros((1, 4, 128, 8, 64), np.float32)
    swvp = np.zeros((1, 4, 128, 8, 64), np.float32)
    swks = np.zeros((1, 32, 128, 8, 64), np.float32)
    swvs = np.zeros((1, 32, 128, 8, 64), np.float32)
    for c in range(8):
        k_ = R[c]["swk"]
        v_ = R[c]["swv"].reshape(8, 128, 2, 64)
        if c % 2 == 1:
            swkp[0, c // 2] = k_[:, 0:64, 0:128].transpose(2, 0, 1)
            swvp[0, c // 2] = v_[:, :, 0, :].transpose(1, 0, 2)
        for bb in range(4):
            swks[0, 4 * c + bb, 0:120] = R[c]["ks_cache"][bb].reshape(120, 8, 64)
            swvs[0, 4 * c + bb, 0:120] = R[c]["vs_cache"][bb].reshape(120, 8, 64)
            swks[0, 4 * c + bb, 120:128] = k_[:, 0:64, 128 + bb * 8:128 + bb * 8 + 8].transpose(2, 0, 1)
            swvs[0, 4 * c + bb, 120:128] = v_[:, bb * 8:bb * 8 + 8, 1, :].transpose(1, 0, 2)
    return (y_p, y_s, gsp, gss, swkp, swvp, swks, swvs,
            sgv_s, conv_p, conv_s)
```

```python
import os
from contextlib import ExitStack
import numpy as np
import concourse.bass as bass
import concourse.mybir as mybir
from concourse.bass_utils import run_bass_kernel_spmd

F32 = mybir.dt.float32
BF16 = mybir.dt.bfloat16
AF = mybir.ActivationFunctionType
ALU = mybir.AluOpType
AX = mybir.AxisListType

D = 2048
NCH = 16
T = 1024
NS = 32
NT = T + NS
NX = NT + 2
DFF = 5632
FCH = 44
DEPTH = 4
ALPHA = (2 * DEPTH) ** 0.25
EPS = 1e-5
TGS = [(0, 512), (512, 1024), (1024, NT)]
WSLOT = 4096
SWC_N = 2 * NT + 128 + 256 + 640 + 512
NWS = 4
PAIRS = [[2 * i, 2 * i + 1] for i in range(4)]

ENGS = ("pe", "act", "dve", "pool", "sp")
NDMA = 8


class Buf:
    __slots__ = ("name", "w", "r")

    def __init__(self, name):
        self.name = name
        self.w = None
        self.r = []


class Sched:
    def __init__(self, nc):
        self.nc = nc
        self.ops = {e: [] for e in ENGS}
        self.sems = {}
        self.cnt = {}
        self.seen = {e: {} for e in ENGS}
        self.dma_i = {e: 0 for e in ENGS}
        self.n_inst = {e: 0 for e in ENGS}

    def setup(self, stack):
        for e in ("pe", "act", "dve", "pool"):
            self.sems[e] = stack.enter_context(self.nc.semaphore("s_" + e))
            self.cnt[e] = 0
        for q in ("sp", "pool"):
            for i in range(NDMA):
                k = "d_%s%d" % (q, i)
                self.sems[k] = stack.enter_context(self.nc.semaphore(k))
                self.cnt[k] = 0
        self.sems["cc"] = stack.enter_context(self.nc.semaphore("s_cc"))
        self.cnt["cc"] = 0

    def _deps(self, reads, writes):
        deps = {}
        for b in reads:
            t = b.w
            if t is not None and deps.get(t[0], 0) < t[1]:
                deps[t[0]] = t[1]
        for b in writes:
            t = b.w
            if t is not None and deps.get(t[0], 0) < t[1]:
                deps[t[0]] = t[1]
            for t in b.r:
                if deps.get(t[0], 0) < t[1]:
                    deps[t[0]] = t[1]
        return deps

    def _waits(self, eng, deps):
        out = []
        seen = self.seen[eng]
        for k, v in deps.items():
            if seen.get(k, 0) < v:
                seen[k] = v
                out.append((k, v))
        return out

    def _mark(self, tok, reads, writes):
        for b in reads:
            b.r.append(tok)
            if len(b.r) > 64:
                m = {}
                for k, v in b.r:
                    if m.get(k, 0) < v:
                        m[k] = v
                b.r = list(m.items())
        for b in writes:
            b.w = tok
            b.r = []

    def group(self, eng, fns, reads=(), writes=()):
        if getattr(self, "_cap", None) is not None:
            self._cap.append((eng, fns, tuple(reads), tuple(writes)))
            return None
        deps = self._deps(reads, writes)
        waits = self._waits(eng, deps)
        self.cnt[eng] += 1
        tok = (eng, self.cnt[eng])
        self._mark(tok, reads, writes)
        sems = self.sems
        semh = sems[eng]
        n = len(fns)

        def run(h):
            for k, v in waits:
                h.wait_ge(sems[k], v)
            for i, f in enumerate(fns):
                ins = f(h)
                if i == n - 1:
                    ins.then_inc(semh, 1)
        self.ops[eng].append(run)
        self.n_inst[eng] += n
        return tok

    def op(self, eng, fn, reads=(), writes=()):
        return self.group(eng, [fn], reads, writes)

    def capture(self, fn):
        self._cap = []
        fn()
        out, self._cap = self._cap, None
        return out

    def replay_interleaved(self, chains):
        n = max(len(c) for c in chains)
        for k in range(n):
            for c in chains:
                if k < len(c):
                    self.group(*c[k])

    def dma(self, q, out_ap, in_ap, reads=(), writes=(), **kw):
        i = self.dma_i[q]
        self.dma_i[q] += 1
        k = "d_%s%d" % (q, i % NDMA)
        deps = self._deps(reads, writes)
        prev = self.cnt[k]
        if prev and deps.get(k, 0) < prev:
            deps[k] = prev
        waits = self._waits(q, deps)
        self.cnt[k] += 16
        tok = (k, self.cnt[k])
        self._mark(tok, reads, writes)
        sems = self.sems
        semh = sems[k]

        def run(h):
            for kk, v in waits:
                h.wait_ge(sems[kk], v)
            h.dma_start(out=out_ap, in_=in_ap, **kw).then_inc(semh, 16)
        self.ops[q].append(run)
        return tok

    def allgather(self, cin, cout, reads=(), writes=()):
        deps = self._deps(reads, writes)
        prev = self.cnt["cc"]
        if prev and deps.get("cc", 0) < prev:
            deps["cc"] = prev
        waits = self._waits("pool", deps)
        self.cnt["cc"] += 1
        tok = ("cc", self.cnt["cc"])
        self._mark(tok, reads, writes)
        sems = self.sems

        def run(h):
            for kk, v in waits:
                h.wait_ge(sems[kk], v)
            h.collective_compute("AllGather", ALU.bypass, replica_groups=PAIRS,
                                 ins=[cin], outs=[cout]).then_inc(sems["cc"], 1)
        self.ops["pool"].append(run)
        return tok

    def barrier(self, engs=ENGS):
        for e in engs:
            waits = [(k, v) for k, v in self.cnt.items() if v > 0 and self.seen[e].get(k, 0) < v]
            for k, v in waits:
                self.seen[e][k] = v
            sems = self.sems

            def run(h, waits=waits):
                for kk, v in waits:
                    h.wait_ge(sems[kk], v)
            self.ops[e].append(run)

    def emit(self, block):
        ops = self.ops

        @block.tensor
        def _(h):
            for f in ops["pe"]:
                f(h)

        @block.scalar
        def _(h):
            for f in ops["act"]:
                f(h)

        @block.vector
        def _(h):
            for f in ops["dve"]:
                f(h)

        @block.gpsimd
        def _(h):
            for f in ops["pool"]:
                f(h)

        @block.sync
        def _(h):
            for f in ops["sp"]:
                f(h)


class Cols:
    def __init__(self):
        self.off = {}
        self.n = 0

    def add(self, name, n):
        self.off[name] = (self.n, n)
        self.n += n


def const_layout():
    C = Cols()
    C.add("ident", 128)
    C.add("onesd", 128)
    C.add("flag", 1)
    C.add("alpha", 1)
    C.add("eps", 1)
    C.add("m16", 1)
    C.add("qs", 1)
    C.add("i512", 1)
    C.add("tmask", 128)
    C.add("smask", 32)
    C.add("rowm", 4)
    C.add("reset", NT)
    C.add("ones", 128)
    C.add("tri", 128)
    C.add("sgbi", 32)
    C.add("sglg", 16)
    C.add("sglb", 16)
    C.add("sgbo", 16)
    C.add("swbq", 16)
    C.add("swbk", 8)
    C.add("sink", 32)
    C.add("sinkp", 32)
    C.add("swbo", 16)
    for j in range(2):
        C.add("bg%d" % j, 8)
        C.add("gnw%d" % j, 4)
    for i in range(DEPTH):
        for nm in ("lmg", "lmb", "lfg", "lfb"):
            C.add("%s%d" % (nm, i), NCH)
        for nm in ("cw0_", "cw1_", "cw2_", "cb_"):
            C.add("%s%d" % (nm, i), FCH)
    return C


def fm(vec):
    v = np.asarray(vec, np.float32)
    return np.ascontiguousarray(v.reshape(-1, 128).T)


class K:
    pass


def build_nc(cfg):
    nc = bass.Bass("TRN2", target_bir_lowering=False)
    C = const_layout()
    k = K()
    k.nc, k.C = nc, C

    def din(name, shape):
        return nc.dram_tensor(name, list(shape), F32, kind="ExternalInput").ap()

    def dout(name, shape):
        return nc.dram_tensor(name, list(shape), F32, kind="ExternalOutput").ap()

    def dint(name, shape):
        return nc.dram_tensor(name, list(shape), F32, kind="Internal").ap()

    xin = din("xin", [128, NCH, NT])
    cst = din("cst", [128, C.n])
    fconv = din("fconv", [128, DEPTH * FCH * 8])
    nl_ = cfg.get("layers", DEPTH)
    kinds_ = cfg.get("kinds", (0, 1, 2, 0))[:nl_] if cfg.get("mix", True) else ()
    need = {"ffn": not cfg.get("noffn", False), "gla": 0 in kinds_, "swa": 1 in kinds_, "sg": 2 in kinds_}
    k.need = need

    def dinw(name, shape, fam):
        return din(name, shape if need[fam] else [1, 128, 128])
    ffn_w_in = dinw("ffn_w_in", [DEPTH, D, 2 * DFF], "ffn")
    ffn_w_out = dinw("ffn_w_out", [DEPTH, DFF, D], "ffn")
    swa_w_qkv = dinw("swa_w_qkv", [1, D, 3072], "swa")
    swa_w_out = dinw("swa_w_out", [1, D, D], "swa")
    swa_wk = din("swa_wk", [8, 128, NCH * 128])
    swa_wv = din("swa_wv", [8, 128, NCH * 64])
    gla_wg = din("gla_wg", [2, 128, NCH * 16])
    swc = din("swc", [128, SWC_N])
    kcache = din("kcache", [8, 128, 512])
    vcache = din("vcache", [8, 128, 768])
    ck = din("ck", [4, 128, 512])
    cv = din("cv", [4, 128, 512])
    swk = dout("swk", [8, 128, 160])
    swv = dout("swv", [8, 128, 128])
    ks_cache = dout("ks_cache", [4, 120, 512])
    vs_cache = dout("vs_cache", [4, 120, 512])
    sx_in = [dint("sx_in%d" % i, [128, 192]) for i in range(8)]
    sx_out = [dint("sx_out%d" % i, [256, 192]) for i in range(8)]
    sg_w_in = dinw("sg_w_in", [1, D, 2 * D], "sg")
    sg_w_out = dinw("sg_w_out", [1, D, D], "sg")
    sgc = din("sgc", [128, 1280])
    sgv = dout("sgv", [128, NCH * 32])
    gla_w_in = dinw("gla_w_in", [2, D, 6160], "gla")
    gla_w_g2 = din("gla_w_g2", [2, 16, 1024])
    gla_w_out = dinw("gla_w_out", [2, D, D], "gla")
    gla_s0 = din("gla_s0", [2, 4, 4, 256, 512] if need["gla"] else [1, 128, 128])
    gla_sp = dout("gla_sp", [2, 4, 256, 512])
    gla_ss = dout("gla_ss", [2, 4, 4, 256, 512])
    gs_in = [dint("gs_in%d" % i, [128, 1024]) for i in range(8)]
    gs_out = [dint("gs_out%d" % i, [256, 1024]) for i in range(8)]
    yT = dout("yT", [128, NCH, NT])
    convT = dout("convT", [128, DEPTH * FCH * 10])
    hx_in = [dint("hx_in%d" % i, [128, 32]) for i in range(DEPTH)]
    hx_out = [dint("hx_out%d" % i, [256, 32]) for i in range(DEPTH)]

    S = Sched(nc)
    k.S = S
    with ExitStack() as st:
        S.setup(st)

        def sb(name, shape, dt):
            return st.enter_context(nc.sbuf_tensor(name, list(shape), dt))

        xT = sb("xT", [128, NCH, NT], F32)
        xb = sb("xb", [128, NCH, NX], BF16)
        cs = sb("cs", [128, C.n], F32)
        fcs = sb("fcs", [128, DEPTH * FCH * 8], F32)
        wsl = [sb("wsl%d" % i, [128, WSLOT], BF16) for i in range(NWS)]
        SCRB = cfg.get("scr_bytes", 59 * 1024)
        scr = sb("scr", [128, SCRB // 4], F32)
        pbank = [st.enter_context(nc.psum_tensor("pb%d" % i, [128, 512], F32)) for i in range(8)]
        block = st.enter_context(nc.Block())

        b_x = [Buf("x%d" % c) for c in range(NCH)]
        b_xb = [Buf("xb%d" % c) for c in range(NCH)]
        b_xh = Buf("xhalo")
        b_cs = Buf("cs")
        b_fcs = Buf("fcs")
        b_ws = [Buf("ws%d" % i) for i in range(NWS)]
        b_pb = [Buf("pb%d" % i) for i in range(8)]
        b_out = Buf("out")

        def col(name, j=0, n=1):
            o, _ = C.off[name]
            return cs[:, o + j:o + j + n]

        class Scr:
            def __init__(self):
                self.p = 0

            def reset(self):
                S.barrier()
                self.p = 0

            def f32(self, n):
                a = scr[:, self.p:self.p + n]
                self.p += n
                assert self.p * 4 <= SCRB, ("scratch overflow", self.p * 4)
                return a

            def bf16(self, n):
                w = (n + 1) // 2
                a = scr[:, self.p:self.p + w].bitcast(BF16)
                self.p += w
                assert self.p * 4 <= SCRB, ("scratch overflow", self.p * 4)
                return a

        scrm = Scr()

        wstate = {"i": 0}

        def wload(src_ap, kc, ncols):
            i = wstate["i"] % NWS
            wstate["i"] += 1
            assert kc * ncols <= WSLOT
            view = wsl[i][:, 0:kc * ncols].rearrange("p (k m) -> p k m", k=kc)
            S.dma("pool", view, src_ap.rearrange("(k p) m -> p k m", p=128), writes=[b_ws[i]])
            return view, b_ws[i]

        def wload_flat(src_ap, kc, ncols):
            i = wstate["i"] % NWS
            wstate["i"] += 1
            flat = wsl[i][:, 0:kc * ncols]
            S.dma("pool", flat, src_ap, writes=[b_ws[i]])
            return flat.rearrange("p (k m) -> p k m", k=kc), b_ws[i]

        class WStream:
            def __init__(self, items, depth=NWS - 1):
                self.items = items
                self.loaded = []
                self.depth = depth

            def get(self, i):
                while len(self.loaded) < min(len(self.items), i + self.depth):
                    self.loaded.append(wload(*self.items[len(self.loaded)]))
                return self.loaded[i]

        S.dma("sp", cs[:], cst, writes=[b_cs])
        S.dma("sp", fcs[:], fconv, writes=[b_fcs])
        for c in range(NCH):
            S.dma("sp", xT[:, c, :], xin[:, c, :], writes=[b_x[c]])
        for c in range(NCH):
            eng = "act" if c % 2 else "dve"
            if eng == "act":
                S.op("act", lambda h, c=c: h.copy(xb[:, c, 0:NT], xT[:, c, :]), reads=[b_x[c]], writes=[b_xb[c]])
            else:
                S.op("dve", lambda h, c=c: h.tensor_copy(xb[:, c, 0:NT], xT[:, c, :]), reads=[b_x[c]], writes=[b_xb[c]])

        ident = col("ident", 0, 128)
        identb = sb("identb", [128, 128], BF16)
        S.op("dve", lambda h: h.tensor_copy(identb[:], ident), reads=[b_cs], writes=[Buf("identb")])
        onesd = col("onesd", 0, 128)
        flag = col("flag")

        def scale_residual():
            for c in range(NCH):
                S.op("act", lambda h, c=c: h.activation(xT[:, c, :], xT[:, c, :], AF.Identity, bias=0.0, scale=ALPHA), reads=[], writes=[b_x[c]])

        def layer_norm(gname, bname):
            scrm.reset()
            s1 = scrm.f32(NT)
            s2 = scrm.f32(NT)
            sq = [scrm.f32(NT), scrm.f32(NT)]
            mean = scrm.f32(NT)
            rstd = scrm.f32(NT)
            tmp = [scrm.f32(NT), scrm.f32(NT)]
            b_s1, b_s2, b_mean, b_rstd = Buf("s1"), Buf("s2"), Buf("mean"), Buf("rstd")
            b_sq = [Buf("sq0"), Buf("sq1")]
            b_tmp = [Buf("t0"), Buf("t1")]
            for c in range(NCH):
                if c == 0:
                    S.op("dve", lambda h: h.tensor_copy(s1, xT[:, 0, :]), reads=[b_x[0]], writes=[b_s1])
                else:
                    S.op("dve", lambda h, c=c: h.tensor_tensor(s1, s1, xT[:, c, :], ALU.add), reads=[b_x[c]], writes=[b_s1])
            for c in range(NCH):
                j = c % 2
                S.op("act", lambda h, c=c, j=j: h.activation(sq[j], xT[:, c, :], AF.Square), reads=[b_x[c]], writes=[b_sq[j]])
                if c == 0:
                    S.op("pool", lambda h, j=j: h.tensor_copy(s2, sq[j]), reads=[b_sq[j]], writes=[b_s2])
                else:
                    S.op("pool", lambda h, j=j: h.tensor_tensor(s2, s2, sq[j], ALU.add), reads=[b_sq[j]], writes=[b_s2])
            fns = []
            for ti, (a, b) in enumerate(TGS):
                fns.append(lambda h, ti=ti, a=a, b=b: h.matmul(pbank[ti][:, 0:b - a], onesd, s1[:, a:b], start=True, stop=True))
            S.group("pe", fns, reads=[b_cs, b_s1], writes=b_pb[0:3])
            fns = []
            for ti, (a, b) in enumerate(TGS):
                fns.append(lambda h, ti=ti, a=a, b=b: h.matmul(pbank[3 + ti][:, 0:b - a], onesd, s2[:, a:b], start=True, stop=True))
            S.group("pe", fns, reads=[b_cs, b_s2], writes=b_pb[3:6])
            for ti, (a, b) in enumerate(TGS):
                S.op("act", lambda h, ti=ti, a=a, b=b: h.copy(mean[:, a:b], pbank[ti][:, 0:b - a]), reads=[b_pb[ti]], writes=[b_mean])
            S.op("dve", lambda h: h.tensor_tensor(rstd, mean, mean, ALU.mult), reads=[b_mean], writes=[b_rstd])
            for ti, (a, b) in enumerate(TGS):
                S.op("dve", lambda h, ti=ti, a=a, b=b: h.tensor_tensor(rstd[:, a:b], pbank[3 + ti][:, 0:b - a], rstd[:, a:b], ALU.subtract),
                     reads=[b_pb[3 + ti]], writes=[b_rstd])
            S.op("act", lambda h: h.activation(rstd, rstd, AF.Sqrt, bias=col("eps"), scale=1.0), reads=[b_cs], writes=[b_rstd])
            S.op("dve", lambda h: h.reciprocal(rstd, rstd), reads=[], writes=[b_rstd])
            for c in range(NCH):
                j = c % 2
                S.op("dve", lambda h, c=c, j=j: h.tensor_tensor(tmp[j], xT[:, c, :], mean, ALU.subtract),
                     reads=[b_x[c], b_mean], writes=[b_tmp[j]])
                S.op("pool", lambda h, j=j: h.tensor_tensor(tmp[j], tmp[j], rstd, ALU.mult), reads=[b_rstd], writes=[b_tmp[j]])
                S.op("act", lambda h, c=c, j=j: h.activation(xT[:, c, :], tmp[j], AF.Identity, bias=col(bname, c), scale=col(gname, c)),
                     reads=[b_tmp[j], b_cs], writes=[b_x[c]])
                S.op("act", lambda h, c=c, j=j: h.activation(xb[:, c, 0:NT], tmp[j], AF.Identity, bias=col(bname, c), scale=col(gname, c)),
                     reads=[b_tmp[j], b_cs], writes=[b_xb[c]])

        def halo_exchange(i):
            scrm.reset()
            hs = scrm.f32(32)
            hr = scrm.f32(32)
            b_hs, b_hr, b_ci, b_co = Buf("hs"), Buf("hr"), Buf("ci"), Buf("co")
            S.op("dve", lambda h: h.tensor_copy(hs.rearrange("p (c t) -> p c t", t=2), xT[:, :, T - 2:T]), reads=b_x, writes=[b_hs])
            S.dma("sp", hx_in[i], hs, reads=[b_hs], writes=[b_ci])
            S.allgather(hx_in[i], hx_out[i], reads=[b_ci], writes=[b_co])
            S.dma("sp", hr, hx_out[i][0:128, :], reads=[b_co], writes=[b_hr])
            S.op("dve", lambda h: h.tensor_scalar(xb[:, :, NT:NX], hr.rearrange("p (c t) -> p c t", t=2), flag, None, ALU.mult),
                 reads=[b_hr, b_cs], writes=[b_xh])

        def ffn(i):
            scrm.reset()
            GW = 11
            hbuf = scrm.bf16(GW * NT).rearrange("p (m t) -> p m t", m=GW)
            gext = [scrm.f32(NT + 10), scrm.f32(NT + 10)]
            acc = [scrm.f32(NT), scrm.f32(NT)]
            convo = scrm.f32(FCH * 10)
            b_h = [Buf("h%d" % m) for m in range(GW)]
            b_ge = [Buf("ge0"), Buf("ge1")]
            b_ac = [Buf("ac0"), Buf("ac1")]
            b_co = Buf("convo")
            win = ffn_w_in[i]
            wout = ffn_w_out[i]
            xbufs = b_xb + [b_xh]

            def rhs(kk, ti):
                a, b = TGS[ti]
                if ti == 2:
                    return xb[:, kk, a:NX]
                return xb[:, kk, a:b]

            mglob = 0
            for g in range(FCH // GW):
                items = []
                blocks = []
                c0 = g * GW * 128
                off = 0
                while off < GW * 128:
                    n = min(256, GW * 128 - off)
                    items.append((win[:, c0 + off:c0 + off + n], NCH, n))
                    items.append((win[:, DFF + c0 + off:DFF + c0 + off + n], NCH, n))
                    blocks.append((off, n))
                    off += n
                ws = WStream(items)
                for bi, (off, n) in enumerate(blocks):
                    gv, gb = ws.get(2 * bi)
                    vv, vb = ws.get(2 * bi + 1)
                    for mo in range(0, n, 128):
                        m = (off + mo) // 128
                        mg = g * GW + m
                        j = mglob % 2
                        mglob += 1
                        fns = []
                        for kk in range(NCH):
                            for ti in range(3):
                                w = (TGS[ti][1] - TGS[ti][0]) + (2 if ti == 2 else 0)
                                fns.append(lambda h, kk=kk, ti=ti, w=w, mo=mo, gv=gv: h.matmul(
                                    pbank[ti][:, 0:w], gv[:, kk, mo:mo + 128], rhs(kk, ti), start=(kk == 0), stop=(kk == NCH - 1)))
                        S.group("pe", fns, reads=[gb] + xbufs, writes=b_pb[0:3])
                        fns = []
                        for kk in range(NCH):
                            for ti in range(3):
                                w = (TGS[ti][1] - TGS[ti][0])
                                fns.append(lambda h, kk=kk, ti=ti, w=w, mo=mo, vv=vv: h.matmul(
                                    pbank[3 + ti][:, 0:w], vv[:, kk, mo:mo + 128], xb[:, kk, TGS[ti][0]:TGS[ti][1]],
                                    start=(kk == 0), stop=(kk == NCH - 1)))
                        S.group("pe", fns, reads=[vb] + b_xb, writes=b_pb[3:6])
                        ge, ac = gext[j], acc[j]
                        ges = ge[:, T + 2:T + 2 + 40].rearrange("p (b t) -> p b t", t=10)
                        acs = ac[:, T:NT].rearrange("p (b t) -> p b t", t=8)
                        S.op("act", lambda h, ge=ge: h.copy(ge[:, 2:514], pbank[0][:, 0:512]), reads=[b_pb[0]], writes=[b_ge[j]])
                        S.op("act", lambda h, ge=ge: h.copy(ge[:, 514:1026], pbank[1][:, 0:512]), reads=[b_pb[1]], writes=[b_ge[j]])
                        S.op("act", lambda h, ges=ges: h.copy(ges[:, :, 2:10], pbank[2][:, 0:32].rearrange("p (b t) -> p b t", t=8)),
                             reads=[b_pb[2]], writes=[b_ge[j]])
                        S.op("act", lambda h, ge=ge: h.copy(ge[:, 0:2], pbank[2][:, 32:34]), reads=[b_pb[2]], writes=[b_ge[j]])
                        fo = (i * FCH + mg) * 8
                        S.op("pool", lambda h, ges=ges, fo=fo: h.tensor_copy(ges[:, :, 0:2], fcs[:, fo:fo + 8].rearrange("p (b t) -> p b t", t=2)),
                             reads=[b_fcs], writes=[b_ge[j]])
                        w0, w1, w2, cb = (col("cw0_%d" % i, mg), col("cw1_%d" % i, mg), col("cw2_%d" % i, mg), col("cb_%d" % i, mg))
                        S.op("act", lambda h, ge=ge, ac=ac, w2=w2, cb=cb: h.activation(ac[:, 0:T], ge[:, 2:T + 2], AF.Identity, bias=cb, scale=w2),
                             reads=[b_ge[j], b_cs], writes=[b_ac[j]])
                        S.op("act", lambda h, ges=ges, acs=acs, w2=w2, cb=cb: h.activation(acs, ges[:, :, 2:10], AF.Identity, bias=cb, scale=w2),
                             reads=[b_ge[j], b_cs], writes=[b_ac[j]])
                        S.op("dve", lambda h, ge=ge, ac=ac, w1=w1: h.scalar_tensor_tensor(ac[:, 0:T], ge[:, 1:T + 1], w1, ac[:, 0:T], ALU.mult, ALU.add),
                             reads=[b_ge[j]], writes=[b_ac[j]])
                        S.op("dve", lambda h, ges=ges, acs=acs, w1=w1: h.scalar_tensor_tensor(acs, ges[:, :, 1:9], w1, acs, ALU.mult, ALU.add),
                             reads=[b_ge[j]], writes=[b_ac[j]])
                        S.op("dve", lambda h, ge=ge, ac=ac, w0=w0: h.scalar_tensor_tensor(ac[:, 0:T], ge[:, 0:T], w0, ac[:, 0:T], ALU.mult, ALU.add),
                             reads=[b_ge[j]], writes=[b_ac[j]])
                        S.op("dve", lambda h, ges=ges, acs=acs, w0=w0: h.scalar_tensor_tensor(acs, ges[:, :, 0:8], w0, acs, ALU.mult, ALU.add),
                             reads=[b_ge[j]], writes=[b_ac[j]])
                        S.op("pool", lambda h, ge=ge, mg=mg: h.tensor_copy(convo[:, mg * 10:mg * 10 + 2], ge[:, T:T + 2]),
                             reads=[b_ge[j]], writes=[b_co])
                        S.op("pool", lambda h, ges=ges, mg=mg: h.tensor_copy(convo[:, mg * 10 + 2:mg * 10 + 10].rearrange("p (b t) -> p b t", t=2), ges[:, :, 8:10]),
                             reads=[b_ge[j]], writes=[b_co])
                        S.op("act", lambda h, ac=ac: h.activation(ac, ac, AF.Gelu), reads=[], writes=[b_ac[j]])
                        for ti, (a, b) in enumerate(TGS):
                            S.op("dve", lambda h, ac=ac, m=m, ti=ti, a=a, b=b: h.tensor_tensor(hbuf[:, m, a:b], ac[:, a:b], pbank[3 + ti][:, 0:b - a], ALU.mult),
                                 reads=[b_ac[j], b_pb[3 + ti]], writes=[b_h[m]])
                r0 = g * GW * 128
                items = [(wout[r0:r0 + GW * 128, n0:n0 + 256], GW, 256) for n0 in range(0, D, 256)]
                ws = WStream(items)
                for li in range(len(items)):
                    wv, wb = ws.get(li)
                    for mo in (0, 128):
                        n = (li * 256 + mo) // 128
                        pset = (n % 2) * 3
                        fns = []
                        for kk in range(GW):
                            for ti, (a, b) in enumerate(TGS):
                                fns.append(lambda h, kk=kk, ti=ti, a=a, b=b, mo=mo, wv=wv, pset=pset: h.matmul(
                                    pbank[pset + ti][:, 0:b - a], wv[:, kk, mo:mo + 128], hbuf[:, kk, a:b], start=(kk == 0), stop=(kk == GW - 1)))
                        S.group("pe", fns, reads=[wb] + b_h, writes=b_pb[pset:pset + 3])
                        for ti, (a, b) in enumerate(TGS):
                            S.op("dve", lambda h, n=n, ti=ti, a=a, b=b, pset=pset: h.tensor_tensor(xT[:, n, a:b], xT[:, n, a:b], pbank[pset + ti][:, 0:b - a], ALU.add),
                                 reads=[b_pb[pset + ti]], writes=[b_x[n]])
            S.dma("sp", convT[:, i * FCH * 10:(i + 1) * FCH * 10], convo, reads=[b_co], writes=[b_out])


        def gla(j):
            scrm.reset()
            win, wg2d, wout = gla_w_in[j], gla_w_g2[j], gla_w_out[j]
            glow = scrm.bf16(NT)
            wg2 = scrm.bf16(1024)
            b_glow, b_wg2 = Buf("glow"), Buf("wg2")
            S.dma("pool", wg2[0:16, :], wg2d, writes=[b_wg2])
            gv, gb = wload_flat(gla_wg[j], NCH, 16)
            fns = []
            for kk in range(NCH):
                for ti, (a, b) in enumerate(TGS):
                    fns.append(lambda h, kk=kk, ti=ti, a=a, b=b: h.matmul(pbank[ti][0:16, 0:b - a], gv[:, kk, 0:16], xb[:, kk, a:b],
                                                                         start=(kk == 0), stop=(kk == NCH - 1)))
            S.group("pe", fns, reads=[gb] + b_xb, writes=b_pb[0:3])
            for ti, (a, b) in enumerate(TGS):
                S.op("act", lambda h, ti=ti, a=a, b=b: h.copy(glow[0:16, a:b], pbank[ti][0:16, 0:b - a]), reads=[b_pb[ti]], writes=[b_glow])
            mark0 = scrm.p
            NTILE = 9
            for hd in range(4):
                S.barrier()
                scrm.p = mark0
                qd = scrm.bf16(2 * NT).rearrange("p (c t) -> p c t", c=2)
                ki = scrm.bf16(2 * NT).rearrange("p (c t) -> p c t", c=2)
                kltm = scrm.bf16(NTILE * 256).rearrange("p (n d) -> p n d", n=NTILE)
                vtm = scrm.bf16(NTILE * 512).rearrange("p (n d) -> p n d", n=NTILE)
                gr = scrm.bf16(4 * NT).rearrange("p (c t) -> p c t", c=4)
                Sf = scrm.f32(1024).rearrange("p (c d) -> p c d", c=2)
                Sb = scrm.bf16(1024).rearrange("p (c d) -> p c d", c=2)
                elast = scrm.f32(2 * 20).rearrange("p (c n) -> p c n", c=2)
                b_qd, b_ki, b_kl, b_v, b_gr, b_S, b_Sb, b_el = (Buf("qd"), Buf("ki"), Buf("kl"), Buf("v"), Buf("gr"), Buf("S"), Buf("Sb"), Buf("el"))
                mark1 = scrm.p
                cum = scrm.f32(NT)
                ep = scrm.f32(NT)
                en = scrm.f32(NT)
                klf = scrm.bf16(NT)
                b_cum, b_ep, b_en, b_klf = Buf("cum"), Buf("ep"), Buf("en"), Buf("klf")
                qv, qb = wload(win[:, hd * 256:(hd + 1) * 256], NCH, 256)
                kv, kb = wload(win[:, 1024 + hd * 256:1024 + (hd + 1) * 256], NCH, 256)
                for dc in range(2):
                    cidx = hd * 2 + dc
                    fns = []
                    for ti, (a, b) in enumerate(TGS):
                        fns.append(lambda h, ti=ti, a=a, b=b, cidx=cidx: h.matmul(pbank[ti][:, 0:b - a], wg2[0:16, cidx * 128:(cidx + 1) * 128],
                                                                                 glow[0:16, a:b], start=True, stop=True))
                    S.group("pe", fns, reads=[b_wg2, b_glow], writes=b_pb[0:3])
                    for ti, (a, b) in enumerate(TGS):
                        S.op("act", lambda h, ti=ti, a=a, b=b, cidx=cidx: h.activation(cum[:, a:b], pbank[ti][:, 0:b - a], AF.Identity,
                                                                                       bias=col("bg%d" % j, cidx), scale=1.0),
                             reads=[b_pb[ti], b_cs], writes=[b_cum])
                    S.op("act", lambda h: h.activation(ep, cum, AF.Exp, scale=-1.0), reads=[b_cum], writes=[b_ep])
                    S.op("act", lambda h: h.activation(ep, ep, AF.Ln, bias=1.0, scale=1.0), reads=[], writes=[b_ep])
                    S.op("act", lambda h: h.activation(en, ep, AF.Identity, bias=0.0, scale=-1.0 / 16.0), reads=[b_ep], writes=[b_en])
                    S.op("dve", lambda h: h.tensor_tensor_scan(cum, col("reset", 0, NT), en, 0.0, ALU.mult, ALU.add),
                         reads=[b_en, b_cs], writes=[b_cum])
                    S.op("act", lambda h: h.activation(ep, cum, AF.Exp), reads=[b_cum], writes=[b_ep])
                    S.op("act", lambda h: h.activation(en, cum, AF.Exp, scale=-1.0), reads=[b_cum], writes=[b_en])
                    S.op("dve", lambda h, dc=dc: h.tensor_copy(elast[:, dc, 0:16], ep[:, 63:T:64]), reads=[b_ep], writes=[b_el])
                    S.op("dve", lambda h, dc=dc: h.tensor_copy(elast[:, dc, 16:20], ep[:, T + 7:NT:8]), reads=[b_ep], writes=[b_el])
                    fns = []
                    for kk in range(NCH):
                        for ti, (a, b) in enumerate(TGS):
                            fns.append(lambda h, kk=kk, ti=ti, a=a, b=b, dc=dc: h.matmul(pbank[3 + ti][:, 0:b - a], kv[:, kk, dc * 128:(dc + 1) * 128],
                                                                                         xb[:, kk, a:b], start=(kk == 0), stop=(kk == NCH - 1)))
                    S.group("pe", fns, reads=[kb] + b_xb, writes=b_pb[3:6])
                    for ti, (a, b) in enumerate(TGS):
                        S.op("dve", lambda h, ti=ti, a=a, b=b, dc=dc: h.tensor_tensor(ki[:, dc, a:b], pbank[3 + ti][:, 0:b - a], en[:, a:b], ALU.mult),
                             reads=[b_pb[3 + ti], b_en], writes=[b_ki])
                    S.op("dve", lambda h, dc=dc: h.tensor_tensor(klf[:, 0:T].rearrange("p (c t) -> p c t", t=64), ki[:, dc, 0:T].rearrange("p (c t) -> p c t", t=64),
                                                                 elast[:, dc, 0:16].unsqueeze(2).to_broadcast([128, 16, 64]), ALU.mult),
                         reads=[b_ki, b_el], writes=[b_klf])
                    S.op("dve", lambda h, dc=dc: h.tensor_tensor(klf[:, T:NT].rearrange("p (c t) -> p c t", t=8), ki[:, dc, T:NT].rearrange("p (c t) -> p c t", t=8),
                                                                 elast[:, dc, 16:20].unsqueeze(2).to_broadcast([128, 4, 8]), ALU.mult),
                         reads=[b_ki, b_el], writes=[b_klf])
                    ptr = pbank[6][:].bitcast(BF16)
                    for half in range(3):
                        tiles = list(range(half * 4, min(NTILE, half * 4 + 4)))
                        fns = []
                        for n in tiles:
                            w = 128 if n < 8 else 32
                            fns.append(lambda h, n=n, w=w, half=half: h.transpose(ptr[0:w, (n - half * 4) * 128:(n - half * 4) * 128 + 128],
                                                                                  klf[:, n * 128:n * 128 + w], identb[:]))
                        S.group("pe", fns, reads=[b_klf], writes=[b_pb[6]])
                        for n in tiles:
                            w = 128 if n < 8 else 32
                            S.op("act", lambda h, n=n, w=w, half=half, dc=dc: h.copy(kltm[0:w, n, dc * 128:(dc + 1) * 128],
                                                                                     ptr[0:w, (n - half * 4) * 128:(n - half * 4) * 128 + 128]),
                                 reads=[b_pb[6]], writes=[b_kl])
                    fns = []
                    for kk in range(NCH):
                        for ti, (a, b) in enumerate(TGS):
                            fns.append(lambda h, kk=kk, ti=ti, a=a, b=b, dc=dc: h.matmul(pbank[ti][:, 0:b - a], qv[:, kk, dc * 128:(dc + 1) * 128],
                                                                                         xb[:, kk, a:b], start=(kk == 0), stop=(kk == NCH - 1)))
                    S.group("pe", fns, reads=[qb] + b_xb, writes=b_pb[0:3])
                    for ti, (a, b) in enumerate(TGS):
                        S.op("dve", lambda h, ti=ti, a=a, b=b, dc=dc: h.scalar_tensor_tensor(qd[:, dc, a:b], pbank[ti][:, 0:b - a], col("qs"), ep[:, a:b],
                                                                                             ALU.mult, ALU.mult),
                             reads=[b_pb[ti], b_ep], writes=[b_qd])
                for hf in range(2):
                    vv, vb = wload(win[:, 2048 + hd * 512 + hf * 256:2048 + hd * 512 + (hf + 1) * 256], NCH, 256)
                    for n in range(NTILE):
                        w = 128 if n < 8 else 32
                        pbk = 6 + (n % 2)
                        fns = []
                        for kk in range(NCH):
                            fns.append(lambda h, kk=kk, n=n, w=w, pbk=pbk, vv=vv: h.matmul(pbank[pbk][0:w, 0:256], xb[:, kk, n * 128:n * 128 + w], vv[:, kk, :],
                                                                                           start=(kk == 0), stop=(kk == NCH - 1)))
                        S.group("pe", fns, reads=[vb] + b_xb, writes=[b_pb[pbk]])
                        S.op("act", lambda h, n=n, w=w, pbk=pbk, hf=hf: h.copy(vtm[0:w, n, hf * 256:(hf + 1) * 256], pbank[pbk][0:w, 0:256]),
                             reads=[b_pb[pbk]], writes=[b_v])
                S.barrier()
                scrm.p = mark1
                at = [scrm.bf16(128), scrm.bf16(128)]
                sq = scrm.f32(512)
                rst = scrm.f32(128)
                t1 = scrm.f32(512)
                s0f = [scrm.f32(1024).rearrange("p (c d) -> p c d", c=2)] * 2
                s0b = [scrm.bf16(1024).rearrange("p (c d) -> p c d", c=2)] * 2
                klm = scrm.bf16(4 * 256).rearrange("p (b d) -> p b d", b=4)
                srecv = scrm.f32(1024).rearrange("p (c d) -> p c d", c=2)
                b_at = [Buf("at0"), Buf("at1")]
                b_sq, b_rst, b_t1, b_klm, b_srecv = Buf("sq"), Buf("rst"), Buf("t1"), Buf("klm"), Buf("srecv")
                b_s0f = [Buf("s0f0")] * 2
                b_s0b = [Buf("s0b0")] * 2
                b_gin, b_gout = Buf("gin"), Buf("gout")

                sp_state = {"banks": [3, 4, 6, 7], "i": 0}

                def state_update(n, r0, r1, ci, Sdst, b_Sdst, Ssrc, b_Ssrc, lhs_fn):
                    for dc in range(2):
                        bk = sp_state["banks"][sp_state["i"] % len(sp_state["banks"])]
                        sp_state["i"] += 1
                        S.group("pe", [lambda h, dc=dc, bk=bk: h.matmul(pbank[bk][:, 0:512], lhs_fn(dc), vtm[r0:r1, n, :], start=True, stop=True)],
                                reads=[b_kl, b_v, b_klm], writes=[b_pb[bk]])
                        S.op("dve", lambda h, dc=dc, bk=bk: h.scalar_tensor_tensor(Sdst[:, dc, :], Ssrc[:, dc, :], elast[:, dc, ci:ci + 1], pbank[bk][:, 0:512],
                                                                                   ALU.mult, ALU.add),
                             reads=[b_pb[bk], b_el, b_Ssrc], writes=[b_Sdst])

                def prompt_scan(want_o):
                    for n in range(8):
                        if want_o:
                            pa = pbank[0][:, (n % 2) * 128:(n % 2) * 128 + 128]
                            fns = [lambda h, dc=dc, n=n, pa=pa: h.matmul(pa, ki[:, dc, n * 128:(n + 1) * 128], qd[:, dc, n * 128:(n + 1) * 128],
                                                                        start=(dc == 0), stop=(dc == 1)) for dc in range(2)]
                            S.group("pe", fns, reads=[b_ki, b_qd], writes=[b_pb[0]])
                            a_ = at[n % 2]
                            S.op("dve", lambda h, pa=pa, a_=a_: h.tensor_tensor(a_, pa, col("tmask", 0, 128), ALU.mult),
                                 reads=[b_pb[0], b_cs], writes=[b_at[n % 2]])
                            ob = 1 + (n % 2)
                            po = pbank[ob][:, 0:512].rearrange("p (c t) -> p c t", c=4)
                        for cc in range(2):
                            ci = 2 * n + cc
                            r0 = cc * 64
                            if want_o:
                                fns = []
                                for dvc in range(4):
                                    fns.append(lambda h, dvc=dvc, cc=cc, n=n, a_=a_, po=po: h.matmul(po[:, dvc, cc * 64:cc * 64 + 64], vtm[:, n, dvc * 128:(dvc + 1) * 128],
                                                                                                   a_[:, cc * 64:cc * 64 + 64], start=True, stop=False))
                                    for dc in range(2):
                                        fns.append(lambda h, dvc=dvc, dc=dc, cc=cc, n=n, po=po: h.matmul(po[:, dvc, cc * 64:cc * 64 + 64], Sb[:, dc, dvc * 128:(dvc + 1) * 128],
                                                                                                       qd[:, dc, n * 128 + cc * 64:n * 128 + cc * 64 + 64], start=False, stop=(dc == 1)))
                                S.group("pe", fns, reads=[b_v, b_at[n % 2], b_Sb, b_qd], writes=[b_pb[ob]])
                            state_update(n, r0, r0 + 64, ci, Sf, b_S, Sf, b_S, lambda dc, n=n, r0=r0: kltm[r0:r0 + 64, n, dc * 128:(dc + 1) * 128])
                            if want_o:
                                S.op("act", lambda h: h.copy(Sb[:], Sf[:]), reads=[b_S], writes=[b_Sb])
                        if want_o:
                            finish_tile(n, 128, ob, po)

                def finish_tile(n, w, ob, po):
                    c0 = n * 128
                    sqv = sq.rearrange("p (c t) -> p c t", c=4)
                    S.op("act", lambda h: h.activation(sqv[:, :, 0:w], po[:, :, 0:w], AF.Square), reads=[b_pb[ob]], writes=[b_sq])
                    fns = [lambda h, dvc=dvc: h.matmul(pbank[5][:, 0:w], col("ones", 0, 128), sqv[:, dvc, 0:w], start=(dvc == 0), stop=(dvc == 3)) for dvc in range(4)]
                    S.group("pe", fns, reads=[b_sq, b_cs], writes=[b_pb[5]])
                    S.op("act", lambda h: h.activation(rst[:, 0:w], pbank[5][:, 0:w], AF.Sqrt, bias=col("eps"), scale=1.0 / 512.0), reads=[b_pb[5], b_cs], writes=[b_rst])
                    S.op("dve", lambda h: h.reciprocal(rst[:, 0:w], rst[:, 0:w]), reads=[], writes=[b_rst])
                    t1v = t1.rearrange("p (c t) -> p c t", c=4)
                    for dvc in range(4):
                        S.op("dve", lambda h, dvc=dvc: h.scalar_tensor_tensor(t1v[:, dvc, 0:w], po[:, dvc, 0:w], col("gnw%d" % j, dvc), rst[:, 0:w], ALU.mult, ALU.mult),
                             reads=[b_pb[ob], b_rst, b_cs], writes=[b_t1])
                    S.op("pool", lambda h: h.tensor_tensor(gr[:, :, c0:c0 + w], t1v[:, :, 0:w], gr[:, :, c0:c0 + w], ALU.mult), reads=[b_t1], writes=[b_gr])

                S.op("pool", lambda h: h.memset(Sf[:], 0.0), writes=[b_S])
                sp_state["banks"] = [0, 1, 2, 3, 4, 5, 6, 7]
                prompt_scan(False)
                sp_state["banks"] = [3, 4, 6, 7]
                xi = j * 4 + hd
                S.dma("sp", gs_in[xi], Sf[:].rearrange("p c d -> p (c d)"), reads=[b_S], writes=[b_gin])
                S.allgather(gs_in[xi], gs_out[xi], reads=[b_gin], writes=[b_gout])
                for hf in range(2):
                    rv, rb = wload(win[:, 4096 + hd * 512 + hf * 256:4096 + hd * 512 + (hf + 1) * 256], NCH, 256)
                    for mo in (0, 128):
                        dvc = hf * 2 + mo // 128
                        pset = (dvc % 2) * 3
                        fns = []
                        for kk in range(NCH):
                            for ti, (a, b) in enumerate(TGS):
                                fns.append(lambda h, kk=kk, ti=ti, a=a, b=b, mo=mo, rv=rv, pset=pset: h.matmul(pbank[pset + ti][:, 0:b - a], rv[:, kk, mo:mo + 128],
                                                                                                               xb[:, kk, a:b], start=(kk == 0), stop=(kk == NCH - 1)))
                        S.group("pe", fns, reads=[rb] + b_xb, writes=b_pb[pset:pset + 3])
                        for ti, (a, b) in enumerate(TGS):
                            S.op("act", lambda h, ti=ti, a=a, b=b, dvc=dvc, pset=pset: h.activation(gr[:, dvc, a:b], pbank[pset + ti][:, 0:b - a], AF.Silu),
                                 reads=[b_pb[pset + ti]], writes=[b_gr])
                S.dma("sp", srecv[:].rearrange("p c d -> p (c d)"), gs_out[xi][0:128, :], reads=[b_gout], writes=[b_srecv])
                S.op("act", lambda h: h.activation(Sf[:], srecv[:], AF.Identity, bias=0.0, scale=flag), reads=[b_srecv, b_cs], writes=[b_S])
                S.op("act", lambda h: h.copy(Sb[:], Sf[:]), reads=[b_S], writes=[b_Sb])
                prompt_scan(True)
                for dc in range(2):
                    S.dma("sp", gla_sp[j, hd, dc * 128:(dc + 1) * 128, :], Sf[:, dc, :], reads=[b_S], writes=[b_out])
                n = 8
                pa = pbank[0][0:32, 0:32]
                fns = [lambda h, dc=dc: h.matmul(pa, ki[:, dc, T:NT], qd[:, dc, T:NT], start=(dc == 0), stop=(dc == 1)) for dc in range(2)]
                S.group("pe", fns, reads=[b_ki, b_qd], writes=[b_pb[0]])
                a_ = at[0]
                S.op("dve", lambda h: h.tensor_tensor(a_[0:32, 0:32], pa, col("smask", 0, 32)[0:32, :], ALU.mult), reads=[b_pb[0], b_cs], writes=[b_at[0]])
                for bb in range(4):
                    S.op("dve", lambda h, bb=bb: h.tensor_scalar(klm[0:32, bb, :], kltm[0:32, 8, :], col("rowm", bb)[0:32, :], None, ALU.mult),
                         reads=[b_kl, b_cs], writes=[b_klm])
                ob = 1
                po = pbank[ob][:, 0:512].rearrange("p (c t) -> p c t", c=4)
                first = [True] * 4
                for bb in range(4):
                    sj = bb % 2
                    for dc in range(2):
                        S.dma("sp", s0f[sj][:, dc, :], gla_s0[j, bb, hd, dc * 128:(dc + 1) * 128, :], writes=[b_s0f[sj]])
                    S.op("act", lambda h, sj=sj: h.copy(s0b[sj][:], s0f[sj][:]), reads=[b_s0f[sj]], writes=[b_s0b[sj]])
                    fns = []
                    for dvc in range(4):
                        fns.append(lambda h, dvc=dvc, bb=bb: h.matmul(po[:, dvc, bb * 8:bb * 8 + 8], vtm[0:32, 8, dvc * 128:(dvc + 1) * 128],
                                                                      a_[0:32, bb * 8:bb * 8 + 8], start=True, stop=False))
                        for dc in range(2):
                            fns.append(lambda h, dvc=dvc, dc=dc, bb=bb, sj=sj: h.matmul(po[:, dvc, bb * 8:bb * 8 + 8], s0b[sj][:, dc, dvc * 128:(dvc + 1) * 128],
                                                                                       qd[:, dc, T + bb * 8:T + bb * 8 + 8], start=False, stop=(dc == 1)))
                    S.group("pe", fns, reads=[b_v, b_at[0], b_s0b[sj], b_qd], writes=[b_pb[ob]])
                    state_update(8, 0, 32, 16 + bb, s0f[sj], b_s0f[sj], s0f[sj], b_s0f[sj], lambda dc, bb=bb: klm[0:32, bb, dc * 128:(dc + 1) * 128])
                    for dc in range(2):
                        S.dma("sp", gla_ss[j, bb, hd, dc * 128:(dc + 1) * 128, :], s0f[sj][:, dc, :], reads=[b_s0f[sj]], writes=[b_out])
                finish_tile(8, 32, ob, po)
                for hf in range(2):
                    wv, wb = wload(wout[hd * 512:(hd + 1) * 512, hf * 1024:(hf + 1) * 1024], 4, 1024)
                    for mo in range(0, 1024, 128):
                        nn = (hf * 1024 + mo) // 128
                        pset = (nn % 2) * 3
                        fns = []
                        for kk in range(4):
                            for ti, (a, b) in enumerate(TGS):
                                fns.append(lambda h, kk=kk, ti=ti, a=a, b=b, mo=mo, wv=wv, pset=pset: h.matmul(pbank[pset + ti][:, 0:b - a], wv[:, kk, mo:mo + 128],
                                                                                                               gr[:, kk, a:b], start=(kk == 0), stop=(kk == 3)))
                        S.group("pe", fns, reads=[wb, b_gr], writes=b_pb[pset:pset + 3])
                        for ti, (a, b) in enumerate(TGS):
                            S.op("dve", lambda h, nn=nn, ti=ti, a=a, b=b, pset=pset: h.tensor_tensor(xT[:, nn, a:b], xT[:, nn, a:b], pbank[pset + ti][:, 0:b - a], ALU.add),
                                 reads=[b_pb[pset + ti]], writes=[b_x[nn]])


        def sgmix():
            scrm.reset()
            win, wout = sg_w_in[0], sg_w_out[0]
            vb = scrm.bf16(NCH * NT).rearrange("p (c t) -> p c t", c=NCH)
            vs32 = scrm.f32(NCH * 32).rearrange("p (c t) -> p c t", c=NCH)
            b_vb = [Buf("vb%d" % c) for c in range(NCH)]
            b_vs = Buf("vs32")
            markA = scrm.p
            tmp = [scrm.f32(NT), scrm.f32(NT)]
            s1 = scrm.f32(NT)
            s2 = scrm.f32(NT)
            ts = scrm.f32(32)
            b_tmp = [Buf("sgt0"), Buf("sgt1")]
            b_s1, b_s2, b_ts = Buf("sgs1"), Buf("sgs2"), Buf("sgts")
            ws = WStream([(win[:, D + n0:D + n0 + 256], NCH, 256) for n0 in range(0, D, 256)])
            for li in range(8):
                wv, wb = ws.get(li)
                for mo in (0, 128):
                    c = (li * 256 + mo) // 128
                    pset = (c % 2) * 3
                    jt = c % 2
                    fns = []
                    for kk in range(NCH):
                        for ti, (a, b) in enumerate(TGS):
                            fns.append(lambda h, kk=kk, ti=ti, a=a, b=b, mo=mo, wv=wv, pset=pset: h.matmul(pbank[pset + ti][:, 0:b - a], wv[:, kk, mo:mo + 128],
                                                                                                           xb[:, kk, a:b], start=(kk == 0), stop=(kk == NCH - 1)))
                    S.group("pe", fns, reads=[wb] + b_xb, writes=b_pb[pset:pset + 3])
                    for ti, (a, b) in enumerate(TGS):
                        S.op("act", lambda h, ti=ti, a=a, b=b, pset=pset, jt=jt, c=c: h.activation(tmp[jt][:, a:b], pbank[pset + ti][:, 0:b - a], AF.Gelu,
                                                                                                   bias=col("sgbi", NCH + c), scale=1.0),
                             reads=[b_pb[pset + ti], b_cs], writes=[b_tmp[jt]])
                    S.op("dve", lambda h, c=c, jt=jt: h.tensor_copy(vb[:, c, :], tmp[jt]), reads=[b_tmp[jt]], writes=[b_vb[c]])
                    S.op("dve", lambda h, c=c, jt=jt: h.tensor_copy(vs32[:, c, :], tmp[jt][:, T:NT]), reads=[b_tmp[jt]], writes=[b_vs])
                    if c == 0:
                        S.op("pool", lambda h, jt=jt: h.tensor_copy(s1, tmp[jt]), reads=[b_tmp[jt]], writes=[b_s1])
                    else:
                        S.op("pool", lambda h, jt=jt: h.tensor_tensor(s1, s1, tmp[jt], ALU.add), reads=[b_tmp[jt]], writes=[b_s1])
                    S.op("act", lambda h, jt=jt: h.activation(tmp[jt], tmp[jt], AF.Square), reads=[], writes=[b_tmp[jt]])
                    if c == 0:
                        S.op("pool", lambda h, jt=jt: h.tensor_copy(s2, tmp[jt]), reads=[b_tmp[jt]], writes=[b_s2])
                    else:
                        S.op("pool", lambda h, jt=jt: h.tensor_tensor(s2, s2, tmp[jt], ALU.add), reads=[b_tmp[jt]], writes=[b_s2])
            fns = [lambda h, ti=ti, a=a, b=b: h.matmul(pbank[ti][:, 0:b - a], onesd, s1[:, a:b], start=True, stop=True) for ti, (a, b) in enumerate(TGS)]
            S.group("pe", fns, reads=[b_cs, b_s1], writes=b_pb[0:3])
            fns = [lambda h, ti=ti, a=a, b=b: h.matmul(pbank[3 + ti][:, 0:b - a], onesd, s2[:, a:b], start=True, stop=True) for ti, (a, b) in enumerate(TGS)]
            S.group("pe", fns, reads=[b_cs, b_s2], writes=b_pb[3:6])
            for ti, (a, b) in enumerate(TGS):
                S.op("act", lambda h, ti=ti, a=a, b=b: h.copy(s1[:, a:b], pbank[ti][:, 0:b - a]), reads=[b_pb[ti]], writes=[b_s1])
            S.op("dve", lambda h: h.tensor_tensor(s2, s1, s1, ALU.mult), reads=[b_s1], writes=[b_s2])
            for ti, (a, b) in enumerate(TGS):
                S.op("dve", lambda h, ti=ti, a=a, b=b: h.tensor_tensor(s2[:, a:b], pbank[3 + ti][:, 0:b - a], s2[:, a:b], ALU.subtract),
                     reads=[b_pb[3 + ti]], writes=[b_s2])
            S.op("act", lambda h: h.activation(s2, s2, AF.Sqrt, bias=col("eps"), scale=1.0), reads=[b_cs], writes=[b_s2])
            S.op("dve", lambda h: h.reciprocal(s2, s2), reads=[], writes=[b_s2])
            for c in range(NCH):
                jt = c % 2
                S.op("dve", lambda h, c=c, jt=jt: h.tensor_tensor(tmp[jt], vb[:, c, :], s1, ALU.subtract), reads=[b_vb[c], b_s1], writes=[b_tmp[jt]])
                S.op("pool", lambda h, jt=jt: h.tensor_tensor(tmp[jt], tmp[jt], s2, ALU.mult), reads=[b_s2], writes=[b_tmp[jt]])
                S.op("act", lambda h, c=c, jt=jt: h.activation(vb[:, c, :], tmp[jt], AF.Identity, bias=col("sglb", c), scale=col("sglg", c)),
                     reads=[b_tmp[jt], b_cs], writes=[b_vb[c]])
                S.op("dve", lambda h, c=c: h.tensor_tensor(ts, vs32[:, c, :], s1[:, T:NT], ALU.subtract), reads=[b_vs, b_s1], writes=[b_ts])
                S.op("dve", lambda h: h.tensor_tensor(ts, ts, s2[:, T:NT], ALU.mult), reads=[b_s2], writes=[b_ts])
                S.op("act", lambda h, c=c: h.activation(vs32[:, c, :], ts, AF.Identity, bias=col("sglb", c), scale=col("sglg", c)),
                     reads=[b_ts, b_cs], writes=[b_vs])
            S.dma("sp", sgv, vs32[:].rearrange("p c t -> p (c t)"), reads=[b_vs], writes=[b_out])
            S.barrier()
            scrm.p = markA
            vtm = [scrm.bf16(D), scrm.bf16(D)]
            utmp = [scrm.f32(NT), scrm.f32(NT)]
            sgcs = scrm.f32(1280)
            wsm = scrm.bf16(512).rearrange("p (g t) -> p g t", g=4)
            wssm = scrm.bf16(128).rearrange("p (g t) -> p g t", g=4)
            b_vtm = [Buf("vtm0"), Buf("vtm1")]
            b_ut = [Buf("ut0"), Buf("ut1")]
            b_sgcs, b_wsm = Buf("sgcs"), Buf("wsm")
            S.dma("sp", sgcs, sgc, writes=[b_sgcs])
            wsT = sgcs[:, 0:512].rearrange("p (g t) -> p g t", g=4)
            wsTs = sgcs[:, 512:640].rearrange("p (g t) -> p g t", g=4)
            bsb = sgcs[:, 640:1152].rearrange("p (g t) -> p g t", g=4)
            bsbs = sgcs[:, 1152:1280].rearrange("p (g t) -> p g t", g=4)
            for g in range(4):
                S.op("dve", lambda h, g=g: h.tensor_tensor(wsm[:, g, :], wsT[:, g, :], col("tri", 0, 128), ALU.mult), reads=[b_sgcs, b_cs], writes=[b_wsm])
                S.op("dve", lambda h, g=g: h.tensor_tensor(wssm[0:32, g, :], wsTs[0:32, g, :], col("smask", 0, 32)[0:32, :], ALU.mult),
                     reads=[b_sgcs, b_cs], writes=[b_wsm])
            ptr = [pbank[6][:].bitcast(BF16), pbank[7][:].bitcast(BF16)]
            for n in range(9):
                w = 128 if n < 8 else 32
                c0 = n * 128
                jt = n % 2
                for hb in range(2):
                    fns = [lambda h, c=c, hb=hb, w=w, c0=c0: h.transpose(ptr[hb][0:w, (c % 8) * 128:(c % 8) * 128 + 128], vb[:, c, c0:c0 + w], identb[:])
                           for c in range(hb * 8, hb * 8 + 8)]
                    S.group("pe", fns, reads=b_vb[hb * 8:hb * 8 + 8], writes=[b_pb[6 + hb]])
                S.op("act", lambda h, jt=jt, w=w: h.copy(vtm[jt][0:w, 0:1024], ptr[0][0:w, :]), reads=[b_pb[6]], writes=[b_vtm[jt]])
                S.op("dve", lambda h, jt=jt, w=w: h.tensor_copy(vtm[jt][0:w, 1024:2048], ptr[1][0:w, :]), reads=[b_pb[7]], writes=[b_vtm[jt]])
                for g in range(4):
                    rhsw = wsm[0:w, g, 0:w] if n < 8 else wssm[0:32, g, 0:32]
                    fns = [lambda h, c=c, g=g, jt=jt, rhsw=rhsw, w=w: h.matmul(pbank[g][:, (c % 4) * 128:(c % 4) * 128 + w], vtm[jt][0:w, c * 128:(c + 1) * 128], rhsw,
                                                                          start=True, stop=True) for c in range(4 * g, 4 * g + 4)]
                    S.group("pe", fns, reads=[b_vtm[jt], b_wsm], writes=[b_pb[g]])
                    bias = (bsb[:, g, 0:w] if n < 8 else bsbs[:, g, 0:32]).unsqueeze(1).to_broadcast([128, 4, w])
                    S.op("dve", lambda h, g=g, bias=bias, w=w, c0=c0: h.tensor_tensor(vb[:, 4 * g:4 * g + 4, c0:c0 + w],
                                                                          pbank[g][:, 0:512].rearrange("p (c t) -> p c t", c=4)[:, :, 0:w], bias, ALU.add),
                         reads=[b_pb[g], b_sgcs], writes=b_vb[4 * g:4 * g + 4])
            ws = WStream([(win[:, n0:n0 + 256], NCH, 256) for n0 in range(0, D, 256)])
            for li in range(8):
                wv, wb = ws.get(li)
                for mo in (0, 128):
                    c = (li * 256 + mo) // 128
                    pset = (c % 2) * 3
                    jt = c % 2
                    fns = []
                    for kk in range(NCH):
                        for ti, (a, b) in enumerate(TGS):
                            fns.append(lambda h, kk=kk, ti=ti, a=a, b=b, mo=mo, wv=wv, pset=pset: h.matmul(pbank[pset + ti][:, 0:b - a], wv[:, kk, mo:mo + 128],
                                                                                                           xb[:, kk, a:b], start=(kk == 0), stop=(kk == NCH - 1)))
                    S.group("pe", fns, reads=[wb] + b_xb, writes=b_pb[pset:pset + 3])
                    for ti, (a, b) in enumerate(TGS):
                        S.op("act", lambda h, ti=ti, a=a, b=b, pset=pset, jt=jt, c=c: h.activation(utmp[jt][:, a:b], pbank[pset + ti][:, 0:b - a], AF.Gelu,
                                                                                                   bias=col("sgbi", c), scale=1.0),
                             reads=[b_pb[pset + ti], b_cs], writes=[b_ut[jt]])
                    S.op("dve", lambda h, c=c, jt=jt: h.tensor_tensor(vb[:, c, :], utmp[jt], vb[:, c, :], ALU.mult), reads=[b_ut[jt]], writes=[b_vb[c]])
            ws = WStream([(wout[:, n0:n0 + 256], NCH, 256) for n0 in range(0, D, 256)])
            for li in range(8):
                wv, wb = ws.get(li)
                for mo in (0, 128):
                    nn = (li * 256 + mo) // 128
                    pset = (nn % 2) * 3
                    fns = []
                    for kk in range(NCH):
                        for ti, (a, b) in enumerate(TGS):
                            fns.append(lambda h, kk=kk, ti=ti, a=a, b=b, mo=mo, wv=wv, pset=pset: h.matmul(pbank[pset + ti][:, 0:b - a], wv[:, kk, mo:mo + 128],
                                                                                                           vb[:, kk, a:b], start=(kk == 0), stop=(kk == NCH - 1)))
                    S.group("pe", fns, reads=[wb] + b_vb, writes=b_pb[pset:pset + 3])
                    for ti, (a, b) in enumerate(TGS):
                        S.op("dve", lambda h, nn=nn, ti=ti, a=a, b=b, pset=pset: h.scalar_tensor_tensor(xT[:, nn, a:b], pbank[pset + ti][:, 0:b - a], col("sgbo", nn),
                                                                                                        xT[:, nn, a:b], ALU.add, ALU.add),
                             reads=[b_pb[pset + ti], b_cs], writes=[b_x[nn]])


        def swa():
            scrm.reset()
            wqkv, wout = swa_w_qkv[0], swa_w_out[0]
            swcs = scrm.f32(SWC_N)
            b_swcs = Buf("swcs")
            S.dma("sp", swcs, swc, writes=[b_swcs])
            cosT, sinT = swcs[:, 0:NT], swcs[:, NT:2 * NT]
            o_ = 2 * NT
            PTf = swcs[:, o_:o_ + 128]
            mask01 = swcs[:, o_ + 128:o_ + 384]
            masks = swcs[:, o_ + 384:o_ + 1024].rearrange("p (b k) -> p b k", b=4)
            bvrow = swcs[:, o_ + 1024:o_ + 1536]
            PTb = scrm.bf16(128)
            mask0 = scrm.f32(256)
            b_PTb, b_mask0 = Buf("PTb"), Buf("mask0")
            S.op("dve", lambda h: h.tensor_copy(PTb, PTf), reads=[b_swcs], writes=[b_PTb])
            S.op("dve", lambda h: h.tensor_scalar(mask0[:, 0:128], mask01[:, 0:128], flag, None, ALU.mult), reads=[b_swcs, b_cs], writes=[b_mask0])
            S.op("dve", lambda h: h.tensor_copy(mask0[:, 128:256], mask01[:, 128:256]), reads=[b_swcs], writes=[b_mask0])
            vz = scrm.bf16(10 * 192).rearrange("p (n d) -> p n d", n=10)
            vcz = scrm.bf16(4 * 192).rearrange("p (n d) -> p n d", n=4)
            b_vz, b_vcz = Buf("vz"), Buf("vcz")
            S.op("pool", lambda h: h.memset(vz[:], 0.0), writes=[b_vz])
            S.op("pool", lambda h: h.memset(vcz[:], 0.0), writes=[b_vcz])
            kT = scrm.bf16(128 + NT)
            qT = scrm.bf16(2 * NT).rearrange("p (c t) -> p c t", c=2)
            kcT = scrm.bf16(512).rearrange("p (b k) -> p b k", b=4)
            raw = [scrm.f32(NT), scrm.f32(NT)]
            rawb = scrm.bf16(NT)
            tcos = scrm.f32(NT)
            kout = scrm.f32(160)
            vout = scrm.f32(128).rearrange("p (n d) -> p n d", n=2)
            xs_ = scrm.f32(192)
            xr = scrm.f32(192)
            b_kT, b_kh, b_kcT, b_rawb, b_tcos, b_kout, b_vout, b_xs, b_xr = (Buf("kT"), Buf("kh"), Buf("kcT"), Buf("rawb"), Buf("tcos"), Buf("kout"),
                                                                              Buf("vout"), Buf("xs"), Buf("xr"))
            b_raw = [Buf("raw0"), Buf("raw1")]
            b_q = [Buf("q%d" % n) for n in range(9)]
            b_si, b_so = Buf("sxi"), Buf("sxo")
            NSET = 2
            e_ = [scrm.f32(256) for _ in range(NSET)]
            pm = [scrm.bf16(256) for _ in range(NSET)]
            pT = [scrm.bf16(256) for _ in range(NSET)]
            dg = [scrm.bf16(128) for _ in range(NSET)]
            stt = [scrm.f32(8) for _ in range(NSET)]
            b_e = [Buf("e%d" % i) for i in range(NSET)]
            b_pm = [Buf("pm%d" % i) for i in range(NSET)]
            b_pT = [Buf("pT%d" % i) for i in range(NSET)]
            b_dg = [Buf("dg%d" % i) for i in range(NSET)]
            b_st = [Buf("st%d" % i) for i in range(NSET)]
            cnt = {"r": 0, "h": 0}

            def rope(pset, bias_col, dst, dst_bufs, scale, side=None):
                j = cnt["r"] % 2
                cnt["r"] += 1
                p2 = 3 - pset
                r_ = raw[j]
                for ti, (a, b) in enumerate(TGS):
                    S.op("act", lambda h, ti=ti, a=a, b=b, r_=r_: h.activation(r_[:, a:b], pbank[pset + ti][:, 0:b - a], AF.Identity, bias=bias_col, scale=1.0),
                         reads=[b_pb[pset + ti], b_cs], writes=[b_raw[j]])
                S.op("dve", lambda h, r_=r_: h.tensor_copy(rawb, r_), reads=[b_raw[j]], writes=[b_rawb])
                fns = [lambda h, ti=ti, a=a, b=b: h.matmul(pbank[p2 + ti][:, 0:b - a], PTb, rawb[:, a:b], start=True, stop=True) for ti, (a, b) in enumerate(TGS)]
                S.group("pe", fns, reads=[b_PTb, b_rawb], writes=b_pb[p2:p2 + 3])
                S.op("dve", lambda h, r_=r_: h.tensor_tensor(tcos, r_, cosT, ALU.mult), reads=[b_raw[j], b_swcs], writes=[b_tcos])
                for ti, (a, b) in enumerate(TGS):
                    S.op("dve", lambda h, ti=ti, a=a, b=b, r_=r_: h.tensor_tensor(r_[:, a:b], pbank[p2 + ti][:, 0:b - a], sinT[:, a:b], ALU.mult),
                         reads=[b_pb[p2 + ti], b_swcs], writes=[b_raw[j]])
                S.op("pool", lambda h, r_=r_: h.tensor_tensor(r_, r_, tcos, ALU.add), reads=[b_tcos], writes=[b_raw[j]])
                S.op("act", lambda h, r_=r_: h.activation(dst, r_, AF.Identity, bias=0.0, scale=scale), reads=[b_raw[j]], writes=dst_bufs)
                if side is not None:
                    side(r_, b_raw[j])

            def softmax_rows(nr, ncols, ps, b_ps, maskap, mask_bufs, sink_col):
                si = cnt["h"] % NSET
                cnt["h"] += 1
                st = stt[si]
                S.op("dve", lambda h: h.tensor_reduce(st[0:nr, 0:1], ps[0:nr, 0:ncols], AX.X, ALU.max), reads=[b_ps], writes=[b_st[si]])
                S.op("dve", lambda h: h.tensor_scalar(st[0:nr, 1:2], st[0:nr, 0:1], -1.0, None, ALU.mult), reads=[], writes=[b_st[si]])
                S.op("act", lambda h: h.activation(e_[si][0:nr, 0:ncols], ps[0:nr, 0:ncols], AF.Exp, bias=st[0:nr, 1:2], scale=1.0),
                     reads=[b_ps, b_st[si]], writes=[b_e[si]])
                S.op("dve", lambda h: h.tensor_tensor(e_[si][0:nr, 0:ncols], e_[si][0:nr, 0:ncols], maskap, ALU.mult), reads=mask_bufs, writes=[b_e[si]])
                S.op("dve", lambda h: h.tensor_reduce(st[0:nr, 2:3], e_[si][0:nr, 0:ncols], AX.X, ALU.add), reads=[b_e[si]], writes=[b_st[si]])
                S.op("act", lambda h: h.copy(pm[si][0:nr, 0:ncols], e_[si][0:nr, 0:ncols]), reads=[b_e[si]], writes=[b_pm[si]])
                S.op("act", lambda h: h.activation(st[0:nr, 3:4], sink_col[0:nr, :], AF.Exp, bias=st[0:nr, 1:2], scale=1.0), reads=[b_cs], writes=[b_st[si]])
                S.op("dve", lambda h: h.tensor_tensor(st[0:nr, 2:3], st[0:nr, 2:3], st[0:nr, 3:4], ALU.add), reads=[], writes=[b_st[si]])
                S.op("dve", lambda h: h.reciprocal(st[0:nr, 2:3], st[0:nr, 2:3]), reads=[], writes=[b_st[si]])
                S.op("pool", lambda h: h.tensor_scalar(dg[si][0:nr, 0:nr], identb[0:nr, 0:nr], st[0:nr, 2:3], None, ALU.mult), reads=[b_st[si]], writes=[b_dg[si]])
                return si

            S.dma("sp", ks_cache, ck[:, 8:128, :], writes=[b_out])
            S.dma("sp", vs_cache, cv[:, 8:128, :], writes=[b_out])
            stage = cfg.get("swa_stage", 9)
            for kv in range(cfg.get("swa_nkv", 8)):
                kvw, b_kvw = wload_flat(swa_wk[kv], NCH, 128)
                fns = []
                for kk in range(NCH):
                    for ti, (a, b) in enumerate(TGS):
                        fns.append(lambda h, kk=kk, ti=ti, a=a, b=b, kvw=kvw: h.matmul(pbank[ti][:, 0:b - a], kvw[:, kk, :], xb[:, kk, a:b],
                                                                                     start=(kk == 0), stop=(kk == NCH - 1)))
                S.group("pe", fns, reads=[b_kvw] + b_xb, writes=b_pb[0:3])

                def kside(r_, b_r):
                    S.op("dve", lambda h: h.tensor_copy(kout[:, 0:128], r_[:, T - 128:T]), reads=[b_r], writes=[b_kout])
                    S.op("dve", lambda h: h.tensor_copy(kout[:, 128:160], r_[:, T:NT]), reads=[b_r], writes=[b_kout])
                rope(0, col("swbk", kv), kT[:, 128:128 + NT], [b_kT], 1.0, side=kside)
                S.dma("sp", swk[kv], kout, reads=[b_kout], writes=[b_out])
                if stage < 2:
                    continue
                vv, vb_ = wload_flat(swa_wv[kv], NCH, 64)
                for n in range(9):
                    w = 128 if n < 8 else 32
                    pbk = 6 + (n % 2)
                    fns = [lambda h, kk=kk, n=n, w=w, pbk=pbk, vv=vv: h.matmul(pbank[pbk][0:w, 0:64], xb[:, kk, n * 128:n * 128 + w], vv[:, kk, :],
                                                                             start=(kk == 0), stop=(kk == NCH - 1)) for kk in range(NCH)]
                    S.group("pe", fns, reads=[vb_] + b_xb, writes=[b_pb[pbk]])
                    S.op("dve", lambda h, n=n, w=w, pbk=pbk, kv=kv: h.tensor_tensor(vz[0:w, n + 1, 64:128], pbank[pbk][0:w, 0:64], bvrow[0:w, kv * 64:(kv + 1) * 64], ALU.add),
                         reads=[b_pb[pbk], b_swcs], writes=[b_vz])
                    if n >= 7:
                        S.op("dve", lambda h, n=n, w=w, pbk=pbk, kv=kv: h.tensor_tensor(vout[0:w, n - 7, :], pbank[pbk][0:w, 0:64], bvrow[0:w, kv * 64:(kv + 1) * 64], ALU.add),
                             reads=[b_pb[pbk], b_swcs], writes=[b_vout])
                S.dma("sp", swv[kv], vout[:].rearrange("p n d -> p (n d)"), reads=[b_vout], writes=[b_out])
                S.op("dve", lambda h: h.tensor_copy(xs_[:, 0:128], kT[:, T:T + 128]), reads=[b_kT], writes=[b_xs])
                S.op("dve", lambda h: h.tensor_copy(xs_[:, 128:192], vout[:, 0, :]), reads=[b_vout], writes=[b_xs])
                S.dma("sp", sx_in[kv], xs_, reads=[b_xs], writes=[b_si])
                S.allgather(sx_in[kv], sx_out[kv], reads=[b_si], writes=[b_so])
                S.dma("sp", xr, sx_out[kv][0:128, :], reads=[b_so], writes=[b_xr])
                S.op("dve", lambda h: h.tensor_copy(kT[:, 0:128], xr[:, 0:128]), reads=[b_xr], writes=[b_kh])
                S.op("dve", lambda h: h.tensor_copy(vz[:, 0, 64:128], xr[:, 128:192]), reads=[b_xr], writes=[b_vz])
                if stage < 3:
                    continue
                S.dma("pool", kcT[:].rearrange("p b k -> p (b k)"), kcache[kv], writes=[b_kcT])
                S.dma("pool", vcz[:].rearrange("p b d -> p (b d)"), vcache[kv], writes=[b_vcz])
                qv, qb_ = wload(wqkv[:, kv * 256:(kv + 1) * 256], NCH, 256)
                for qc in range(2):
                    pset = qc * 3
                    fns = []
                    for kk in range(NCH):
                        for ti, (a, b) in enumerate(TGS):
                            fns.append(lambda h, kk=kk, ti=ti, a=a, b=b, qc=qc, qv=qv, pset=pset: h.matmul(pbank[pset + ti][:, 0:b - a], qv[:, kk, qc * 128:(qc + 1) * 128],
                                                                                                           xb[:, kk, a:b], start=(kk == 0), stop=(kk == NCH - 1)))
                    S.group("pe", fns, reads=[qb_] + b_xb, writes=b_pb[pset:pset + 3])
                    rope(pset, col("swbq", kv * 2 + qc), qT[:, qc, :], b_q, 0.125)
                if stage < 4:
                    continue
                for i in range(8):
                    po = pbank[4]

                    def head_body(g, i=i, po=po):
                        hp, qc = g % 2, g // 2
                        rows = slice(hp * 64, hp * 64 + 64)
                        sbk = g % 2
                        ps = pbank[sbk]
                        S.group("pe", [lambda h, rows=rows, qc=qc, i=i, ps=ps: h.matmul(ps[:, 0:256], qT[rows, qc, i * 128:(i + 1) * 128], kT[rows, i * 128:i * 128 + 256],
                                                                                         start=True, stop=True)],
                                reads=[b_q[i], b_kT, b_kh], writes=[b_pb[sbk]])
                        mk = mask0 if i == 0 else mask01
                        si = softmax_rows(128, 256, ps, b_pb[sbk], mk, [b_mask0, b_swcs], col("sink", kv * 4 + g))
                        tb = 2 + (g % 2)
                        fns = [lambda h, j=j, si=si, tb=tb: h.matmul(pbank[tb][:, j * 128:(j + 1) * 128], pm[si][:, j * 128:(j + 1) * 128], dg[si][:, :], start=True, stop=True)
                               for j in range(2)]
                        S.group("pe", fns, reads=[b_pm[si], b_dg[si]], writes=[b_pb[tb]])
                        S.op("act", lambda h, si=si, tb=tb: h.copy(pT[si][:, :], pbank[tb][:, 0:256]), reads=[b_pb[tb]], writes=[b_pT[si]])
                        vs0, vs1 = (64, 192) if hp == 0 else (0, 128)
                        fns = [lambda h, j=j, si=si, qc=qc, i=i, hp=hp, vs0=vs0, vs1=vs1: h.matmul(po[:, qc * 128:(qc + 1) * 128], vz[:, i + j, vs0:vs1], pT[si][:, j * 128:(j + 1) * 128],
                                                                                                     start=(hp == 0 and j == 0), stop=(hp == 1 and j == 1)) for j in range(2)]
                        S.group("pe", fns, reads=[b_vz, b_pT[si]], writes=[b_pb[4]])
                    for g0 in (0, 2):
                        S.replay_interleaved([S.capture(lambda: head_body(g0)), S.capture(lambda: head_body(g0 + 1))])
                    S.op("act", lambda h, i=i: h.copy(qT[:, :, i * 128:(i + 1) * 128], pbank[4][:, 0:256].rearrange("p (c t) -> p c t", c=2)),
                         reads=[b_pb[4]], writes=[b_q[i]])
                if stage < 5:
                    continue
                pos_ = pbank[5]
                def sample_body(qc, bb, hp):
                    g = qc * 2 + hp
                    rows = slice(hp * 64, hp * 64 + 64)
                    vs0, vs1 = (64, 192) if hp == 0 else (0, 128)
                    if True:
                        sbk = hp
                        ps = pbank[sbk]
                        fns = [lambda h, rows=rows, qc=qc, bb=bb, ps=ps: h.matmul(ps[0:8, 0:128], qT[rows, qc, T + bb * 8:T + bb * 8 + 8], kcT[rows, bb, :], start=True, stop=True),
                               lambda h, rows=rows, qc=qc, bb=bb, ps=ps: h.matmul(ps[0:8, 128:160], qT[rows, qc, T + bb * 8:T + bb * 8 + 8], kT[rows, 128 + T:128 + NT], start=True, stop=True)]
                        S.group("pe", fns, reads=[b_q[8], b_kcT, b_kT], writes=[b_pb[sbk]])
                        si = softmax_rows(8, 160, ps, b_pb[sbk], masks[0:8, bb, :], [b_swcs], col("sink", kv * 4 + g))
                        tb = 2 + hp
                        fns = [lambda h, si=si, tb=tb: h.matmul(pbank[tb][:, 0:8], pm[si][0:8, 0:128], dg[si][0:8, 0:8], start=True, stop=True),
                               lambda h, si=si, tb=tb: h.matmul(pbank[tb][0:32, 8:16], pm[si][0:8, 128:160], dg[si][0:8, 0:8], start=True, stop=True)]
                        S.group("pe", fns, reads=[b_pm[si], b_dg[si]], writes=[b_pb[tb]])
                        S.op("act", lambda h, si=si, tb=tb: h.copy(pT[si][:, 0:8], pbank[tb][:, 0:8]), reads=[b_pb[tb]], writes=[b_pT[si]])
                        S.op("act", lambda h, si=si, tb=tb: h.copy(pT[si][0:32, 8:16], pbank[tb][0:32, 8:16]), reads=[b_pb[tb]], writes=[b_pT[si]])
                        oc = qc * 32 + bb * 8
                        fns = [lambda h, si=si, bb=bb, oc=oc, hp=hp, vs0=vs0, vs1=vs1: h.matmul(pos_[:, oc:oc + 8], vcz[:, bb, vs0:vs1], pT[si][:, 0:8], start=(hp == 0), stop=False),
                               lambda h, si=si, oc=oc, hp=hp, vs0=vs0, vs1=vs1: h.matmul(pos_[:, oc:oc + 8], vz[0:32, 9, vs0:vs1], pT[si][0:32, 8:16], start=False, stop=(hp == 1))]
                        S.group("pe", fns, reads=[b_vcz, b_vz, b_pT[si]], writes=[b_pb[5]])
                for qc in range(2):
                    for bb in range(4):
                        S.replay_interleaved([S.capture(lambda: sample_body(qc, bb, 0)), S.capture(lambda: sample_body(qc, bb, 1))])
                S.op("act", lambda h: h.copy(qT[:, :, T:NT], pbank[5][:, 0:64].rearrange("p (c t) -> p c t", c=2)), reads=[b_pb[5]], writes=[b_q[8]])
                if stage < 6:
                    continue
                for hf in range(2):
                    wv, wb = wload(wout[kv * 256:(kv + 1) * 256, hf * 1024:(hf + 1) * 1024], 2, 1024)
                    for mo in range(0, 1024, 128):
                        nn = (hf * 1024 + mo) // 128
                        pset = (nn % 2) * 3
                        fns = []
                        for kk in range(2):
                            for ti, (a, b) in enumerate(TGS):
                                fns.append(lambda h, kk=kk, ti=ti, a=a, b=b, mo=mo, wv=wv, pset=pset: h.matmul(pbank[pset + ti][:, 0:b - a], wv[:, kk, mo:mo + 128],
                                                                                                               qT[:, kk, a:b], start=(kk == 0), stop=(kk == 1)))
                        S.group("pe", fns, reads=[wb] + b_q, writes=b_pb[pset:pset + 3])
                        for ti, (a, b) in enumerate(TGS):
                            if kv == 0:
                                S.op("dve", lambda h, nn=nn, ti=ti, a=a, b=b, pset=pset: h.scalar_tensor_tensor(xT[:, nn, a:b], pbank[pset + ti][:, 0:b - a], col("swbo", nn),
                                                                                                                xT[:, nn, a:b], ALU.add, ALU.add),
                                     reads=[b_pb[pset + ti], b_cs], writes=[b_x[nn]])
                            else:
                                S.op("dve", lambda h, nn=nn, ti=ti, a=a, b=b, pset=pset: h.tensor_tensor(xT[:, nn, a:b], xT[:, nn, a:b], pbank[pset + ti][:, 0:b - a], ALU.add),
                                     reads=[b_pb[pset + ti]], writes=[b_x[nn]])

        nlayers = cfg.get("layers", DEPTH)
        for i in range(nlayers):
            scale_residual()
            kind = cfg.get("kinds", (0, 1, 2, 0))[i]
            if cfg.get("mix", True):
                if kind == 0:
                    gla(i // 3)
                elif kind == 2:
                    sgmix()
                else:
                    swa()
            layer_norm("lmg%d" % i, "lmb%d" % i)
            if need["ffn"]:
                halo_exchange(i)
                scale_residual()
                ffn(i)
                layer_norm("lfg%d" % i, "lfb%d" % i)
        scrm.reset()
        for c in range(NCH):
            S.dma("sp", yT[:, c, :], xT[:, c, :], reads=[b_x[c]], writes=[b_out])
        S.barrier()
        S.emit(block)
    k.ninst = dict(S.n_inst)
    return nc, k


def make_consts(core, inp):
    C = const_layout()
    a = np.zeros((128, C.n), np.float32)

    def put(name, arr):
        o, n = C.off[name]
        a[:, o:o + n] = arr
    put("ident", np.eye(128, dtype=np.float32))
    put("onesd", np.full((128, 128), 1.0 / D, np.float32))
    put("flag", np.full((128, 1), float(core % 2), np.float32))
    put("alpha", np.full((128, 1), ALPHA, np.float32))
    put("eps", np.full((128, 1), EPS, np.float32))
    put("m16", np.full((128, 1), -1.0 / 16.0, np.float32))
    put("qs", np.full((128, 1), 256.0 ** -0.5, np.float32))
    put("i512", np.full((128, 1), 1.0 / 512.0, np.float32))
    ii = np.arange(128)
    put("tmask", ((ii[:, None] // 64 == ii[None, :] // 64) & (ii[:, None] <= ii[None, :])).astype(np.float32))
    sm = np.zeros((128, 32), np.float32)
    jj = np.arange(32)
    sm[:32] = ((jj[:, None] // 8 == jj[None, :] // 8) & (jj[:, None] <= jj[None, :])).astype(np.float32)
    put("smask", sm)
    rm = np.zeros((128, 4), np.float32)
    rm[:32] = (jj[:, None] // 8 == np.arange(4)[None, :]).astype(np.float32)
    put("rowm", rm)
    rs = np.ones((128, NT), np.float32)
    rs[:, 0:T:64] = 0.0
    rs[:, T:NT:8] = 0.0
    put("reset", rs)
    put("ones", np.ones((128, 128), np.float32))
    put("tri", (ii[:, None] <= ii[None, :]).astype(np.float32))
    put("sgbi", fm(inp["sg_b_in"][0]))
    put("sglg", fm(inp["sg_ln_g"][0]))
    put("sglb", fm(inp["sg_ln_b"][0]))
    put("sgbo", fm(inp["sg_b_out"][0]))
    bq = inp["swa_b_qkv"][0]
    put("swbq", fm(bq[0:2048]))
    put("swbk", np.concatenate([bq[2048:2560].reshape(8, 64).T] * 2, 0))
    put("sink", np.broadcast_to(inp["swa_sinks"][0].reshape(1, 32), (128, 32)))
    put("sinkp", np.broadcast_to(inp["swa_sinks"][0].reshape(8, 2, 2).transpose(0, 2, 1).reshape(1, 32), (128, 32)))
    put("swbo", fm(inp["swa_b_out"][0]))
    for j in range(2):
        put("bg%d" % j, fm(inp["gla_b_g"][j]))
        put("gnw%d" % j, fm(inp["gla_norm_w"][j]))
    for i in range(DEPTH):
        put("lmg%d" % i, fm(inp["ln_mix_g"][i]))
        put("lmb%d" % i, fm(inp["ln_mix_b"][i]))
        put("lfg%d" % i, fm(inp["ln_ffn_g"][i]))
        put("lfb%d" % i, fm(inp["ln_ffn_b"][i]))
        for t in range(3):
            put("cw%d_%d" % (t, i), fm(inp["ffn_conv_w"][i, t]))
        put("cb_%d" % i, fm(inp["ffn_conv_b"][i]))
    return a


def make_swc(core, inp):
    a = np.zeros((128, SWC_N), np.float32)
    half = core % 2
    pos = np.concatenate([half * T + np.arange(T), 16384 + (np.arange(NS) % 8)]).astype(np.float32)
    inv = (np.float32(500000.0) ** (-np.arange(8, dtype=np.float32) / np.float32(8))).astype(np.float32)
    ang = (pos[:, None] * inv[None, :]).astype(np.float32)
    cos = np.ones((128, NT), np.float32)
    sin = np.zeros((128, NT), np.float32)
    for p in range(128):
        d = p % 64
        if d < 16:
            cos[p] = np.cos(ang[:, d % 8])
            sin[p] = np.sin(ang[:, d % 8])
    a[:, 0:NT] = cos
    a[:, NT:2 * NT] = sin
    o = 2 * NT
    PT = np.zeros((128, 128), np.float32)
    for m in range(128):
        d = m % 64
        if d < 8:
            PT[m + 8, m] = -1.0
        elif d < 16:
            PT[m - 8, m] = 1.0
    a[:, o:o + 128] = PT
    o += 128
    qa = np.arange(128)[:, None]
    kj = np.arange(256)[None, :]
    m01 = ((kj >= qa + 1) & (kj <= qa + 128)).astype(np.float32)
    a[:, o:o + 256] = m01
    o += 256
    ms = np.zeros((128, 4, 160), np.float32)
    key = np.arange(32)[None, :]
    qq = np.arange(8)[:, None]
    for bb in range(4):
        ms[0:8, bb, 0:128] = m01[0:8, 0:128]
        ms[0:8, bb, 128:160] = ((key // 8 == bb) & (key % 8 <= qq)).astype(np.float32)
    a[:, o:o + 640] = ms.reshape(128, 640)
    o += 640
    a[:, o:o + 512] = np.broadcast_to(inp["swa_b_qkv"][0][2560:3072].reshape(1, 512), (128, 512))
    return a


def make_sgc(inp):
    ws, bs = inp["sg_w_s"][0], inp["sg_b_s"][0]
    a = np.zeros((128, 1280), np.float32)
    a[:, 0:512] = ws.transpose(2, 0, 1).reshape(128, 512)
    i32 = np.arange(32)
    corner = ws[:, :8, :8]
    tiled = corner[:, i32[None, :] % 8, i32[:, None] % 8]
    a[0:32, 512:640] = tiled.transpose(1, 0, 2).reshape(32, 128)
    a[:, 640:1152] = np.broadcast_to(bs.reshape(1, 512), (128, 512))
    a[:, 1152:1280] = np.broadcast_to(bs[:, i32 % 8].reshape(1, 128), (128, 128))
    return a


_CACHE = {}


def kernel(**inp):
    cfg = {"layers": int(os.environ.get("K_LAYERS", DEPTH)), "mix": os.environ.get("K_MIX", "1") == "1",
           "kinds": tuple(int(ch) for ch in os.environ.get("K_KINDS", "0120")),
           "noffn": os.environ.get("K_NOFFN", "0") == "1", "swa_stage": int(os.environ.get("K_SWA_STAGE", "9")),
           "swa_nkv": int(os.environ.get("K_SWA_NKV", "8")),
           "ch": int(os.environ.get("K_CH", "99")), "chs": int(os.environ.get("K_CHS", "99"))}
    inp = {k_: np.asarray(v) for k_, v in inp.items()}
    key = tuple(sorted(cfg.items()))
    if key not in _CACHE:
        _CACHE[key] = build_nc(cfg)
    nc, kk = _CACHE[key]
    xp, xs = inp["x_prompt"], inp["x_sample"]
    wq = inp["swa_w_qkv"][0]
    wk_ = wq[:, 2048:2560].reshape(NCH, 128, 8, 64).transpose(2, 1, 0, 3)
    SWK = np.ascontiguousarray(np.concatenate([wk_, wk_], 3)).reshape(8, 128, NCH * 128)
    SWV = np.ascontiguousarray(wq[:, 2560:3072].reshape(NCH, 128, 8, 64).transpose(2, 1, 0, 3)).reshape(8, 128, NCH * 64)
    GWG = np.ascontiguousarray(inp["gla_w_in"][:, :, 6144:6160].reshape(2, NCH, 128, 16).transpose(0, 2, 1, 3)).reshape(2, 128, NCH * 16)
    in_maps = []
    for c in range(8):
        b, half = c // 2, c % 2
        xtok = np.concatenate([xp[b, half * T:(half + 1) * T], xs[4 * c:4 * c + 4].reshape(NS, D)], 0)
        xin = np.ascontiguousarray(xtok.T.reshape(NCH, 128, NT).transpose(1, 0, 2))
        fc = inp["state_ffn_conv"][:, 4 * c:4 * c + 4]
        fc = fc.reshape(DEPTH, 8, FCH, 128).transpose(3, 0, 2, 1)
        in_maps.append({
            "xin": xin,
            "cst": make_consts(c, inp),
            "fconv": np.ascontiguousarray(fc).reshape(128, -1),
            "ffn_w_in": inp["ffn_w_in"],
            "ffn_w_out": inp["ffn_w_out"],
            "swa_w_qkv": inp["swa_w_qkv"], "swa_w_out": inp["swa_w_out"], "swc": make_swc(c, inp),
            "swa_wk": SWK, "swa_wv": SWV, "gla_wg": GWG,
            "kcache": np.ascontiguousarray(np.concatenate([inp["cache_swa_k"][0, 4 * c:4 * c + 4].transpose(2, 3, 0, 1)] * 2, 1)).reshape(8, 128, 512),
            "vcache": np.ascontiguousarray(np.pad(inp["cache_swa_v"][0, 4 * c:4 * c + 4].transpose(2, 1, 0, 3), ((0, 0), (0, 0), (0, 0), (64, 64)))).reshape(8, 128, 768),
            "ck": np.ascontiguousarray(inp["cache_swa_k"][0, 4 * c:4 * c + 4]).reshape(4, 128, 512),
            "cv": np.ascontiguousarray(inp["cache_swa_v"][0, 4 * c:4 * c + 4]).reshape(4, 128, 512),
            "sg_w_in": inp["sg_w_in"], "sg_w_out": inp["sg_w_out"], "sgc": make_sgc(inp),
            "gla_w_in": inp["gla_w_in"], "gla_w_g2": inp["gla_w_g2"], "gla_w_out": inp["gla_w_out"],
            "gla_s0": np.ascontiguousarray(inp["state_gla"][:, 4 * c:4 * c + 4]),
        })
    dummy = np.zeros((1, 128, 128), np.float32)
    fam = {"ffn_w_in": "ffn", "ffn_w_out": "ffn", "swa_w_qkv": "swa", "swa_w_out": "swa", "sg_w_in": "sg", "sg_w_out": "sg",
           "gla_w_in": "gla", "gla_w_out": "gla", "gla_s0": "gla"}
    for m_ in in_maps:
        for nm, f_ in fam.items():
            if not kk.need[f_]:
                m_[nm] = dummy
    res = run_bass_kernel_spmd(nc, in_maps, core_ids=list(range(8)))
    R = res.results
    y_p = np.zeros((4, 2048, D), np.float32)
    y_s = np.zeros((32, 8, D), np.float32)
    conv_p = np.zeros((DEPTH, 4, 2, DFF), np.float32)
    conv_s = np.zeros((DEPTH, 32, 2, DFF), np.float32)
    for c in range(8):
        b, half = c // 2, c % 2
        y = R[c]["yT"].transpose(1, 0, 2).reshape(D, NT).T
        y_p[b, half * T:(half + 1) * T] = y[:T]
        y_s[4 * c:4 * c + 4] = y[T:].reshape(4, 8, D)
        cv = R[c]["convT"].reshape(128, DEPTH, FCH, 10).transpose(1, 3, 2, 0).reshape(DEPTH, 10, DFF)
        if half == 1:
            conv_p[:, b] = cv[:, 0:2]
        conv_s[:, 4 * c:4 * c + 4] = cv[:, 2:10].reshape(DEPTH, 4, 2, DFF)
    z = np.zeros
    gsp = np.zeros((2, 4, 4, 256, 512), np.float32)
    gss = np.zeros((2, 32, 4, 256, 512), np.float32)
    for c in range(8):
        if c % 2 == 1:
            gsp[:, c // 2] = R[c]["gla_sp"]
        gss[:, 4 * c:4 * c + 4] = R[c]["gla_ss"]
    sgv_s = np.zeros((1, 32, 8, D), np.float32)
    for c in range(8):
        sv = R[c]["sgv"].reshape(128, NCH, 32).transpose(2, 1, 0).reshape(32, D)
        sgv_s[0, 4 * c:4 * c + 4] = sv.reshape(4, 8, D)
    swkp = np.zeros((1, 4, 128, 8, 64), np.float32)
    swvp = np.zeros((1, 4, 128, 8, 64), np.float32)
    swks = np.zeros((1, 32, 128, 8, 64), np.float32)
    swvs = np.zeros((1, 32, 128, 8, 64), np.float32)
    for c in range(8):
        k_ = R[c]["swk"]
        v_ = R[c]["swv"].reshape(8, 128, 2, 64)
        if c % 2 == 1:
            swkp[0, c // 2] = k_[:, 0:64, 0:128].transpose(2, 0, 1)
            swvp[0, c // 2] = v_[:, :, 0, :].transpose(1, 0, 2)
        for bb in range(4):
            swks[0, 4 * c + bb, 0:120] = R[c]["ks_cache"][bb].reshape(120, 8, 64)
            swvs[0, 4 * c + bb, 0:120] = R[c]["vs_cache"][bb].reshape(120, 8, 64)
            swks[0, 4 * c + bb, 120:128] = k_[:, 0:64, 128 + bb * 8:128 + bb * 8 + 8].transpose(2, 0, 1)
            swvs[0, 4 * c + bb, 120:128] = v_[:, bb * 8:bb * 8 + 8, 1, :].transpose(1, 0, 2)
    return (y_p, y_s, gsp, gss, swkp, swvp, swks, swvs,
            sgv_s, conv_p, conv_s)
```
